# Optimizing a Trainium2 kernel written in Bass

```python
import jax, jax.numpy as jnp
from jax import lax
import numpy as np

D_MODEL = 1024
BATCH = 2
SEQ = 8192
DEPTH = 1

N_META = 16
HEAD_DIM = 64
RMS_EPS = 1e-6
NEG_INF = -1e30
A_HEADS = 8
A_KV_HEADS = 2
A_WIDTH = A_HEADS * HEAD_DIM
A_KV_WIDTH = A_KV_HEADS * HEAD_DIM
WINDOW = 128
BLOCK = 128
ROT_DIM = HEAD_DIM // 4
ROPE_THETA = 500000.0
B_HEADS = 8
B_WIDTH = B_HEADS * HEAD_DIM
GRID_W = 64
NA_KH_MAX = 8
NA_KW = 16
SPLIT_SIZES = (A_WIDTH, A_KV_WIDTH, A_KV_WIDTH, A_WIDTH,
               B_WIDTH, B_WIDTH, B_WIDTH, B_WIDTH,
               D_MODEL, D_MODEL)
IN_COLS = sum(SPLIT_SIZES)

kernel_name = "hybrid_window_gqa_natten_gated_encoder"


def _rmsnorm(x, gain):
    x32 = x.astype(jnp.float32)
    y = x32 * lax.rsqrt(jnp.mean(x32 * x32, axis=-1, keepdims=True) + RMS_EPS)
    return y.astype(x.dtype) * gain


def _partial_rope(x, pos):
    half = ROT_DIM // 2
    inv_freq = ROPE_THETA ** (-jnp.arange(half, dtype=jnp.float32) / half)
    ang = pos[:, None] * inv_freq[None, :]
    cos = jnp.cos(ang)[None, :, None, :].astype(x.dtype)
    sin = jnp.sin(ang)[None, :, None, :].astype(x.dtype)
    x1 = x[..., :half]
    x2 = x[..., half:ROT_DIM]
    return jnp.concatenate([x1 * cos - x2 * sin, x2 * cos + x1 * sin, x[..., ROT_DIM:]], axis=-1)


def _softmax_with_sink(s, sink):
    s = jnp.concatenate([s, jnp.broadcast_to(sink, s.shape[:-1] + (1,))], axis=-1)
    return jax.nn.softmax(s, axis=-1)[..., :-1]


def _window_gqa(q, k, v, sink):
    bsz, L, _, dh = q.shape
    S = L - N_META
    nb = S // BLOCK
    G = A_HEADS // A_KV_HEADS
    scale = dh ** -0.5
    q = q.reshape(bsz, L, A_KV_HEADS, G, dh)
    qm, qr = q[:, :N_META], q[:, N_META:]
    km, kr = k[:, :N_META], k[:, N_META:]
    vm, vr = v[:, :N_META], v[:, N_META:]
    sink_f = sink.astype(jnp.float32).reshape(A_KV_HEADS, G)

    qb = qr.reshape(bsz, nb, BLOCK, A_KV_HEADS, G, dh)

    def band(t):
        tp = jnp.pad(t, ((0, 0), (BLOCK, BLOCK), (0, 0), (0, 0)))
        tp = tp.reshape(bsz, nb + 2, BLOCK, A_KV_HEADS, dh)
        return jnp.concatenate([tp[:, :-2], tp[:, 1:-1], tp[:, 2:]], axis=2)

    kband, vband = band(kr), band(vr)
    s_band = jnp.einsum('bnqkgd,bnjkd->bkgnqj', qb, kband).astype(jnp.float32) * scale
    qi = jnp.arange(BLOCK)[:, None]
    kj = jnp.arange(3 * BLOCK)[None, :]
    tk = (jnp.arange(nb)[:, None, None] - 1) * BLOCK + kj[None]
    band_mask = (jnp.abs(kj - BLOCK - qi)[None] <= WINDOW) & (tk >= 0) & (tk < S)
    s_band = jnp.where(band_mask, s_band, NEG_INF)
    s_meta = jnp.einsum('bnqkgd,bmkd->bkgnqm', qb, km).astype(jnp.float32) * scale
    p = _softmax_with_sink(jnp.concatenate([s_meta, s_band], axis=-1),
                           sink_f[None, :, :, None, None, None]).astype(v.dtype)
    o_r = (jnp.einsum('bkgnqm,bmkd->bnqkgd', p[..., :N_META], vm)
           + jnp.einsum('bkgnqj,bnjkd->bnqkgd', p[..., N_META:], vband))
    o_r = o_r.reshape(bsz, S, A_HEADS, dh)

    kmq = jnp.concatenate([km, kr[:, :BLOCK]], axis=1)
    vmq = jnp.concatenate([vm, vr[:, :BLOCK]], axis=1)
    s_mq = jnp.einsum('bqkgd,bjkd->bkgqj', qm, kmq).astype(jnp.float32) * scale
    kpos = jnp.arange(N_META + BLOCK)[None, :]
    qpos = jnp.arange(N_META)[:, None]
    s_mq = jnp.where(jnp.abs(kpos - qpos) <= WINDOW, s_mq, NEG_INF)
    p_mq = _softmax_with_sink(s_mq, sink_f[None, :, :, None, None]).astype(v.dtype)
    o_m = jnp.einsum('bkgqj,bjkd->bqkgd', p_mq, vmq).reshape(bsz, N_META, A_HEADS, dh)
    return jnp.concatenate([o_m, o_r], axis=1)


def _neighbourhood_attn(q, k, v, rpb):
    bsz, L, _, dh = q.shape
    S = L - N_META
    rows = S // GRID_W
    kh = min(NA_KH_MAX, rows)
    scale = dh ** -0.5
    qm, km, vm = q[:, :N_META], k[:, :N_META], v[:, :N_META]
    qg = q[:, N_META:].reshape(bsz, rows, GRID_W, B_HEADS, dh)
    kg = k[:, N_META:].reshape(bsz, rows, GRID_W, B_HEADS, dh)
    vg = v[:, N_META:].reshape(bsz, rows, GRID_W, B_HEADS, dh)

    col = jnp.arange(GRID_W)
    col_start = jnp.clip(col - NA_KW // 2, 0, GRID_W - NA_KW)
    col_idx = col_start[:, None] + jnp.arange(NA_KW)[None, :]
    dc_idx = col_idx - col[:, None] + (NA_KW - 1)

    def row_fn(r):
        r_start = jnp.clip(r - kh // 2, 0, rows - kh)
        q_r = lax.dynamic_index_in_dim(qg, r, axis=1, keepdims=False)
        k_rows = lax.dynamic_slice_in_dim(kg, r_start, kh, axis=1)
        v_rows = lax.dynamic_slice_in_dim(vg, r_start, kh, axis=1)
        k_win = k_rows[:, :, col_idx]
        v_win = v_rows[:, :, col_idx]
        s_loc = jnp.einsum('bqhd,brqwhd->bhqrw', q_r, k_win).astype(jnp.float32) * scale
        dr_idx = r_start + jnp.arange(kh) - r + (NA_KH_MAX - 1)
        bias = rpb[:, dr_idx[None, :, None], dc_idx[:, None, :]]
        s_loc = (s_loc + bias[None].astype(jnp.float32)).reshape(bsz, B_HEADS, GRID_W, kh * NA_KW)
        s_met = jnp.einsum('bqhd,bmhd->bhqm', q_r, km).astype(jnp.float32) * scale
        p = jax.nn.softmax(jnp.concatenate([s_met, s_loc], axis=-1), axis=-1).astype(v.dtype)
        p_loc = p[..., N_META:].reshape(bsz, B_HEADS, GRID_W, kh, NA_KW)
        return (jnp.einsum('bhqm,bmhd->bqhd', p[..., :N_META], vm)
                + jnp.einsum('bhqrw,brqwhd->bqhd', p_loc, v_win))

    o_g = lax.map(row_fn, jnp.arange(rows))
    o_r = jnp.transpose(o_g, (1, 0, 2, 3, 4)).reshape(bsz, S, B_HEADS, dh)
    s_mm = jnp.einsum('bqhd,bmhd->bhqm', qm, km).astype(jnp.float32) * scale
    p_mm = jax.nn.softmax(s_mm, axis=-1).astype(v.dtype)
    o_m = jnp.einsum('bhqm,bmhd->bqhd', p_mm, vm)
    return jnp.concatenate([o_m, o_r], axis=1)


def setup_inputs(seed: int = 0) -> dict:
    key = jax.random.key(seed)
    ks = jax.random.split(key, 10)
    f32 = jnp.float32
    x = jax.random.normal(ks[0], (BATCH, SEQ, D_MODEL), f32)
    meta_tokens = jax.random.normal(ks[1], (N_META, D_MODEL), f32)
    norm_gain = 1.0 + 0.01 * jax.random.normal(ks[2], (DEPTH, D_MODEL), f32)
    w_in = jax.random.normal(ks[3], (DEPTH, D_MODEL, IN_COLS), f32) * D_MODEL ** -0.5
    sink_logits = 0.5 * jax.random.normal(ks[4], (DEPTH, A_HEADS), f32)
    rel_pos_bias = 0.02 * jax.random.normal(ks[5], (DEPTH, B_HEADS, 2 * NA_KH_MAX - 1, 2 * NA_KW - 1), f32)
    w_proj_a = jax.random.normal(ks[6], (DEPTH, A_WIDTH, D_MODEL), f32) * A_WIDTH ** -0.5
    w_proj_b = jax.random.normal(ks[7], (DEPTH, B_WIDTH, D_MODEL), f32) * B_WIDTH ** -0.5
    w_out = jax.random.normal(ks[8], (DEPTH, D_MODEL, D_MODEL), f32) * D_MODEL ** -0.5
    final_norm_gain = 1.0 + 0.01 * jax.random.normal(ks[9], (D_MODEL,), f32)
    return {"x": x, "meta_tokens": meta_tokens, "norm_gain": norm_gain, "w_in": w_in,
            "sink_logits": sink_logits, "rel_pos_bias": rel_pos_bias, "w_proj_a": w_proj_a,
            "w_proj_b": w_proj_b, "w_out": w_out, "final_norm_gain": final_norm_gain}


def reference(x, meta_tokens, norm_gain, w_in, sink_logits, rel_pos_bias, w_proj_a, w_proj_b,
              w_out, final_norm_gain):
    bsz, S, _ = x.shape
    L = S + N_META
    h = jnp.concatenate([jnp.broadcast_to(meta_tokens[None].astype(x.dtype), (bsz, N_META, D_MODEL)), x], axis=1)
    pos = jnp.arange(L, dtype=jnp.float32)
    split_points = list(np.cumsum(SPLIT_SIZES)[:-1])
    for l in range(DEPTH):
        n = _rmsnorm(h, norm_gain[l])
        proj = jnp.einsum('bld,dc->blc', n, w_in[l])
        qa, ka, va, za, qb, kb, vb, zb, ga, gb = jnp.split(proj, split_points, axis=-1)
        qa = _partial_rope(qa.reshape(bsz, L, A_HEADS, HEAD_DIM), pos)
        ka = _partial_rope(ka.reshape(bsz, L, A_KV_HEADS, HEAD_DIM), pos)
        va = va.reshape(bsz, L, A_KV_HEADS, HEAD_DIM)
        oa = _window_gqa(qa, ka, va, sink_logits[l]).reshape(bsz, L, A_WIDTH) * jax.nn.silu(za)
        ob = _neighbourhood_attn(qb.reshape(bsz, L, B_HEADS, HEAD_DIM),
                                 kb.reshape(bsz, L, B_HEADS, HEAD_DIM),
                                 vb.reshape(bsz, L, B_HEADS, HEAD_DIM),
                                 rel_pos_bias[l]).reshape(bsz, L, B_WIDTH) * jax.nn.silu(zb)
        merged = (jax.nn.sigmoid(ga) * jnp.einsum('blc,cd->bld', oa, w_proj_a[l])
                  + jax.nn.sigmoid(gb) * jnp.einsum('blc,cd->bld', ob, w_proj_b[l]))
        h = h + jnp.einsum('bld,de->ble', merged, w_out[l])
    return _rmsnorm(h, final_norm_gain)[:, N_META:]
```

```python
import numpy as np
from contextlib import ExitStack
import concourse.bass as bass
import concourse.mybir as mybir
from concourse.bass_utils import run_bass_kernel_spmd

F32 = mybir.dt.float32
BF16 = mybir.dt.bfloat16
AF = mybir.ActivationFunctionType
ALU = mybir.AluOpType

D = 1024
NCOL = 5376
SEQ = 8192
NMETA = 16
TOK = 2048
HALO = 256
NPB = 20
EPS = 1e-6
NEG = -30000.0
C_QA, C_KA, C_VA, C_ZA, C_QB, C_KB, C_VB, C_ZB, C_GA, C_GB = 0, 512, 640, 768, 1280, 1792, 2304, 2816, 3328, 4352
W_PIECES = [("kva", 512, 768), ("vb", 2304, 2816), ("kb", 1792, 2304), ("qa", 0, 512), ("qb", 1280, 1792),
            ("za", 768, 1280), ("zb", 2816, 3328), ("ga0", 3328, 3840), ("ga1", 3840, 4352),
            ("gb0", 4352, 4864), ("gb1", 4864, 5376)]


class Buf:
    __slots__ = ("name", "lw", "rd")

    def __init__(self, name):
        self.name = name
        self.lw = None
        self.rd = []


class Prog:
    ENGS = ("pe", "act", "dve", "pool", "sp")

    def __init__(self, nc, same_raw=True):
        self.nc = nc
        self.stream = {e: [] for e in self.ENGS}
        self.seq = {e: 0 for e in self.ENGS}
        self.known = {e: {} for e in self.ENGS}
        self.same_raw = same_raw
        self.dma_sems = []
        self.rec = None
        self.efree = {}
        self.done = {}

    def _deps(self, eng, reads, writes):
        deps = {}

        def add(d, raw):
            key, val = d
            if key == eng and not (raw and self.same_raw and eng != "pe"):
                return
            if deps.get(key, 0) < val:
                deps[key] = val

        for b in reads:
            if b.lw is not None:
                add(b.lw, True)
        for b in writes:
            if b.lw is not None:
                add(b.lw, False)
            for r in b.rd:
                add(r, False)
        out = []
        kn = self.known[eng]
        for key, val in deps.items():
            if kn.get(key, 0) < val:
                kn[key] = val
                out.append((key, val))
        return out

    @staticmethod
    def _mark(me, reads, writes):
        for b in reads:
            b.rd.append(me)
        for b in writes:
            b.lw = me
            b.rd = []

    DUR = {"act": 450.0, "dve": 350.0, "pool": 500.0, "sp": 60.0, "pe": 100.0}

    def op(self, eng, fn, reads=(), writes=(), cost=0.0, dur=None):
        if dur is None:
            dur = cost if eng == "pe" else self.DUR[eng]
        item = ("op", eng, fn, tuple(reads), tuple(writes), cost, dur, None)
        if self.rec is not None:
            self.rec.append(item)
            return
        self._play_item(item)

    def dma(self, eng, sem, fn, reads=(), writes=()):
        item = ("dma", eng, fn, tuple(reads), tuple(writes), 0.0, 60.0, sem)
        if self.rec is not None:
            self.rec.append(item)
            return
        self._play_item(item)

    def check(self, f):
        item = ("chk", None, f, (), (), 0.0, 0.0, None)
        if self.rec is not None:
            self.rec.append(item)
        else:
            f()

    def gate(self, pred, what=""):
        item = ("gate", None, pred, (), (), 0.0, 0.0, what)
        if self.rec is not None:
            self.rec.append(item)
        else:
            assert pred(), what

    def record(self, f):
        self.rec = []
        f()
        r, self.rec = self.rec, None
        return r

    def _ready(self, eng, reads, writes):
        t = 0.0
        for b in reads:
            if b.lw is not None:
                t = max(t, self.done.get(b.lw, 0.0))
        for b in writes:
            if b.lw is not None and b.lw[0] != eng:
                t = max(t, self.done.get(b.lw, 0.0))
            for r in b.rd:
                if r[0] != eng:
                    t = max(t, self.done.get(r, 0.0))
        return t

    def _play_item(self, item):
        kind, eng, fn, reads, writes, cost, dur, sem = item
        if kind == "chk":
            fn()
            return
        if kind == "gate":
            assert fn(), ("gate violated", sem)
            return
        start = max(self.efree.get(eng, 0.0), self._ready(eng, reads, writes) + 120.0)
        if kind == "op":
            self._op(eng, fn, reads, writes)
            me = (eng, self.seq[eng])
            if eng == "pe":
                self.efree[eng] = start + dur
                self.done[me] = start + dur + 180.0
            else:
                self.efree[eng] = start + dur
                self.done[me] = start + dur
        else:
            self._dma(eng, sem, fn, reads, writes)
            me = (sem, self.seq[sem])
            self.efree[eng] = start + dur
            self.done[me] = start + 3500.0

    def play(self, items):
        for it in items:
            self._play_item(it)

    def schedule(self, threads):
        def bundles(L):
            out = []
            for it in L:
                if (it[0] == "op" and it[1] == "pe") or it[0] == "gate" or not out:
                    out.append([it])
                else:
                    out[-1].append(it)
            return out
        bl = [bundles(L) for L in threads]
        pos = [0] * len(bl)
        tot = [sum(it[5] for bd in b for it in bd) or 1.0 for b in bl]
        acc = [0.0] * len(bl)
        while True:
            cands = []
            for i, b in enumerate(bl):
                if pos[i] >= len(b):
                    continue
                bd0 = b[pos[i]]
                if bd0[0][0] == "gate" and not bd0[0][2]():
                    continue
                first = next((it for it in bd0 if it[0] not in ("chk", "gate")), None)
                if first is None:
                    est = 0.0
                else:
                    est = max(self.efree.get(first[1], 0.0), self._ready(first[1], first[3], first[4]) + 120.0)
                cands.append((est, i))
            if not cands:
                assert all(pos[i] >= len(b) for i, b in enumerate(bl)), "schedule deadlock: all threads gated"
                break
            pe_free = self.efree.get("pe", 0.0)
            ok = [(acc[i] / tot[i], i) for est, i in cands if est <= pe_free + 60.0]
            if ok:
                pick = min(ok)[1]
            else:
                pick = min(cands)[1]
            bd = bl[pick][pos[pick]]
            pos[pick] += 1
            acc[pick] += sum(it[5] for it in bd)
            for it in bd:
                self._play_item(it)

    def _op(self, eng, fn, reads, writes):
        waits = self._deps(eng, reads, writes)
        self.seq[eng] += 1
        self._mark((eng, self.seq[eng]), reads, writes)
        self.stream[eng].append((waits, fn, eng, 1))

    def new_dma_sem(self, name):
        self.dma_sems.append(name)
        self.seq[name] = 0
        return name

    def _dma(self, eng, sem, fn, reads, writes):
        waits = self._deps(eng, reads, writes)
        self.seq[sem] += 16
        self._mark((sem, self.seq[sem]), reads, writes)
        self.stream[eng].append((waits, fn, sem, 16))

    def emit(self, final_waits):
        nc = self.nc
        with ExitStack() as es:
            H = {}
            for k in list(self.ENGS) + self.dma_sems:
                H[k] = es.enter_context(nc.semaphore("s_" + k))
            blk = es.enter_context(nc.Block())

            def run(e, items):
                for waits, fn, key, inc in items:
                    for wk, wv in waits:
                        e.wait_ge(H[wk], wv)
                    if fn is not None:
                        fn(e).then_inc(H[key], inc)

            self.stream["sp"].append((final_waits, None, None, 0))
            blk.tensor(lambda e: run(e, self.stream["pe"]))
            blk.scalar(lambda e: run(e, self.stream["act"]))
            blk.vector(lambda e: run(e, self.stream["dve"]))
            blk.gpsimd(lambda e: run(e, self.stream["pool"]))
            blk.sync(lambda e: run(e, self.stream["sp"]))


def V(base, dims):
    return bass.AP(base.tensor, base.offset, [list(base.ap[0])] + [list(d) for d in dims])


class Ring:
    def __init__(self, items):
        self.items = items
        self.i = 0

    def next(self):
        it = self.items[self.i % len(self.items)]
        self.i += 1
        return it


class T:
    __slots__ = ("t", "b")

    def __init__(self, t, name):
        self.t = t
        self.b = Buf(name)


def build_nc():
    nc = bass.Bass("TRN2", target_bir_lowering=False, dynamic_dma_scratch_size=12288)

    def din(name, shape):
        return nc.dram_tensor(name, shape, F32, kind="ExternalInput").ap()

    xp_d = din("xp", [NPB * 128, D])
    meta_d = din("meta", [NMETA, D])
    win_d = din("w_in", [128, 8, NCOL])
    wpa_d = din("w_pa", [128, 4, D])
    wpb_d = din("w_pb", [128, 4, D])
    wo_d = din("w_out", [128, 8, D])
    gain_d = din("gain", [1, D])
    fgain_d = din("fgain", [1, D])
    sink_d = din("sink", [1, 8])
    rope_d = din("rope", [128, 21, 32])
    amask_d = din("amask", [128, 4, 128])
    btab_d = din("btab", [27 * 128, 1024])
    y_d = nc.dram_tensor("y", [TOK, D], F32, kind="ExternalOutput").ap()

    with ExitStack() as es:
        def sb(name, shape, dt=BF16):
            return es.enter_context(nc.sbuf_tensor(name, shape, dt))

        def ps(name, shape, dt=F32):
            return es.enter_context(nc.psum_tensor(name, shape, dt))

        P = Prog(nc)
        W = sb("W", [128, 8, NCOL])
        WPA = sb("WPA", [128, 4, D])
        WPB = sb("WPB", [128, 4, D])
        WO = sb("WO", [128, 8, D])
        GREP = sb("GREP", [128, D], F32)
        FGREP = sb("FGREP", [128, D], F32)
        XR = [T(sb("XR%d" % i, [128, D], F32), "XR%d" % i) for i in range(3)]
        NBt = T(sb("NB", [128, D]), "NB")
        NT = [sb("NT%d" % i, [128, 8, 256]) for i in range(3)]
        Bnt = [[Buf("nt%d_%d" % (i, j)) for j in range(2)] for i in range(3)]
        NTM = T(sb("NTM", [128, 8, 16]), "NTM")
        QKq2 = [T(sb("QKq%d" % i, [128, 512]), "QKq%d" % i) for i in range(2)]
        QKk = T(sb("QKk", [128, 128]), "QKk")
        QTA = sb("QTA", [128, 4, 256])
        Bqta = [Buf("qta0"), Buf("qta1")]
        QTB = T(sb("QTB", [128, 4, 256]), "QTB")
        KAT = sb("KAT", [128, 8 * 128])
        VA = sb("VA", [128, 8, 2, 65])
        KBT = sb("KBT", [128, 4, 8 * 128])
        VB = sb("VB", [128, 8, 8, 65])
        Bkat = [Buf("kat%d" % i) for i in range(8)]
        Bva = [Buf("va%d" % i) for i in range(8)]
        Bkbt = [Buf("kbt%d" % i) for i in range(8)]
        Bvb = [Buf("vb%d" % i) for i in range(8)]
        KATM = T(sb("KATM", [128, 16]), "KATM")
        KBTM = T(sb("KBTM", [128, 4, 16]), "KBTM")
        VAM = T(sb("VAM", [128, 2, 65]), "VAM")
        VBM = T(sb("VBM", [128, 8, 65]), "VBM")
        PT = Ring([T(sb("PT%d" % i, [128, 512]), "PT%d" % i) for i in range(4)])
        TBE = Ring([T(sb("TBE%d" % i, [128, 1024]), "TBE%d" % i) for i in range(2)])
        AM = T(sb("AM", [128, 4, 128]), "AM")
        ROPE = T(sb("ROPE", [128, 21, 32], F32), "ROPE")
        THX = T(sb("THX", [128, 512], F32), "THX")
        THY = T(sb("THY", [128, 512], F32), "THY")
        ZS = T(sb("ZS", [128, D]), "ZS")
        OAG = T(sb("OAG", [128, D]), "OAG")
        OAGT2 = [sb("OAGT%d" % i, [128, 8, 256]) for i in range(2)]
        Boagt2 = [[Buf("oagt%d_%d" % (i, j)) for j in range(2)] for i in range(2)]
        MT = sb("MT", [128, 8, 256])
        Bmt = [Buf("mt%d" % i) for i in range(8)]
        SS = sb("SS", [128, 4, 4], F32)
        Bss = Ring([(i, Buf("ss%d" % i)) for i in range(4)])
        RT = [T(sb("RT%d" % i, [128, 8, 16], F32), "RT%d" % i) for i in range(2)]
        RTK = [T(sb("RTK%d" % i, [128, 2, 16], F32), "RTK%d" % i) for i in range(2)]
        DEN = Ring([T(sb("DEN%d" % i, [128, 8], F32), "DEN%d" % i) for i in range(2)])
        ES2 = T(sb("ES2", [128, 8], F32), "ES2")
        NEGH = T(sb("NEGH", [128, 1], F32), "NEGH")
        ID = T(sb("ID", [128, 128]), "ID")

        MMB = [T(ps("MM%d" % i, [128, 512]), "MM%d" % i) for i in range(2)]
        STB = [T(ps("ST%d" % i, [128, 512]), "ST%d" % i) for i in range(4)]
        OB = [T(ps("O%d" % i, [128, 512]), "O%d" % i) for i in range(2)]
        mm = Ring(MMB)
        st_ring = Ring(STB)
        o_ring = Ring(OB)
        xr = Ring(XR)

        def cdma(eng, name, out_ap, in_ap, buf):
            s = P.new_dma_sem(name)
            P.dma(eng, s, lambda e: e.dma_start(out=out_ap, in_=in_ap), writes=[buf])

        Bgrep, Bfgrep = Buf("grep"), Buf("fgrep")
        cdma("sp", "c_grep", GREP[:, :], gain_d.partition_broadcast(128), Bgrep)
        cdma("sp", "c_rope", ROPE.t[:, :, :], rope_d, ROPE.b)
        cdma("sp", "c_sink", ES2.t[:, :], sink_d.partition_broadcast(128), ES2.b)
        cdma("pool", "c_am", AM.t[:, :, :], amask_d, AM.b)
        Bw = {name: Buf("w_" + name) for name, _, _ in W_PIECES}
        Bwpa, Bwpb, Bwo = Buf("wpa"), Buf("wpb"), [Buf("wo0"), Buf("wo1")]

        def wdma(names):
            for name in names:
                if name == "w_pa":
                    cdma("pool", "w_pa", WPA[:, :, :], wpa_d, Bwpa)
                elif name == "w_pb":
                    cdma("pool", "w_pb", WPB[:, :, :], wpb_d, Bwpb)
                elif name == "w_o0":
                    cdma("pool", "w_o0", WO[:, :, 0:512], wo_d[:, :, 0:512], Bwo[0])
                elif name == "w_o1":
                    cdma("pool", "w_o1", WO[:, :, 512:1024], wo_d[:, :, 512:1024], Bwo[1])
                else:
                    c0, c1 = [(a, b) for n_, a, b in W_PIECES if n_ == name][0]
                    cdma("pool", "w_" + name, W[:, :, c0:c1], win_d[:, :, c0:c1], Bw[name])
        wdma(["kva", "vb", "kb"])
        cdma("sp", "c_fgrep", FGREP[:, :], fgain_d.partition_broadcast(128), Bfgrep)

        def wbuf(c0):
            for name, a, b in W_PIECES:
                if a <= c0 < b:
                    return Bw[name]
            raise KeyError(c0)

        P.op("dve", lambda e: e.memset(THY.t[:, 0:128], 0.0), writes=[THY.b])
        P.op("pool", lambda e: e.affine_select(out=THY.t[:, 0:128], in_=THY.t[:, 0:128], pattern=[[-1, 128]],
                                               compare_op=ALU.not_equal, fill=1.0, base=0, channel_multiplier=1),
             reads=[THY.b], writes=[THY.b])
        P.op("dve", lambda e: e.tensor_copy(out=ID.t[:, :], in_=THY.t[:, 0:128]), reads=[THY.b], writes=[ID.b])
        P.op("dve", lambda e: e.memset(NEGH.t[:, :], -0.5), writes=[NEGH.b])
        P.op("dve", lambda e: e.memset(VA[:, :, :, :], 1.0), writes=Bva)
        P.op("dve", lambda e: e.memset(VB[:, :, :, :], 1.0), writes=Bvb)
        P.op("dve", lambda e: e.memset(VAM.t[:, :, :], 1.0), writes=[VAM.b])
        P.op("dve", lambda e: e.memset(VBM.t[:, :, :], 1.0), writes=[VBM.b])
        P.op("act", lambda e: e.activation(out=ES2.t[:, :], in_=ES2.t[:, :], func=AF.Exp), reads=[ES2.b], writes=[ES2.b])
        P.op("dve", lambda e: e.tensor_scalar(out=ES2.t[:, :], in0=ES2.t[:, :], scalar1=2.0, scalar2=None, op0=ALU.mult),
             reads=[ES2.b], writes=[ES2.b])

        def mmc(n_mm, ncols):
            return n_mm * max(95.0, 0.53 * ncols)

        def mm_group(out_ap, pairs, reads, wbufs, ncols=512):
            n = len(pairs)

            def fn(e):
                last = None
                for i, (l, r) in enumerate(pairs):
                    last = e.matmul(out=out_ap, lhsT=l, rhs=r, start=(i == 0), stop=(i == n - 1))
                return last
            P.op("pe", fn, reads=reads, writes=wbufs, cost=mmc(n, ncols))

        cp_toggle = [0]

        def copy(out_ap, in_ap, reads, writes, eng=None, dur=None):
            if eng is None:
                eng = ("act", "act", "dve")[cp_toggle[0] % 3]
                cp_toggle[0] += 1
            if eng == "act":
                P.op("act", lambda e: e.activation(out=out_ap, in_=in_ap, func=AF.Copy), reads=reads, writes=writes, dur=dur)
            else:
                P.op(eng, lambda e: e.tensor_copy(out=out_ap, in_=in_ap), reads=reads, writes=writes, dur=dur)

        def transposes(n_in, cols, src_t, src_col0, np_, bank):
            trv = bank.t.bitcast(BF16)

            def fn(e):
                last = None
                for i in range(n_in):
                    last = e.transpose(out=trv[:, i * 128:i * 128 + np_],
                                       in_=src_t.t[0:np_, src_col0 + i * 128:src_col0 + (i + 1) * 128],
                                       identity=ID.t[0:np_, 0:np_])
                return last
            P.op("pe", fn, reads=[src_t.b, ID.b], writes=[bank.b], cost=mmc(n_in, 128))
            return trv

        def norm_rows(j, np_, src_ap, gain_t, gain_b, out_t, junk_t):
            xt = XR[j]
            si, sbuf_ = Bss.next()
            ss = SS[0:np_, si, :]
            P.op("act", lambda e: e.activation(out=junk_t.t[0:np_, :], in_=xt.t[0:np_, :], func=AF.Square,
                                               accum_out=ss[:, 0:1]), reads=[xt.b], writes=[junk_t.b, sbuf_], dur=1100.0)
            P.op("dve", lambda e: e.tensor_scalar(out=ss[:, 1:2], in0=ss[:, 0:1], scalar1=1.0 / D, scalar2=EPS,
                                                  op0=ALU.mult, op1=ALU.add), reads=[sbuf_], writes=[sbuf_], dur=120.0)
            P.op("pool", lambda e: e.tensor_tensor(out=ss[:, 2:3], in0=ss[:, 1:2], in1=NEGH.t[0:np_, :], op=ALU.pow),
                 reads=[sbuf_, NEGH.b], writes=[sbuf_])
            P.op("dve", lambda e: e.scalar_tensor_tensor(out=out_t.t[0:np_, :], in0=xt.t[0:np_, :], scalar=ss[:, 2:3],
                                                         in1=gain_t[0:np_, :], op0=ALU.mult, op1=ALU.mult),
                 reads=[xt.b, sbuf_, gain_b], writes=[out_t.b], dur=1300.0)

        def rope(src, sdims, dst, ddims, np_, rslot, rts, dst_buf, src_buf):
            nh = 1
            for _, c in sdims:
                nh *= c
            zero = [(0, c) for _, c in sdims]
            tdims = []
            acc = 16
            for _, c in reversed(sdims):
                tdims.insert(0, (acc, c))
                acc *= c
            rp = ROPE.t[0:np_, rslot, :]
            t1, t2 = rts

            def sub(base, off):
                return bass.AP(base.tensor, base.offset + off, base.ap)
            P.op("dve", lambda e: e.tensor_tensor(out=V(t1.t[0:np_, 0, :], tdims + [(1, 16)]), in0=V(src, sdims + [(1, 16)]),
                                                  in1=V(rp, zero + [(1, 16)]), op=ALU.mult),
                 reads=[src_buf, ROPE.b], writes=[t1.b])
            P.op("dve", lambda e: e.tensor_tensor(out=V(t2.t[0:np_, 0, :], tdims + [(1, 8)]), in0=V(sub(src, 8), sdims + [(1, 8)]),
                                                  in1=V(sub(rp, 16), zero + [(1, 8)]), op=ALU.mult),
                 reads=[src_buf, ROPE.b], writes=[t2.b])
            P.op("dve", lambda e: e.tensor_tensor(out=V(sub(t2.t[0:np_, 0, :], 8), tdims + [(1, 8)]), in0=V(src, sdims + [(1, 8)]),
                                                  in1=V(sub(rp, 24), zero + [(1, 8)]), op=ALU.mult),
                 reads=[src_buf, ROPE.b], writes=[t2.b])
            P.op("dve", lambda e: e.tensor_tensor(out=V(dst, ddims + [(1, 16)]), in0=V(t1.t[0:np_, 0, :], tdims + [(1, 16)]),
                                                  in1=V(t2.t[0:np_, 0, :], tdims + [(1, 16)]), op=ALU.add),
                 reads=[t1.b, t2.b], writes=[dst_buf])
            P.op("act", lambda e: e.activation(out=V(sub(dst, 16), ddims + [(1, 48)]), in_=V(sub(src, 16), sdims + [(1, 48)]),
                                               func=AF.Copy), reads=[src_buf], writes=[dst_buf])

        def kv_block(np_, lhs, lhs_bufs, rslot, kat_dst, kat_buf, va_dst, va_buf, vb_dst, vb_buf):
            b1 = mm.next()
            mm_group(b1.t[0:np_, 0:256], [(lhs(kc), W[:, kc, C_KA:C_KA + 256]) for kc in range(8)],
                     lhs_bufs + [Bw["kva"]], [b1.b], ncols=256)
            rope(b1.t[0:np_, 0:64], [(64, 2)], QKk.t[0:np_, 0:64], [(64, 2)], np_, rslot, RTK, QKk.b, b1.b)
            copy(va_dst, V(b1.t[0:np_, 128:192], [(64, 2), (1, 64)]), [b1.b], [va_buf], eng="act")
            b2 = mm.next()
            mm_group(b2.t[0:np_, 0:512], [(lhs(kc), W[:, kc, C_VB:C_VB + 512]) for kc in range(8)],
                     lhs_bufs + [Bw["vb"]], [b2.b])
            copy(vb_dst, V(b2.t[0:np_, 0:64], [(64, 8), (1, 64)]), [b2.b], [vb_buf], eng="act")
            b3 = mm.next()
            trv = transposes(1, 128, QKk, 0, np_, b3)
            copy(kat_dst, trv[:, 0:np_], [b3.b], [kat_buf])

        def kb_feature(n_tok, rhs, rhs_bufs, dst, dst_bufs):
            for c in range(4):
                b = mm.next()
                mm_group(b.t[:, 0:n_tok], [(W[:, kc, C_KB + c * 128:C_KB + (c + 1) * 128], rhs(kc)) for kc in range(8)],
                         rhs_bufs + [Bw["kb"]], [b.b], ncols=n_tok)
                copy(dst(c), b.t[:, 0:n_tok], [b.b], dst_bufs)

        def do_meta():
            x0 = xr.next()
            j = XR.index(x0)
            s = P.new_dma_sem("ld_meta")
            P.dma("sp", s, lambda e: e.dma_start(out=x0.t[0:16, :], in_=meta_d), writes=[x0.b])
            norm_rows(j, 16, None, GREP, Bgrep, NBt, NBt)
            b0 = mm.next()
            trv = transposes(8, 128, NBt, 0, 16, b0)
            copy(NTM.t[:, :, :], V(trv[:, 0:16], [(128, 8), (1, 16)]), [b0.b], [NTM.b])
            kv_block(16, lambda kc: NTM.t[:, kc, 0:16], [NTM.b], 20,
                     KATM.t[:, 0:16], KATM.b, VAM.t[0:16, :, 0:64], VAM.b, VBM.t[0:16, :, 0:64], VBM.b)
            kb_feature(16, lambda kc: NTM.t[:, kc, 0:16], [NTM.b], lambda c: KBTM.t[:, c, 0:16], [KBTM.b])

        ld_sems = [P.new_dma_sem("ld%d" % i) for i in range(3)]
        st_sems = [P.new_dma_sem("st%d" % i) for i in range(3)]

        def front(t, part=None):
            s = t % 3

            def load_norm(bi):
                pb = 2 * t + bi
                x0 = xr.next()
                j = XR.index(x0)
                P.dma("sp", ld_sems[j], lambda e, x0=x0, pb=pb: e.dma_start(out=x0.t[:, :], in_=xp_d[pb * 128:(pb + 1) * 128, :]),
                      writes=[x0.b])
                norm_rows(j, 128, None, GREP, Bgrep, NBt, NBt)

            def tr_nt(bi):
                b0 = mm.next()
                trv = transposes(8, 128, NBt, 0, 128, b0)
                copy(NT[s][:, :, bi * 128:(bi + 1) * 128], V(trv[:, 0:128], [(128, 8), (1, 128)]), [b0.b], [Bnt[s][bi]], dur=950.0)

            def kv(bi):
                pb = 2 * t + bi
                slot = pb % 8
                kv_block(128, lambda kc, bi=bi: NT[s][:, kc, bi * 128:(bi + 1) * 128], [Bnt[s][bi]], pb,
                         KAT[:, slot * 128:(slot + 1) * 128], Bkat[slot],
                         VA[:, slot, :, 0:64], Bva[slot], VB[:, slot, :, 0:64], Bvb[slot])
            if part in (None, "pre"):
                load_norm(0)
            if part == "pre":
                return
            P.gate(lambda: attn_done[0] >= t - 3, "front(%d) before attn(%d)" % (t, t - 3))
            tr_nt(0)
            load_norm(1)
            kv(0)
            tr_nt(1)
            kv(1)
            slot0 = (2 * t) % 8
            kb_feature(256, lambda kc: NT[s][:, kc, 0:256], [Bnt[s][0], Bnt[s][1]],
                       lambda c: KBT[:, c, slot0 * 128:slot0 * 128 + 256], [Bkbt[slot0], Bkbt[slot0 + 1]])

            def fdone():
                front_done[0] = max(front_done[0], t)
            P.check(fdone)

        def normalize(ob, unit_is_a, k_or_u):
            den = DEN.next()
            o_den = V(ob.t[:, 64:65], [(65, 4)])
            if unit_is_a:
                k = k_or_u
                P.op("dve", lambda e: e.scalar_tensor_tensor(out=den.t[:, 0:4], in0=o_den, scalar=2.0,
                                                             in1=ES2.t[:, 4 * k:4 * k + 4], op0=ALU.mult, op1=ALU.add),
                     reads=[ob.b, ES2.b], writes=[den.b])
            else:
                P.op("dve", lambda e: e.tensor_scalar(out=den.t[:, 0:4], in0=o_den, scalar1=2.0, scalar2=None, op0=ALU.mult),
                     reads=[ob.b], writes=[den.b])
            P.op("dve", lambda e: e.reciprocal(out=den.t[:, 4:8], in_=den.t[:, 0:4]), reads=[den.b], writes=[den.b], dur=200.0)
            for g in range(4):
                col = (k_or_u * 256 + g * 64) if unit_is_a else (512 + (2 * g + k_or_u) * 64)
                P.op("dve", lambda e, g=g, col=col: e.scalar_tensor_tensor(
                    out=OAG.t[:, col:col + 64], in0=ob.t[:, g * 65:g * 65 + 64], scalar=den.t[:, 4 + g:5 + g],
                    in1=ZS.t[:, col:col + 64], op0=ALU.mult, op1=ALU.mult), reads=[ob.b, den.b, ZS.b], writes=[OAG.b], dur=360.0)

        def pv_op(ob, pt, nk, rhs_fn, first, last, reads):
            def fn(e):
                r = None
                for g in range(4):
                    r = e.matmul(out=ob.t[:, g * 65:(g + 1) * 65], lhsT=pt.t[0:nk, g * 128:(g + 1) * 128], rhs=rhs_fn(g),
                                 start=(first and g == 0), stop=(last and g == 3), skip_group_check=True)
                return r
            P.op("pe", fn, reads=[pt.b] + reads, writes=[ob.b], cost=220.0)

        def piece_idx(m, jj):
            if m == 0:
                return 5 + jj
            if m == 1:
                return 11 + jj
            if m == 14:
                return 16 + jj
            if m == 15:
                return 21 + jj
            return jj

        tb_sems = [P.new_dma_sem("tb%d" % i) for i in range(3)]

        def attn_block(pb, bi, extra=None):
            lb = pb - 2
            tasks = []
            obsA = [o_ring.next(), o_ring.next()]
            chunksA = [("M", 16, None), ("L", 128, pb - 1), ("C", 128, pb), ("R", 128, pb + 1)]
            for ci, (typ, nk, kb_) in enumerate(chunksA):
                def S(state, typ=typ, nk=nk, kb_=kb_):
                    sts = [st_ring.next(), st_ring.next()]
                    if typ == "M":
                        kbuf = KATM.b
                        lhs_fn = lambda k: KATM.t[64 * k:64 * k + 64, 0:16]
                    else:
                        slot = kb_ % 8
                        kbuf = Bkat[slot]
                        lhs_fn = lambda k, slot=slot: KAT[64 * k:64 * k + 64, slot * 128:(slot + 1) * 128]

                    def sfn(e):
                        r = None
                        for k in range(2):
                            r = e.matmul(out=sts[k].t[0:nk, 0:512], lhsT=lhs_fn(k),
                                         rhs=V(QTA[64 * k:64 * k + 64, 0, bi * 128:(bi + 1) * 128], [(256, 4), (1, 128)]),
                                         start=True, stop=True)
                        return r
                    P.op("pe", sfn, reads=[kbuf, Bqta[bi]], writes=[sts[0].b, sts[1].b], cost=410.0)
                    state["pts"] = []
                    for k in range(2):
                        st = sts[k]
                        pt = PT.next()
                        P.op("act", lambda e, st=st, pt=pt: e.activation(out=pt.t[0:nk, :], in_=st.t[0:nk, 0:512], func=AF.Exp, scale=0.125),
                             reads=[st.b], writes=[pt.b], dur=560.0)
                        if typ in ("L", "R"):
                            mt_ = (2 if lb == 0 else 0) if typ == "L" else (3 if lb == 15 else 1)
                            P.op("dve", lambda e, pt=pt, mt_=mt_: e.tensor_tensor(out=V(pt.t[:, 0:128], [(128, 4), (1, 128)]),
                                                                                  in0=V(pt.t[:, 0:128], [(128, 4), (1, 128)]),
                                                                                  in1=V(AM.t[:, mt_, :], [(0, 4), (1, 128)]), op=ALU.mult),
                                 reads=[pt.b, AM.b], writes=[pt.b])
                        state["pts"].append(pt)

                def PV(state, ci=ci, typ=typ, nk=nk, kb_=kb_):
                    for k in range(2):
                        if typ == "M":
                            vb_, rhs_v = VAM.b, (lambda g, k=k: VAM.t[0:16, k, :])
                        else:
                            slot = kb_ % 8
                            vb_, rhs_v = Bva[slot], (lambda g, k=k, slot=slot: VA[:, slot, k, :])
                        pv_op(obsA[k], state["pts"][k], nk, rhs_v, ci == 0, ci == 3, [vb_])
                    if ci == 3:
                        for k in range(2):
                            normalize(obsA[k], True, k)
                tasks.append((S, PV, {}))
            m = lb
            js = list(range(0, 6)) if m == 0 else (list(range(-1, 5)) if m == 15 else list(range(0, 5)))
            obsB = [o_ring.next(), o_ring.next()]
            chunksB = [("M", 16, None, None)] + [("W", 128, pb + j - 2, jj) for jj, j in enumerate(js)]
            nB = len(chunksB)
            for ci, (typ, nk, kb_, jj) in enumerate(chunksB):
                def S(state, typ=typ, nk=nk, kb_=kb_, jj=jj):
                    tb = None
                    if typ == "W":
                        tb = TBE.next()
                        ti = TBE.items.index(tb)
                        pi = piece_idx(m, jj)
                        P.dma("pool", tb_sems[ti], lambda e, tb=tb, pi=pi: e.dma_start(out=tb.t[:, :], in_=btab_d[pi * 128:(pi + 1) * 128, :]),
                              writes=[tb.b])
                    sts = [st_ring.next(), st_ring.next()]
                    if typ == "M":
                        kbuf = KBTM.b
                        lhs_fn = lambda c, u: KBTM.t[64 * u:64 * u + 64, c, 0:16]
                    else:
                        slot = kb_ % 8
                        kbuf = Bkbt[slot]
                        lhs_fn = lambda c, u, slot=slot: KBT[64 * u:64 * u + 64, c, slot * 128:(slot + 1) * 128]

                    def sfn(e):
                        r = None
                        for c in range(4):
                            for u in range(2):
                                r = e.matmul(out=sts[u].t[0:nk, c * 128:(c + 1) * 128], lhsT=lhs_fn(c, u),
                                             rhs=QTB.t[64 * u:64 * u + 64, c, bi * 128:(bi + 1) * 128], start=True, stop=True)
                        return r
                    P.op("pe", sfn, reads=[kbuf, QTB.b], writes=[sts[0].b, sts[1].b], cost=460.0)
                    state["pts"] = []
                    for u in range(2):
                        st = sts[u]
                        pt = PT.next()
                        if typ == "W":
                            P.op("dve", lambda e, st=st, tb=tb, u=u: e.scalar_tensor_tensor(
                                out=st.t[:, 0:512], in0=st.t[:, 0:512], scalar=0.125,
                                in1=V(tb.t[:, u * 128:(u + 1) * 128], [(256, 4), (1, 128)]), op0=ALU.mult, op1=ALU.add),
                                reads=[st.b, tb.b], writes=[st.b], dur=520.0)
                            P.op("act", lambda e, st=st, pt=pt: e.activation(out=pt.t[:, :], in_=st.t[:, 0:512], func=AF.Exp),
                                 reads=[st.b], writes=[pt.b], dur=620.0)
                        else:
                            P.op("act", lambda e, st=st, pt=pt: e.activation(out=pt.t[0:16, :], in_=st.t[0:16, 0:512], func=AF.Exp, scale=0.125),
                                 reads=[st.b], writes=[pt.b])
                        state["pts"].append(pt)

                def PV(state, ci=ci, typ=typ, nk=nk, kb_=kb_):
                    for u in range(2):
                        if typ == "M":
                            vb_, rhs_v = VBM.b, (lambda c, u=u: VBM.t[0:16, 2 * c + u, :])
                        else:
                            slot = kb_ % 8
                            vb_, rhs_v = Bvb[slot], (lambda c, u=u, slot=slot: VB[:, slot, 2 * c + u, :])
                        pv_op(obsB[u], state["pts"][u], nk, rhs_v, ci == 0, ci == nB - 1, [vb_])
                    if ci == nB - 1:
                        for u in range(2):
                            normalize(obsB[u], False, u)
                tasks.append((S, PV, {}))
            prev = None
            for ti, (S, PV, state) in enumerate(tasks):
                S(state)
                if prev is not None:
                    prev[0](prev[1])
                prev = (PV, state)
                if extra is not None and ti == extra[0]:
                    extra[1]()
            prev[0](prev[1])

        oag_set = set()
        dc_done = [0]
        front_done = [1]
        attn_done = [0]

        def qproj(t):
            s = t % 3

            def qa(bi):
                pb = 2 * t + bi
                b = st_ring.next()
                mm_group(b.t[:, 0:512], [(NT[s][:, kc, bi * 128:(bi + 1) * 128], W[:, kc, C_QA:C_QA + 512]) for kc in range(8)],
                         [Bnt[s][bi], Bw["qa"]], [b.b])
                rope(b.t[:, 0:64], [(256, 2), (64, 4)], QKq2[bi].t[:, 0:64], [(64, 2), (128, 4)], 128, pb, RT, QKq2[bi].b, b.b)

            def qtr(bi):
                b2 = st_ring.next()
                trv = transposes(4, 128, QKq2[bi], 0, 128, b2)
                copy(QTA[:, :, bi * 128:(bi + 1) * 128], V(trv[:, 0:128], [(128, 4), (1, 128)]), [b2.b], [Bqta[bi]])

            def qb(c):
                b = st_ring.next()
                mm_group(b.t[:, 0:256], [(W[:, kc, C_QB + c * 128:C_QB + (c + 1) * 128], NT[s][:, kc, 0:256]) for kc in range(8)],
                         [Bnt[s][0], Bnt[s][1], Bw["qb"]], [b.b], ncols=256)
                copy(QTB.t[:, c, :], b.t[:, 0:256], [b.b], [QTB.b])
            qa(0)
            qa(1)
            qb(0)
            qtr(0)
            qb(1)
            qtr(1)
            qb(2)
            qb(3)

        zstate = {}

        def attn(t, part, sec=None):
            s = t % 3
            zbanks = zstate.setdefault(t, [])
            if part == 0:
                P.gate(lambda: front_done[0] >= t + 1, "attn(%d) before front(%d)" % (t, t + 1))
            if part == 1 and sec in (None, "a"):
                for br, c0, wn in ((0, C_ZA, "za"), (1, C_ZB, "zb")):
                    b = st_ring.next()
                    mm_group(b.t[:, 0:512], [(NT[s][:, kc, 128:256], W[:, kc, c0:c0 + 512]) for kc in range(8)],
                             [Bnt[s][1], Bw[wn]], [b.b])
                    zbanks.append(b)

            def oag_b0():
                P.gate(lambda: dc_done[0] >= t - 2, "OAGT blk0 of tile %d before back_dc(%d)" % (t, t - 2))
                b2 = st_ring.next()
                trv = transposes(8, 128, OAG, 0, 128, b2)
                copy(OAGT2[t % 2][:, :, 0:128], V(trv[:, 0:128], [(128, 8), (1, 128)]), [b2.b], [Boagt2[t % 2][0]], dur=950.0)

                def set0():
                    oag_set.add((t, 0))
                P.check(set0)
            if part == 1 and sec in (None, "a") and t == 8:
                oag_b0()
            for bi in ((0,) if part == 0 else (1,)):
                pb = 2 * t + bi
                for br, c0, wn in ((0, C_ZA, "za"), (1, C_ZB, "zb")):
                    if part == 1 and sec == "b":
                        continue
                    if part == 1:
                        b = zbanks[br]
                        th = THX
                        P.op("act", lambda e, b=b, th=th: e.activation(out=th.t[:, :], in_=b.t[:, 0:512], func=AF.Tanh, scale=0.5),
                             reads=[b.b], writes=[th.b])
                        P.op("dve", lambda e, b=b, th=th, br=br: e.scalar_tensor_tensor(
                            out=ZS.t[:, br * 512:(br + 1) * 512], in0=th.t[:, :], scalar=1.0, in1=b.t[:, 0:512],
                            op0=ALU.add, op1=ALU.mult), reads=[th.b, b.b], writes=[ZS.b])
                        continue
                    b = st_ring.next()
                    mm_group(b.t[:, 0:512], [(NT[s][:, kc, bi * 128:(bi + 1) * 128], W[:, kc, c0:c0 + 512]) for kc in range(8)],
                             [Bnt[s][bi], Bw[wn]], [b.b])
                    th = THX
                    P.op("act", lambda e, b=b, th=th: e.activation(out=th.t[:, :], in_=b.t[:, 0:512], func=AF.Tanh, scale=0.5),
                         reads=[b.b], writes=[th.b])
                    P.op("dve", lambda e, b=b, th=th, br=br: e.scalar_tensor_tensor(
                        out=ZS.t[:, br * 512:(br + 1) * 512], in0=th.t[:, :], scalar=1.0, in1=b.t[:, 0:512],
                        op0=ALU.add, op1=ALU.mult), reads=[th.b, b.b], writes=[ZS.b])
                if part == 1 and sec == "a":
                    continue
                attn_block(pb, bi, extra=((1, oag_b0) if (bi == 1 and t < 8) else None))
                if bi == 1:
                    if t < 8:
                        qproj(t + 1)
                    P.gate(lambda: dc_done[0] >= t - 2, "OAGT blk1 of tile %d before back_dc(%d)" % (t, t - 2))
                    b2 = st_ring.next()
                    trv = transposes(8, 128, OAG, 0, 128, b2)
                    copy(OAGT2[t % 2][:, :, 128:256], V(trv[:, 0:128], [(128, 8), (1, 128)]), [b2.b], [Boagt2[t % 2][1]], dur=950.0)

                    def set1():
                        oag_set.add((t, 1))
                        attn_done[0] = t
                    P.check(set1)

        def back_dc(t, blks=(0, 1)):
            s = t % 3
            lo, n = blks[0] * 128, 128 * len(blks)

            P.gate(lambda: all((t, bb) in oag_set for bb in blks), "back_dc(%d,%s) before its OAGT" % (t, blks))
            for dc in range(8):
                bA, bB = mm.next(), mm.next()
                th = THY
                for bnk, wp, bwp, koff, cg, half in ((bA, WPA, Bwpa, 0, C_GA, 0), (bB, WPB, Bwpb, 4, C_GB, 1)):
                    mm_group(bnk.t[:, 0:n], [(wp[:, kc, dc * 128:(dc + 1) * 128], OAGT2[t % 2][:, koff + kc, lo:lo + n]) for kc in range(4)],
                             [bwp] + [Boagt2[t % 2][bb] for bb in blks], [bnk.b], ncols=n)
                    gc = cg + dc * 128
                    mm_group(bnk.t[:, 256:256 + n], [(W[:, kc, gc:gc + 128], NT[s][:, kc, lo:lo + n]) for kc in range(8)],
                             [wbuf(gc)] + [Bnt[s][bb] for bb in blks], [bnk.b], ncols=n)
                    P.op("act", lambda e, bnk=bnk, th=th, half=half: e.activation(out=th.t[:, half * 256:half * 256 + n], in_=bnk.t[:, 256:256 + n],
                                                                                 func=AF.Tanh, scale=0.5), reads=[bnk.b], writes=[th.b])
                    P.op("dve", lambda e, bnk=bnk, th=th, half=half: e.scalar_tensor_tensor(
                        out=th.t[:, half * 256:half * 256 + n], in0=th.t[:, half * 256:half * 256 + n], scalar=1.0,
                        in1=bnk.t[:, 0:n], op0=ALU.add, op1=ALU.mult), reads=[th.b, bnk.b], writes=[th.b])
                P.op("dve", lambda e, th=th, dc=dc: e.tensor_tensor(out=MT[:, dc, lo:lo + n], in0=th.t[:, 0:n], in1=th.t[:, 256:256 + n], op=ALU.add),
                     reads=[th.b], writes=[Bmt[dc]])

            def done():
                if len(blks) == 2 or blks[0] == 1:
                    dc_done[0] = t
            P.check(done)

        def back_out(t, blks=(0, 1)):
            xs = {}
            for bi in blks:
                pb = 2 * t + bi
                x0 = xr.next()
                j = XR.index(x0)
                P.dma("sp", ld_sems[j], lambda e, x0=x0, pb=pb: e.dma_start(out=x0.t[:, :], in_=xp_d[pb * 128:(pb + 1) * 128, :]),
                      writes=[x0.b])
                xs[bi] = (x0, j)
            for bi in blks:
                pb = 2 * t + bi
                lb = pb - 2
                x0, j = xs[bi]
                for half in range(2):
                    b = mm.next()
                    mm_group(b.t[:, 0:512], [(MT[:, kc, bi * 128:(bi + 1) * 128], WO[:, kc, half * 512:(half + 1) * 512]) for kc in range(8)],
                             Bmt + [Bwo[half]], [b.b])
                    P.op("dve", lambda e, b=b, x0=x0, half=half: e.scalar_tensor_tensor(
                        out=x0.t[:, half * 512:(half + 1) * 512], in0=b.t[:, 0:512], scalar=0.5,
                        in1=x0.t[:, half * 512:(half + 1) * 512], op0=ALU.mult, op1=ALU.add), reads=[b.b, x0.b], writes=[x0.b], dur=560.0)
                norm_rows(j, 128, None, FGREP, Bfgrep, x0, NBt)
                P.dma("sp", st_sems[j], lambda e, x0=x0, lb=lb: e.dma_start(out=y_d[lb * 128:(lb + 1) * 128, :], in_=x0.t[:, :]),
                      reads=[x0.b])

        import os
        if os.environ.get('KDEBUG'):
            print('SBUF base/top', nc.sbuf_base, nc.sbuf_top, 'free', nc.sbuf_top - nc.sbuf_base)
        do_meta()
        wdma(["qa", "qb"])
        front(0)
        wdma(["za", "zb", "ga0", "ga1"])
        front(1)
        wdma(["gb0", "gb1", "w_pa", "w_pb", "w_o0", "w_o1"])
        def xthread():
            qproj(1)
            for t in range(1, 9):
                attn(t, 0)
                if t < 8:
                    attn(t, 1)
                else:
                    attn(t, 1, "a")
                    attn(t, 1, "b")

        def ythread():
            front(2)
            front(3)
            for t in range(2, 9):
                if t + 2 <= 9:
                    front(t + 2, "pre")
                back_dc(t - 1)
                if t + 2 <= 9:
                    front(t + 2, "rest")
                back_out(t - 1)
            back_dc(8, (0,))
            back_dc(8, (1,))
            back_out(8, (0,))
            back_out(8, (1,))
        P.schedule([P.record(xthread), P.record(ythread)])
        final = [(s, P.seq[s]) for s in st_sems if P.seq[s] > 0]
        P.emit(final)
    return nc


def _rope_table(c):
    half = 8
    inv_freq = (np.float32(500000.0) ** (-np.arange(half, dtype=np.float32) / np.float32(half))).astype(np.float32)
    tab = np.zeros((128, 21, 32), np.float32)
    p = np.arange(128)
    for pb in range(NPB):
        tok = c * TOK - HALO + pb * 128 + p
        pos = (tok + NMETA).astype(np.float32)
        ang = (pos[:, None] * inv_freq[None, :]).astype(np.float32)
        cs, sn = np.cos(ang).astype(np.float32), np.sin(ang).astype(np.float32)
        tab[:, pb, 0:8] = cs
        tab[:, pb, 8:16] = cs
        tab[:, pb, 16:24] = -sn
        tab[:, pb, 24:32] = sn
    pos = np.arange(128).astype(np.float32)
    ang = (pos[:, None] * inv_freq[None, :]).astype(np.float32)
    cs, sn = np.cos(ang).astype(np.float32), np.sin(ang).astype(np.float32)
    tab[:, 20, 0:8] = cs
    tab[:, 20, 8:16] = cs
    tab[:, 20, 16:24] = -sn
    tab[:, 20, 24:32] = sn
    return tab


def _amask(c):
    j = np.arange(128)[:, None]
    i = np.arange(128)[None, :]
    L = (j >= i).astype(np.float32)
    R = (j <= i).astype(np.float32)
    am = np.zeros((128, 4, 128), np.float32)
    am[:, 0] = L
    am[:, 1] = R
    am[:, 2] = L if c > 0 else 0.0
    am[:, 3] = R if c < 3 else 0.0
    return am


def _btab_piece(rpb, c, m, j):
    R0 = 32 * c + 2 * m
    KR0 = R0 + 2 * (j - 2)
    kp = np.arange(128)
    qp = np.arange(128)
    kR = KR0 + kp // 64
    kc = kp % 64
    r = R0 + qp // 64
    qc = qp % 64
    r_start = np.clip(r - 4, 0, 128 - 8)
    cstart = np.clip(qc - 8, 0, 64 - 16)
    ok_r = (kR[:, None] >= r_start[None, :]) & (kR[:, None] < r_start[None, :] + 8) & (kR[:, None] >= 0) & (kR[:, None] < 128)
    ok_c = (kc[:, None] >= cstart[None, :]) & (kc[:, None] < cstart[None, :] + 16)
    ok = ok_r & ok_c
    dr = np.clip(kR[:, None] - r[None, :] + 7, 0, 14)
    dc = np.clip(kc[:, None] - qc[None, :] + 15, 0, 30)
    out = np.full((128, 8, 128), NEG, np.float32)
    for h in range(8):
        g = rpb[h][dr, dc]
        out[:, h, :] = np.where(ok, g, np.float32(NEG))
    return out


def _btab(rpb, c):
    pieces = []
    for j in range(5):
        pieces.append(_btab_piece(rpb, 1, 8, j))
    for j in range(6):
        pieces.append(_btab_piece(rpb, c, 0, j))
    for j in range(5):
        pieces.append(_btab_piece(rpb, c, 1, j))
    for j in range(5):
        pieces.append(_btab_piece(rpb, c, 14, j))
    for j in range(-1, 5):
        pieces.append(_btab_piece(rpb, c, 15, j))
    return np.stack(pieces).reshape(27 * 128, 1024)


_NC_CACHE = {}


def kernel(x, meta_tokens, norm_gain, w_in, sink_logits, rel_pos_bias, w_proj_a, w_proj_b, w_out, final_norm_gain):
    f = np.float32
    x = np.asarray(x, f)
    w_in_l = np.ascontiguousarray(np.asarray(w_in, f)[0].reshape(8, 128, NCOL).transpose(1, 0, 2))
    w_pa_l = np.ascontiguousarray(np.asarray(w_proj_a, f)[0].reshape(4, 128, D).transpose(1, 0, 2))
    w_pb_l = np.ascontiguousarray(np.asarray(w_proj_b, f)[0].reshape(4, 128, D).transpose(1, 0, 2))
    w_out_l = np.ascontiguousarray(np.asarray(w_out, f)[0].reshape(8, 128, D).transpose(1, 0, 2))
    gain = np.ascontiguousarray(np.asarray(norm_gain, f).reshape(1, D))
    fgain = np.ascontiguousarray(np.asarray(final_norm_gain, f).reshape(1, D))
    sink = np.ascontiguousarray(np.asarray(sink_logits, f).reshape(1, 8))
    meta = np.ascontiguousarray(np.asarray(meta_tokens, f))
    rpb = np.asarray(rel_pos_bias, f)[0]
    ropes = [_rope_table(c) for c in range(4)]
    amasks = [_amask(c) for c in range(4)]
    btabs = [_btab(rpb, c) for c in range(4)]
    in_maps = []
    for ci in range(8):
        b, c = divmod(ci, 4)
        xp = np.zeros((NPB * 128, D), f)
        lo, hi = c * TOK - HALO, c * TOK + TOK + HALO
        slo, shi = max(lo, 0), min(hi, SEQ)
        xp[slo - lo:shi - lo] = x[b, slo:shi]
        in_maps.append({"xp": xp, "meta": meta, "w_in": w_in_l, "w_pa": w_pa_l, "w_pb": w_pb_l, "w_out": w_out_l,
                        "gain": gain, "fgain": fgain, "sink": sink, "rope": ropes[c], "amask": amasks[c], "btab": btabs[c]})
    if "nc" not in _NC_CACHE:
        _NC_CACHE["nc"] = build_nc()
    res = run_bass_kernel_spmd(_NC_CACHE["nc"], in_maps, core_ids=list(range(8)))
    out = np.zeros((2, SEQ, D), f)
    for ci in range(8):
        b, c = divmod(ci, 4)
        out[b, c * TOK:(c + 1) * TOK] = res.results[ci]["y"]
    return out
```

```python
import numpy as np
from contextlib import ExitStack
import concourse.bass as bass
import concourse.mybir as mybir
from concourse.bass_utils import run_bass_kernel_spmd

F32 = mybir.dt.float32
BF16 = mybir.dt.bfloat16
AF = mybir.ActivationFunctionType
ALU = mybir.AluOpType

D = 1024
NCOL = 5376
SEQ = 8192
NMETA = 16
TOK = 2048
HALO = 256
NPB = 20
EPS = 1e-6
NEG = -30000.0
C_QA, C_KA, C_VA, C_ZA, C_QB, C_KB, C_VB, C_ZB, C_GA, C_GB = 0, 512, 640, 768, 1280, 1792, 2304, 2816, 3328, 4352
W_PIECES = [("kva", 512, 768), ("vb", 2304, 2816), ("kb", 1792, 2304), ("qa", 0, 512), ("qb", 1280, 1792),
            ("za", 768, 1280), ("zb", 2816, 3328), ("ga0", 3328, 3840), ("ga1", 3840, 4352),
            ("gb0", 4352, 4864), ("gb1", 4864, 5376)]


class Buf:
    __slots__ = ("name", "lw", "rd")

    def __init__(self, name):
        self.name = name
        self.lw = None
        self.rd = []


class Prog:
    ENGS = ("pe", "act", "dve", "pool", "sp")

    def __init__(self, nc, same_raw=True):
        self.nc = nc
        self.stream = {e: [] for e in self.ENGS}
        self.seq = {e: 0 for e in self.ENGS}
        self.known = {e: {} for e in self.ENGS}
        self.same_raw = same_raw
        self.dma_sems = []
        self.rec = None
        self.efree = {}
        self.done = {}

    def _deps(self, eng, reads, writes):
        deps = {}

        def add(d, raw):
            key, val = d
            if key == eng and (eng == "pe" or not self.same_raw):
                return
            if deps.get(key, 0) < val:
                deps[key] = val

        for b in reads:
            if b.lw is not None:
                add(b.lw, True)
        for b in writes:
            if b.lw is not None:
                add(b.lw, False)
            for r in b.rd:
                add(r, False)
        out = []
        kn = self.known[eng]
        for key, val in deps.items():
            if kn.get(key, 0) < val:
                kn[key] = val
                out.append((key, val))
        return out

    @staticmethod
    def _mark(me, reads, writes):
        for b in reads:
            b.rd.append(me)
        for b in writes:
            b.lw = me
            b.rd = []

    DUR = {"act": 450.0, "dve": 350.0, "pool": 500.0, "sp": 60.0, "pe": 100.0}

    def op(self, eng, fn, reads=(), writes=(), cost=0.0, dur=None):
        if dur is None:
            dur = cost if eng == "pe" else self.DUR[eng]
        item = ("op", eng, fn, tuple(reads), tuple(writes), cost, dur, None)
        if self.rec is not None:
            self.rec.append(item)
            return
        self._play_item(item)

    def dma(self, eng, sem, fn, reads=(), writes=()):
        item = ("dma", eng, fn, tuple(reads), tuple(writes), 0.0, 60.0, sem)
        if self.rec is not None:
            self.rec.append(item)
            return
        self._play_item(item)

    def check(self, f):
        item = ("chk", None, f, (), (), 0.0, 0.0, None)
        if self.rec is not None:
            self.rec.append(item)
        else:
            f()

    def record(self, f):
        self.rec = []
        f()
        r, self.rec = self.rec, None
        return r

    def _ready(self, eng, reads, writes):
        t = 0.0
        for b in reads:
            if b.lw is not None:
                t = max(t, self.done.get(b.lw, 0.0))
        for b in writes:
            if b.lw is not None and b.lw[0] != eng:
                t = max(t, self.done.get(b.lw, 0.0))
            for r in b.rd:
                if r[0] != eng:
                    t = max(t, self.done.get(r, 0.0))
        return t

    def _play_item(self, item):
        kind, eng, fn, reads, writes, cost, dur, sem = item
        if kind == "chk":
            fn()
            return
        start = max(self.efree.get(eng, 0.0), self._ready(eng, reads, writes) + 120.0)
        if kind == "op":
            self._op(eng, fn, reads, writes)
            me = (eng, self.seq[eng])
            if eng == "pe":
                self.efree[eng] = start + dur
                self.done[me] = start + dur + 180.0
            else:
                self.efree[eng] = start + dur
                self.done[me] = start + dur
        else:
            self._dma(eng, sem, fn, reads, writes)
            me = (sem, self.seq[sem])
            self.efree[eng] = start + dur
            self.done[me] = start + 3500.0

    def play(self, items):
        for it in items:
            self._play_item(it)

    def schedule(self, threads):
        def bundles(L):
            out = []
            for it in L:
                if (it[0] == "op" and it[1] == "pe") or not out:
                    out.append([it])
                else:
                    out[-1].append(it)
            return out
        bl = [bundles(L) for L in threads]
        pos = [0] * len(bl)
        rem = [sum(it[5] for bd in b for it in bd) for b in bl]
        while True:
            best = None
            for i, b in enumerate(bl):
                if pos[i] >= len(b):
                    continue
                first = next((it for it in b[pos[i]] if it[0] != "chk"), None)
                if first is None:
                    est = 0.0
                else:
                    est = max(self.efree.get(first[1], 0.0), self._ready(first[1], first[3], first[4]) + 120.0)
                key = (est - 0.02 * rem[i], i)
                if best is None or key < best[0]:
                    best = (key, i)
            if best is None:
                break
            i = best[1]
            bd = bl[i][pos[i]]
            pos[i] += 1
            rem[i] -= sum(it[5] for it in bd)
            for it in bd:
                self._play_item(it)

    def _op(self, eng, fn, reads, writes):
        waits = self._deps(eng, reads, writes)
        self.seq[eng] += 1
        self._mark((eng, self.seq[eng]), reads, writes)
        self.stream[eng].append((waits, fn, eng, 1))

    def new_dma_sem(self, name):
        self.dma_sems.append(name)
        self.seq[name] = 0
        return name

    def _dma(self, eng, sem, fn, reads, writes):
        waits = self._deps(eng, reads, writes)
        self.seq[sem] += 16
        self._mark((sem, self.seq[sem]), reads, writes)
        self.stream[eng].append((waits, fn, sem, 16))

    def emit(self, final_waits):
        nc = self.nc
        with ExitStack() as es:
            H = {}
            for k in list(self.ENGS) + self.dma_sems:
                H[k] = es.enter_context(nc.semaphore("s_" + k))
            blk = es.enter_context(nc.Block())

            def run(e, items):
                for waits, fn, key, inc in items:
                    for wk, wv in waits:
                        e.wait_ge(H[wk], wv)
                    if fn is not None:
                        fn(e).then_inc(H[key], inc)

            self.stream["sp"].append((final_waits, None, None, 0))
            blk.tensor(lambda e: run(e, self.stream["pe"]))
            blk.scalar(lambda e: run(e, self.stream["act"]))
            blk.vector(lambda e: run(e, self.stream["dve"]))
            blk.gpsimd(lambda e: run(e, self.stream["pool"]))
            blk.sync(lambda e: run(e, self.stream["sp"]))


def V(base, dims):
    return bass.AP(base.tensor, base.offset, [list(base.ap[0])] + [list(d) for d in dims])


class Ring:
    def __init__(self, items):
        self.items = items
        self.i = 0

    def next(self):
        it = self.items[self.i % len(self.items)]
        self.i += 1
        return it


class T:
    __slots__ = ("t", "b")

    def __init__(self, t, name):
        self.t = t
        self.b = Buf(name)


def build_nc():
    nc = bass.Bass("TRN2", target_bir_lowering=False, dynamic_dma_scratch_size=12288)

    def din(name, shape):
        return nc.dram_tensor(name, shape, F32, kind="ExternalInput").ap()

    xp_d = din("xp", [NPB * 128, D])
    meta_d = din("meta", [NMETA, D])
    win_d = din("w_in", [128, 8, NCOL])
    wpa_d = din("w_pa", [128, 4, D])
    wpb_d = din("w_pb", [128, 4, D])
    wo_d = din("w_out", [128, 8, D])
    gain_d = din("gain", [1, D])
    fgain_d = din("fgain", [1, D])
    sink_d = din("sink", [1, 8])
    rope_d = din("rope", [128, 21, 32])
    amask_d = din("amask", [128, 4, 128])
    btab_d = din("btab", [27 * 128, 1024])
    y_d = nc.dram_tensor("y", [TOK, D], F32, kind="ExternalOutput").ap()

    with ExitStack() as es:
        def sb(name, shape, dt=BF16):
            return es.enter_context(nc.sbuf_tensor(name, shape, dt))

        def ps(name, shape, dt=F32):
            return es.enter_context(nc.psum_tensor(name, shape, dt))

        P = Prog(nc)
        W = sb("W", [128, 8, NCOL])
        WPA = sb("WPA", [128, 4, D])
        WPB = sb("WPB", [128, 4, D])
        WO = sb("WO", [128, 8, D])
        GREP = sb("GREP", [128, D], F32)
        FGREP = sb("FGREP", [128, D], F32)
        XR = [T(sb("XR%d" % i, [128, D], F32), "XR%d" % i) for i in range(3)]
        NBt = T(sb("NB", [128, D]), "NB")
        NT = [sb("NT%d" % i, [128, 8, 256]) for i in range(3)]
        Bnt = [[Buf("nt%d_%d" % (i, j)) for j in range(2)] for i in range(3)]
        NTM = T(sb("NTM", [128, 8, 16]), "NTM")
        QKq2 = [T(sb("QKq%d" % i, [128, 512]), "QKq%d" % i) for i in range(2)]
        QKk = T(sb("QKk", [128, 128]), "QKk")
        QTA = sb("QTA", [128, 4, 256])
        Bqta = [Buf("qta0"), Buf("qta1")]
        QTB = T(sb("QTB", [128, 4, 256]), "QTB")
        KAT = sb("KAT", [128, 8 * 128])
        VA = sb("VA", [128, 8, 2, 65])
        KBT = sb("KBT", [128, 4, 8 * 128])
        VB = sb("VB", [128, 8, 8, 65])
        Bkat = [Buf("kat%d" % i) for i in range(8)]
        Bva = [Buf("va%d" % i) for i in range(8)]
        Bkbt = [Buf("kbt%d" % i) for i in range(8)]
        Bvb = [Buf("vb%d" % i) for i in range(8)]
        KATM = T(sb("KATM", [128, 16]), "KATM")
        KBTM = T(sb("KBTM", [128, 4, 16]), "KBTM")
        VAM = T(sb("VAM", [128, 2, 65]), "VAM")
        VBM = T(sb("VBM", [128, 8, 65]), "VBM")
        PT = Ring([T(sb("PT%d" % i, [128, 512]), "PT%d" % i) for i in range(4)])
        TBE = Ring([T(sb("TBE%d" % i, [128, 1024]), "TBE%d" % i) for i in range(2)])
        AM = T(sb("AM", [128, 4, 128]), "AM")
        ROPE = T(sb("ROPE", [128, 21, 32], F32), "ROPE")
        THX = T(sb("THX", [128, 512], F32), "THX")
        THY = T(sb("THY", [128, 512], F32), "THY")
        ZS = T(sb("ZS", [128, D]), "ZS")
        OAG = T(sb("OAG", [128, D]), "OAG")
        OAGT2 = [sb("OAGT%d" % i, [128, 8, 256]) for i in range(2)]
        Boagt2 = [[Buf("oagt%d_%d" % (i, j)) for j in range(2)] for i in range(2)]
        MT = sb("MT", [128, 8, 256])
        Bmt = [Buf("mt%d" % i) for i in range(8)]
        SS = sb("SS", [128, 4, 4], F32)
        Bss = Ring([(i, Buf("ss%d" % i)) for i in range(4)])
        RT = [T(sb("RT%d" % i, [128, 8, 16], F32), "RT%d" % i) for i in range(2)]
        RTK = [T(sb("RTK%d" % i, [128, 2, 16], F32), "RTK%d" % i) for i in range(2)]
        DEN = Ring([T(sb("DEN%d" % i, [128, 8], F32), "DEN%d" % i) for i in range(2)])
        ES2 = T(sb("ES2", [128, 8], F32), "ES2")
        NEGH = T(sb("NEGH", [128, 1], F32), "NEGH")
        ID = T(sb("ID", [128, 128]), "ID")

        MMB = [T(ps("MM%d" % i, [128, 512]), "MM%d" % i) for i in range(2)]
        STB = [T(ps("ST%d" % i, [128, 512]), "ST%d" % i) for i in range(4)]
        OB = [T(ps("O%d" % i, [128, 512]), "O%d" % i) for i in range(2)]
        mm = Ring(MMB)
        st_ring = Ring(STB)
        o_ring = Ring(OB)
        xr = Ring(XR)

        def cdma(eng, name, out_ap, in_ap, buf):
            s = P.new_dma_sem(name)
            P.dma(eng, s, lambda e: e.dma_start(out=out_ap, in_=in_ap), writes=[buf])

        Bgrep, Bfgrep = Buf("grep"), Buf("fgrep")
        cdma("sp", "c_grep", GREP[:, :], gain_d.partition_broadcast(128), Bgrep)
        cdma("sp", "c_rope", ROPE.t[:, :, :], rope_d, ROPE.b)
        cdma("sp", "c_sink", ES2.t[:, :], sink_d.partition_broadcast(128), ES2.b)
        cdma("pool", "c_am", AM.t[:, :, :], amask_d, AM.b)
        Bw = {name: Buf("w_" + name) for name, _, _ in W_PIECES}
        Bwpa, Bwpb, Bwo = Buf("wpa"), Buf("wpb"), [Buf("wo0"), Buf("wo1")]

        def wdma(names):
            for name in names:
                if name == "w_pa":
                    cdma("pool", "w_pa", WPA[:, :, :], wpa_d, Bwpa)
                elif name == "w_pb":
                    cdma("pool", "w_pb", WPB[:, :, :], wpb_d, Bwpb)
                elif name == "w_o0":
                    cdma("pool", "w_o0", WO[:, :, 0:512], wo_d[:, :, 0:512], Bwo[0])
                elif name == "w_o1":
                    cdma("pool", "w_o1", WO[:, :, 512:1024], wo_d[:, :, 512:1024], Bwo[1])
                else:
                    c0, c1 = [(a, b) for n_, a, b in W_PIECES if n_ == name][0]
                    cdma("pool", "w_" + name, W[:, :, c0:c1], win_d[:, :, c0:c1], Bw[name])
        wdma(["kva", "vb", "kb"])
        cdma("sp", "c_fgrep", FGREP[:, :], fgain_d.partition_broadcast(128), Bfgrep)

        def wbuf(c0):
            for name, a, b in W_PIECES:
                if a <= c0 < b:
                    return Bw[name]
            raise KeyError(c0)

        P.op("dve", lambda e: e.memset(THY.t[:, 0:128], 0.0), writes=[THY.b])
        P.op("pool", lambda e: e.affine_select(out=THY.t[:, 0:128], in_=THY.t[:, 0:128], pattern=[[-1, 128]],
                                               compare_op=ALU.not_equal, fill=1.0, base=0, channel_multiplier=1),
             reads=[THY.b], writes=[THY.b])
        P.op("dve", lambda e: e.tensor_copy(out=ID.t[:, :], in_=THY.t[:, 0:128]), reads=[THY.b], writes=[ID.b])
        P.op("dve", lambda e: e.memset(NEGH.t[:, :], -0.5), writes=[NEGH.b])
        P.op("dve", lambda e: e.memset(VA[:, :, :, :], 1.0), writes=Bva)
        P.op("dve", lambda e: e.memset(VB[:, :, :, :], 1.0), writes=Bvb)
        P.op("dve", lambda e: e.memset(VAM.t[:, :, :], 1.0), writes=[VAM.b])
        P.op("dve", lambda e: e.memset(VBM.t[:, :, :], 1.0), writes=[VBM.b])
        P.op("act", lambda e: e.activation(out=ES2.t[:, :], in_=ES2.t[:, :], func=AF.Exp), reads=[ES2.b], writes=[ES2.b])
        P.op("dve", lambda e: e.tensor_scalar(out=ES2.t[:, :], in0=ES2.t[:, :], scalar1=2.0, scalar2=None, op0=ALU.mult),
             reads=[ES2.b], writes=[ES2.b])

        def mmc(n_mm, ncols):
            return n_mm * max(95.0, 0.53 * ncols)

        def mm_group(out_ap, pairs, reads, wbufs, ncols=512):
            n = len(pairs)

            def fn(e):
                last = None
                for i, (l, r) in enumerate(pairs):
                    last = e.matmul(out=out_ap, lhsT=l, rhs=r, start=(i == 0), stop=(i == n - 1))
                return last
            P.op("pe", fn, reads=reads, writes=wbufs, cost=mmc(n, ncols))

        cp_toggle = [0]

        def copy(out_ap, in_ap, reads, writes, eng=None, dur=None):
            if eng is None:
                eng = ("act", "act", "dve")[cp_toggle[0] % 3]
                cp_toggle[0] += 1
            if eng == "act":
                P.op("act", lambda e: e.activation(out=out_ap, in_=in_ap, func=AF.Copy), reads=reads, writes=writes, dur=dur)
            else:
                P.op(eng, lambda e: e.tensor_copy(out=out_ap, in_=in_ap), reads=reads, writes=writes, dur=dur)

        def transposes(n_in, cols, src_t, src_col0, np_, bank):
            trv = bank.t.bitcast(BF16)

            def fn(e):
                last = None
                for i in range(n_in):
                    last = e.transpose(out=trv[:, i * 128:i * 128 + np_],
                                       in_=src_t.t[0:np_, src_col0 + i * 128:src_col0 + (i + 1) * 128],
                                       identity=ID.t[0:np_, 0:np_])
                return last
            P.op("pe", fn, reads=[src_t.b, ID.b], writes=[bank.b], cost=mmc(n_in, 128))
            return trv

        def norm_rows(j, np_, src_ap, gain_t, gain_b, out_t, junk_t):
            xt = XR[j]
            si, sbuf_ = Bss.next()
            ss = SS[0:np_, si, :]
            P.op("act", lambda e: e.activation(out=junk_t.t[0:np_, :], in_=xt.t[0:np_, :], func=AF.Square,
                                               accum_out=ss[:, 0:1]), reads=[xt.b], writes=[junk_t.b, sbuf_], dur=1100.0)
            P.op("dve", lambda e: e.tensor_scalar(out=ss[:, 1:2], in0=ss[:, 0:1], scalar1=1.0 / D, scalar2=EPS,
                                                  op0=ALU.mult, op1=ALU.add), reads=[sbuf_], writes=[sbuf_], dur=120.0)
            P.op("pool", lambda e: e.tensor_tensor(out=ss[:, 2:3], in0=ss[:, 1:2], in1=NEGH.t[0:np_, :], op=ALU.pow),
                 reads=[sbuf_, NEGH.b], writes=[sbuf_])
            P.op("dve", lambda e: e.scalar_tensor_tensor(out=out_t.t[0:np_, :], in0=xt.t[0:np_, :], scalar=ss[:, 2:3],
                                                         in1=gain_t[0:np_, :], op0=ALU.mult, op1=ALU.mult),
                 reads=[xt.b, sbuf_, gain_b], writes=[out_t.b], dur=1300.0)

        def rope(src, sdims, dst, ddims, np_, rslot, rts, dst_buf, src_buf):
            nh = 1
            for _, c in sdims:
                nh *= c
            zero = [(0, c) for _, c in sdims]
            tdims = []
            acc = 16
            for _, c in reversed(sdims):
                tdims.insert(0, (acc, c))
                acc *= c
            rp = ROPE.t[0:np_, rslot, :]
            t1, t2 = rts

            def sub(base, off):
                return bass.AP(base.tensor, base.offset + off, base.ap)
            P.op("dve", lambda e: e.tensor_tensor(out=V(t1.t[0:np_, 0, :], tdims + [(1, 16)]), in0=V(src, sdims + [(1, 16)]),
                                                  in1=V(rp, zero + [(1, 16)]), op=ALU.mult),
                 reads=[src_buf, ROPE.b], writes=[t1.b])
            P.op("dve", lambda e: e.tensor_tensor(out=V(t2.t[0:np_, 0, :], tdims + [(1, 8)]), in0=V(sub(src, 8), sdims + [(1, 8)]),
                                                  in1=V(sub(rp, 16), zero + [(1, 8)]), op=ALU.mult),
                 reads=[src_buf, ROPE.b], writes=[t2.b])
            P.op("dve", lambda e: e.tensor_tensor(out=V(sub(t2.t[0:np_, 0, :], 8), tdims + [(1, 8)]), in0=V(src, sdims + [(1, 8)]),
                                                  in1=V(sub(rp, 24), zero + [(1, 8)]), op=ALU.mult),
                 reads=[src_buf, ROPE.b], writes=[t2.b])
            P.op("dve", lambda e: e.tensor_tensor(out=V(dst, ddims + [(1, 16)]), in0=V(t1.t[0:np_, 0, :], tdims + [(1, 16)]),
                                                  in1=V(t2.t[0:np_, 0, :], tdims + [(1, 16)]), op=ALU.add),
                 reads=[t1.b, t2.b], writes=[dst_buf])
            P.op("act", lambda e: e.activation(out=V(sub(dst, 16), ddims + [(1, 48)]), in_=V(sub(src, 16), sdims + [(1, 48)]),
                                               func=AF.Copy), reads=[src_buf], writes=[dst_buf])

        def kv_block(np_, lhs, lhs_bufs, rslot, kat_dst, kat_buf, va_dst, va_buf, vb_dst, vb_buf):
            b1 = mm.next()
            mm_group(b1.t[0:np_, 0:256], [(lhs(kc), W[:, kc, C_KA:C_KA + 256]) for kc in range(8)],
                     lhs_bufs + [Bw["kva"]], [b1.b], ncols=256)
            rope(b1.t[0:np_, 0:64], [(64, 2)], QKk.t[0:np_, 0:64], [(64, 2)], np_, rslot, RTK, QKk.b, b1.b)
            copy(va_dst, V(b1.t[0:np_, 128:192], [(64, 2), (1, 64)]), [b1.b], [va_buf], eng="act")
            b2 = mm.next()
            mm_group(b2.t[0:np_, 0:512], [(lhs(kc), W[:, kc, C_VB:C_VB + 512]) for kc in range(8)],
                     lhs_bufs + [Bw["vb"]], [b2.b])
            copy(vb_dst, V(b2.t[0:np_, 0:64], [(64, 8), (1, 64)]), [b2.b], [vb_buf], eng="act")
            b3 = mm.next()
            trv = transposes(1, 128, QKk, 0, np_, b3)
            copy(kat_dst, trv[:, 0:np_], [b3.b], [kat_buf])

        def kb_feature(n_tok, rhs, rhs_bufs, dst, dst_bufs):
            for c in range(4):
                b = mm.next()
                mm_group(b.t[:, 0:n_tok], [(W[:, kc, C_KB + c * 128:C_KB + (c + 1) * 128], rhs(kc)) for kc in range(8)],
                         rhs_bufs + [Bw["kb"]], [b.b], ncols=n_tok)
                copy(dst(c), b.t[:, 0:n_tok], [b.b], dst_bufs)

        def do_meta():
            x0 = xr.next()
            j = XR.index(x0)
            s = P.new_dma_sem("ld_meta")
            P.dma("sp", s, lambda e: e.dma_start(out=x0.t[0:16, :], in_=meta_d), writes=[x0.b])
            norm_rows(j, 16, None, GREP, Bgrep, NBt, NBt)
            b0 = mm.next()
            trv = transposes(8, 128, NBt, 0, 16, b0)
            copy(NTM.t[:, :, :], V(trv[:, 0:16], [(128, 8), (1, 16)]), [b0.b], [NTM.b])
            kv_block(16, lambda kc: NTM.t[:, kc, 0:16], [NTM.b], 20,
                     KATM.t[:, 0:16], KATM.b, VAM.t[0:16, :, 0:64], VAM.b, VBM.t[0:16, :, 0:64], VBM.b)
            kb_feature(16, lambda kc: NTM.t[:, kc, 0:16], [NTM.b], lambda c: KBTM.t[:, c, 0:16], [KBTM.b])

        ld_sems = [P.new_dma_sem("ld%d" % i) for i in range(3)]
        st_sems = [P.new_dma_sem("st%d" % i) for i in range(3)]

        def front(t, part=None):
            s = t % 3

            def load_norm(bi):
                pb = 2 * t + bi
                x0 = xr.next()
                j = XR.index(x0)
                P.dma("sp", ld_sems[j], lambda e, x0=x0, pb=pb: e.dma_start(out=x0.t[:, :], in_=xp_d[pb * 128:(pb + 1) * 128, :]),
                      writes=[x0.b])
                norm_rows(j, 128, None, GREP, Bgrep, NBt, NBt)

            def tr_nt(bi):
                b0 = mm.next()
                trv = transposes(8, 128, NBt, 0, 128, b0)
                copy(NT[s][:, :, bi * 128:(bi + 1) * 128], V(trv[:, 0:128], [(128, 8), (1, 128)]), [b0.b], [Bnt[s][bi]], dur=950.0)

            def kv(bi):
                pb = 2 * t + bi
                slot = pb % 8
                kv_block(128, lambda kc, bi=bi: NT[s][:, kc, bi * 128:(bi + 1) * 128], [Bnt[s][bi]], pb,
                         KAT[:, slot * 128:(slot + 1) * 128], Bkat[slot],
                         VA[:, slot, :, 0:64], Bva[slot], VB[:, slot, :, 0:64], Bvb[slot])
            if part in (None, "pre"):
                load_norm(0)
            if part == "pre":
                return
            tr_nt(0)
            load_norm(1)
            kv(0)
            tr_nt(1)
            kv(1)
            slot0 = (2 * t) % 8
            kb_feature(256, lambda kc: NT[s][:, kc, 0:256], [Bnt[s][0], Bnt[s][1]],
                       lambda c: KBT[:, c, slot0 * 128:slot0 * 128 + 256], [Bkbt[slot0], Bkbt[slot0 + 1]])

        def normalize(ob, unit_is_a, k_or_u):
            den = DEN.next()
            o_den = V(ob.t[:, 64:65], [(65, 4)])
            if unit_is_a:
                k = k_or_u
                P.op("dve", lambda e: e.scalar_tensor_tensor(out=den.t[:, 0:4], in0=o_den, scalar=2.0,
                                                             in1=ES2.t[:, 4 * k:4 * k + 4], op0=ALU.mult, op1=ALU.add),
                     reads=[ob.b, ES2.b], writes=[den.b])
            else:
                P.op("dve", lambda e: e.tensor_scalar(out=den.t[:, 0:4], in0=o_den, scalar1=2.0, scalar2=None, op0=ALU.mult),
                     reads=[ob.b], writes=[den.b])
            P.op("dve", lambda e: e.reciprocal(out=den.t[:, 4:8], in_=den.t[:, 0:4]), reads=[den.b], writes=[den.b], dur=200.0)
            for g in range(4):
                col = (k_or_u * 256 + g * 64) if unit_is_a else (512 + (2 * g + k_or_u) * 64)
                P.op("dve", lambda e, g=g, col=col: e.scalar_tensor_tensor(
                    out=OAG.t[:, col:col + 64], in0=ob.t[:, g * 65:g * 65 + 64], scalar=den.t[:, 4 + g:5 + g],
                    in1=ZS.t[:, col:col + 64], op0=ALU.mult, op1=ALU.mult), reads=[ob.b, den.b, ZS.b], writes=[OAG.b], dur=360.0)

        def pv_op(ob, pt, nk, rhs_fn, first, last, reads):
            def fn(e):
                r = None
                for g in range(4):
                    r = e.matmul(out=ob.t[:, g * 65:(g + 1) * 65], lhsT=pt.t[0:nk, g * 128:(g + 1) * 128], rhs=rhs_fn(g),
                                 start=(first and g == 0), stop=(last and g == 3), skip_group_check=True)
                return r
            P.op("pe", fn, reads=[pt.b] + reads, writes=[ob.b], cost=220.0)

        def piece_idx(m, jj):
            if m == 0:
                return 5 + jj
            if m == 1:
                return 11 + jj
            if m == 14:
                return 16 + jj
            if m == 15:
                return 21 + jj
            return jj

        tb_sems = [P.new_dma_sem("tb%d" % i) for i in range(3)]

        def attn_block(pb, bi, extra=None):
            lb = pb - 2
            tasks = []
            obsA = [o_ring.next(), o_ring.next()]
            chunksA = [("M", 16, None), ("L", 128, pb - 1), ("C", 128, pb), ("R", 128, pb + 1)]
            for ci, (typ, nk, kb_) in enumerate(chunksA):
                def S(state, typ=typ, nk=nk, kb_=kb_):
                    sts = [st_ring.next(), st_ring.next()]
                    if typ == "M":
                        kbuf = KATM.b
                        lhs_fn = lambda k: KATM.t[64 * k:64 * k + 64, 0:16]
                    else:
                        slot = kb_ % 8
                        kbuf = Bkat[slot]
                        lhs_fn = lambda k, slot=slot: KAT[64 * k:64 * k + 64, slot * 128:(slot + 1) * 128]

                    def sfn(e):
                        r = None
                        for k in range(2):
                            r = e.matmul(out=sts[k].t[0:nk, 0:512], lhsT=lhs_fn(k),
                                         rhs=V(QTA[64 * k:64 * k + 64, 0, bi * 128:(bi + 1) * 128], [(256, 4), (1, 128)]),
                                         start=True, stop=True)
                        return r
                    P.op("pe", sfn, reads=[kbuf, Bqta[bi]], writes=[sts[0].b, sts[1].b], cost=410.0)
                    state["pts"] = []
                    for k in range(2):
                        st = sts[k]
                        pt = PT.next()
                        P.op("act", lambda e, st=st, pt=pt: e.activation(out=pt.t[0:nk, :], in_=st.t[0:nk, 0:512], func=AF.Exp, scale=0.125),
                             reads=[st.b], writes=[pt.b], dur=560.0)
                        if typ in ("L", "R"):
                            mt_ = (2 if lb == 0 else 0) if typ == "L" else (3 if lb == 15 else 1)
                            P.op("dve", lambda e, pt=pt, mt_=mt_: e.tensor_tensor(out=V(pt.t[:, 0:128], [(128, 4), (1, 128)]),
                                                                                  in0=V(pt.t[:, 0:128], [(128, 4), (1, 128)]),
                                                                                  in1=V(AM.t[:, mt_, :], [(0, 4), (1, 128)]), op=ALU.mult),
                                 reads=[pt.b, AM.b], writes=[pt.b])
                        state["pts"].append(pt)

                def PV(state, ci=ci, typ=typ, nk=nk, kb_=kb_):
                    for k in range(2):
                        if typ == "M":
                            vb_, rhs_v = VAM.b, (lambda g, k=k: VAM.t[0:16, k, :])
                        else:
                            slot = kb_ % 8
                            vb_, rhs_v = Bva[slot], (lambda g, k=k, slot=slot: VA[:, slot, k, :])
                        pv_op(obsA[k], state["pts"][k], nk, rhs_v, ci == 0, ci == 3, [vb_])
                    if ci == 3:
                        for k in range(2):
                            normalize(obsA[k], True, k)
                tasks.append((S, PV, {}))
            m = lb
            js = list(range(0, 6)) if m == 0 else (list(range(-1, 5)) if m == 15 else list(range(0, 5)))
            obsB = [o_ring.next(), o_ring.next()]
            chunksB = [("M", 16, None, None)] + [("W", 128, pb + j - 2, jj) for jj, j in enumerate(js)]
            nB = len(chunksB)
            for ci, (typ, nk, kb_, jj) in enumerate(chunksB):
                def S(state, typ=typ, nk=nk, kb_=kb_, jj=jj):
                    tb = None
                    if typ == "W":
                        tb = TBE.next()
                        ti = TBE.items.index(tb)
                        pi = piece_idx(m, jj)
                        P.dma("pool", tb_sems[ti], lambda e, tb=tb, pi=pi: e.dma_start(out=tb.t[:, :], in_=btab_d[pi * 128:(pi + 1) * 128, :]),
                              writes=[tb.b])
                    sts = [st_ring.next(), st_ring.next()]
                    if typ == "M":
                        kbuf = KBTM.b
                        lhs_fn = lambda c, u: KBTM.t[64 * u:64 * u + 64, c, 0:16]
                    else:
                        slot = kb_ % 8
                        kbuf = Bkbt[slot]
                        lhs_fn = lambda c, u, slot=slot: KBT[64 * u:64 * u + 64, c, slot * 128:(slot + 1) * 128]

                    def sfn(e):
                        r = None
                        for c in range(4):
                            for u in range(2):
                                r = e.matmul(out=sts[u].t[0:nk, c * 128:(c + 1) * 128], lhsT=lhs_fn(c, u),
                                             rhs=QTB.t[64 * u:64 * u + 64, c, bi * 128:(bi + 1) * 128], start=True, stop=True)
                        return r
                    P.op("pe", sfn, reads=[kbuf, QTB.b], writes=[sts[0].b, sts[1].b], cost=460.0)
                    state["pts"] = []
                    for u in range(2):
                        st = sts[u]
                        pt = PT.next()
                        if typ == "W":
                            P.op("dve", lambda e, st=st, tb=tb, u=u: e.scalar_tensor_tensor(
                                out=st.t[:, 0:512], in0=st.t[:, 0:512], scalar=0.125,
                                in1=V(tb.t[:, u * 128:(u + 1) * 128], [(256, 4), (1, 128)]), op0=ALU.mult, op1=ALU.add),
                                reads=[st.b, tb.b], writes=[st.b], dur=520.0)
                            P.op("act", lambda e, st=st, pt=pt: e.activation(out=pt.t[:, :], in_=st.t[:, 0:512], func=AF.Exp),
                                 reads=[st.b], writes=[pt.b], dur=620.0)
                        else:
                            P.op("act", lambda e, st=st, pt=pt: e.activation(out=pt.t[0:16, :], in_=st.t[0:16, 0:512], func=AF.Exp, scale=0.125),
                                 reads=[st.b], writes=[pt.b])
                        state["pts"].append(pt)

                def PV(state, ci=ci, typ=typ, nk=nk, kb_=kb_):
                    for u in range(2):
                        if typ == "M":
                            vb_, rhs_v = VBM.b, (lambda c, u=u: VBM.t[0:16, 2 * c + u, :])
                        else:
                            slot = kb_ % 8
                            vb_, rhs_v = Bvb[slot], (lambda c, u=u, slot=slot: VB[:, slot, 2 * c + u, :])
                        pv_op(obsB[u], state["pts"][u], nk, rhs_v, ci == 0, ci == nB - 1, [vb_])
                    if ci == nB - 1:
                        for u in range(2):
                            normalize(obsB[u], False, u)
                tasks.append((S, PV, {}))
            prev = None
            for ti, (S, PV, state) in enumerate(tasks):
                S(state)
                if prev is not None:
                    prev[0](prev[1])
                prev = (PV, state)
                if extra is not None and ti == extra[0]:
                    extra[1]()
            prev[0](prev[1])

        oagt_ver = [0, 0]
        dc_done = [0]

        def qproj(t):
            s = t % 3

            def qa(bi):
                pb = 2 * t + bi
                b = st_ring.next()
                mm_group(b.t[:, 0:512], [(NT[s][:, kc, bi * 128:(bi + 1) * 128], W[:, kc, C_QA:C_QA + 512]) for kc in range(8)],
                         [Bnt[s][bi], Bw["qa"]], [b.b])
                rope(b.t[:, 0:64], [(256, 2), (64, 4)], QKq2[bi].t[:, 0:64], [(64, 2), (128, 4)], 128, pb, RT, QKq2[bi].b, b.b)

            def qtr(bi):
                b2 = st_ring.next()
                trv = transposes(4, 128, QKq2[bi], 0, 128, b2)
                copy(QTA[:, :, bi * 128:(bi + 1) * 128], V(trv[:, 0:128], [(128, 4), (1, 128)]), [b2.b], [Bqta[bi]])

            def qb(c):
                b = st_ring.next()
                mm_group(b.t[:, 0:256], [(W[:, kc, C_QB + c * 128:C_QB + (c + 1) * 128], NT[s][:, kc, 0:256]) for kc in range(8)],
                         [Bnt[s][0], Bnt[s][1], Bw["qb"]], [b.b], ncols=256)
                copy(QTB.t[:, c, :], b.t[:, 0:256], [b.b], [QTB.b])
            qa(0)
            qa(1)
            qb(0)
            qtr(0)
            qb(1)
            qtr(1)
            qb(2)
            qb(3)

        zstate = {}

        def attn(t, part, sec=None):
            s = t % 3
            zbanks = zstate.setdefault(t, [])
            if part == 1 and sec in (None, "a"):
                for br, c0, wn in ((0, C_ZA, "za"), (1, C_ZB, "zb")):
                    b = st_ring.next()
                    mm_group(b.t[:, 0:512], [(NT[s][:, kc, 128:256], W[:, kc, c0:c0 + 512]) for kc in range(8)],
                             [Bnt[s][1], Bw[wn]], [b.b])
                    zbanks.append(b)

            def oag_b0():
                def chk0():
                    assert dc_done[0] >= t - 2, ("OAGT blk0 overwritten before back_dc", t, dc_done[0])
                P.check(chk0)
                b2 = st_ring.next()
                trv = transposes(8, 128, OAG, 0, 128, b2)
                copy(OAGT2[t % 2][:, :, 0:128], V(trv[:, 0:128], [(128, 8), (1, 128)]), [b2.b], [Boagt2[t % 2][0]], dur=950.0)

                def set0():
                    oagt_ver[0] = t
                P.check(set0)
            if part == 1 and sec in (None, "a") and t == 8:
                oag_b0()
            for bi in ((0,) if part == 0 else (1,)):
                pb = 2 * t + bi
                for br, c0, wn in ((0, C_ZA, "za"), (1, C_ZB, "zb")):
                    if part == 1 and sec == "b":
                        continue
                    if part == 1:
                        b = zbanks[br]
                        th = THX
                        P.op("act", lambda e, b=b, th=th: e.activation(out=th.t[:, :], in_=b.t[:, 0:512], func=AF.Tanh, scale=0.5),
                             reads=[b.b], writes=[th.b])
                        P.op("dve", lambda e, b=b, th=th, br=br: e.scalar_tensor_tensor(
                            out=ZS.t[:, br * 512:(br + 1) * 512], in0=th.t[:, :], scalar=1.0, in1=b.t[:, 0:512],
                            op0=ALU.add, op1=ALU.mult), reads=[th.b, b.b], writes=[ZS.b])
                        continue
                    b = st_ring.next()
                    mm_group(b.t[:, 0:512], [(NT[s][:, kc, bi * 128:(bi + 1) * 128], W[:, kc, c0:c0 + 512]) for kc in range(8)],
                             [Bnt[s][bi], Bw[wn]], [b.b])
                    th = THX
                    P.op("act", lambda e, b=b, th=th: e.activation(out=th.t[:, :], in_=b.t[:, 0:512], func=AF.Tanh, scale=0.5),
                         reads=[b.b], writes=[th.b])
                    P.op("dve", lambda e, b=b, th=th, br=br: e.scalar_tensor_tensor(
                        out=ZS.t[:, br * 512:(br + 1) * 512], in0=th.t[:, :], scalar=1.0, in1=b.t[:, 0:512],
                        op0=ALU.add, op1=ALU.mult), reads=[th.b, b.b], writes=[ZS.b])
                if part == 1 and sec == "a":
                    continue
                attn_block(pb, bi, extra=((1, oag_b0) if (bi == 1 and t < 8) else None))
                if bi == 1:
                    if t < 8:
                        qproj(t + 1)
                    def chk1():
                        assert dc_done[0] >= t - 2, ("OAGT blk1 overwritten before back_dc", t, dc_done[0])
                    P.check(chk1)
                    b2 = st_ring.next()
                    trv = transposes(8, 128, OAG, 0, 128, b2)
                    copy(OAGT2[t % 2][:, :, 128:256], V(trv[:, 0:128], [(128, 8), (1, 128)]), [b2.b], [Boagt2[t % 2][1]], dur=950.0)

                    def set1():
                        oagt_ver[1] = t
                    P.check(set1)

        def back_dc(t, blks=(0, 1)):
            s = t % 3
            lo, n = blks[0] * 128, 128 * len(blks)

            def chk():
                for bb in blks:
                    assert oagt_ver[bb] == t, ("back_dc reads stale OAGT", t, bb, oagt_ver)
            P.check(chk)
            for dc in range(8):
                bA, bB = mm.next(), mm.next()
                th = THY
                for bnk, wp, bwp, koff, cg, half in ((bA, WPA, Bwpa, 0, C_GA, 0), (bB, WPB, Bwpb, 4, C_GB, 1)):
                    mm_group(bnk.t[:, 0:n], [(wp[:, kc, dc * 128:(dc + 1) * 128], OAGT2[t % 2][:, koff + kc, lo:lo + n]) for kc in range(4)],
                             [bwp] + [Boagt2[t % 2][bb] for bb in blks], [bnk.b], ncols=n)
                    gc = cg + dc * 128
                    mm_group(bnk.t[:, 256:256 + n], [(W[:, kc, gc:gc + 128], NT[s][:, kc, lo:lo + n]) for kc in range(8)],
                             [wbuf(gc)] + [Bnt[s][bb] for bb in blks], [bnk.b], ncols=n)
                    P.op("act", lambda e, bnk=bnk, th=th, half=half: e.activation(out=th.t[:, half * 256:half * 256 + n], in_=bnk.t[:, 256:256 + n],
                                                                                 func=AF.Tanh, scale=0.5), reads=[bnk.b], writes=[th.b])
                    P.op("dve", lambda e, bnk=bnk, th=th, half=half: e.scalar_tensor_tensor(
                        out=th.t[:, half * 256:half * 256 + n], in0=th.t[:, half * 256:half * 256 + n], scalar=1.0,
                        in1=bnk.t[:, 0:n], op0=ALU.add, op1=ALU.mult), reads=[th.b, bnk.b], writes=[th.b])
                P.op("dve", lambda e, th=th, dc=dc: e.tensor_tensor(out=MT[:, dc, lo:lo + n], in0=th.t[:, 0:n], in1=th.t[:, 256:256 + n], op=ALU.add),
                     reads=[th.b], writes=[Bmt[dc]])

            def done():
                if len(blks) == 2 or blks[0] == 1:
                    dc_done[0] = t
            P.check(done)

        def back_out(t, blks=(0, 1)):
            xs = {}
            for bi in blks:
                pb = 2 * t + bi
                x0 = xr.next()
                j = XR.index(x0)
                P.dma("sp", ld_sems[j], lambda e, x0=x0, pb=pb: e.dma_start(out=x0.t[:, :], in_=xp_d[pb * 128:(pb + 1) * 128, :]),
                      writes=[x0.b])
                xs[bi] = (x0, j)
            for bi in blks:
                pb = 2 * t + bi
                lb = pb - 2
                x0, j = xs[bi]
                for half in range(2):
                    b = mm.next()
                    mm_group(b.t[:, 0:512], [(MT[:, kc, bi * 128:(bi + 1) * 128], WO[:, kc, half * 512:(half + 1) * 512]) for kc in range(8)],
                             Bmt + [Bwo[half]], [b.b])
                    P.op("dve", lambda e, b=b, x0=x0, half=half: e.scalar_tensor_tensor(
                        out=x0.t[:, half * 512:(half + 1) * 512], in0=b.t[:, 0:512], scalar=0.5,
                        in1=x0.t[:, half * 512:(half + 1) * 512], op0=ALU.mult, op1=ALU.add), reads=[b.b, x0.b], writes=[x0.b], dur=560.0)
                norm_rows(j, 128, None, FGREP, Bfgrep, x0, NBt)
                P.dma("sp", st_sems[j], lambda e, x0=x0, lb=lb: e.dma_start(out=y_d[lb * 128:(lb + 1) * 128, :], in_=x0.t[:, :]),
                      reads=[x0.b])

        import os
        if os.environ.get('KDEBUG'):
            print('SBUF base/top', nc.sbuf_base, nc.sbuf_top, 'free', nc.sbuf_top - nc.sbuf_base)
        do_meta()
        wdma(["qa", "qb"])
        front(0)
        wdma(["za", "zb", "ga0", "ga1"])
        front(1)
        wdma(["gb0", "gb1", "w_pa", "w_pb", "w_o0", "w_o1"])
        front(2)
        qproj(1)
        for t in range(1, 9):
            if t < 8:
                def xthread():
                    attn(t, 0)
                    attn(t, 1)

                def ythread():
                    if t + 2 <= 9:
                        front(t + 2, "pre")
                    if t > 1:
                        back_dc(t - 1)
                    if t + 2 <= 9:
                        front(t + 2, "rest")
                    if t > 1:
                        back_out(t - 1)
                P.schedule([P.record(xthread), P.record(ythread)])
            else:
                def y8a():
                    back_dc(7)
                    back_out(7)
                def x8a():
                    attn(8, 0)
                    attn(8, 1, "a")
                P.schedule([P.record(x8a), P.record(y8a)])
                X1b = P.record(lambda: attn(8, 1, "b"))

                def y8b():
                    back_dc(8, (0,))
                    back_out(8, (0,))
                P.schedule([X1b, P.record(y8b)])
        back_dc(8, (1,))
        back_out(8, (1,))
        final = [(s, P.seq[s]) for s in st_sems if P.seq[s] > 0]
        P.emit(final)
    return nc


def _rope_table(c):
    half = 8
    inv_freq = (np.float32(500000.0) ** (-np.arange(half, dtype=np.float32) / np.float32(half))).astype(np.float32)
    tab = np.zeros((128, 21, 32), np.float32)
    p = np.arange(128)
    for pb in range(NPB):
        tok = c * TOK - HALO + pb * 128 + p
        pos = (tok + NMETA).astype(np.float32)
        ang = (pos[:, None] * inv_freq[None, :]).astype(np.float32)
        cs, sn = np.cos(ang).astype(np.float32), np.sin(ang).astype(np.float32)
        tab[:, pb, 0:8] = cs
        tab[:, pb, 8:16] = cs
        tab[:, pb, 16:24] = -sn
        tab[:, pb, 24:32] = sn
    pos = np.arange(128).astype(np.float32)
    ang = (pos[:, None] * inv_freq[None, :]).astype(np.float32)
    cs, sn = np.cos(ang).astype(np.float32), np.sin(ang).astype(np.float32)
    tab[:, 20, 0:8] = cs
    tab[:, 20, 8:16] = cs
    tab[:, 20, 16:24] = -sn
    tab[:, 20, 24:32] = sn
    return tab


def _amask(c):
    j = np.arange(128)[:, None]
    i = np.arange(128)[None, :]
    L = (j >= i).astype(np.float32)
    R = (j <= i).astype(np.float32)
    am = np.zeros((128, 4, 128), np.float32)
    am[:, 0] = L
    am[:, 1] = R
    am[:, 2] = L if c > 0 else 0.0
    am[:, 3] = R if c < 3 else 0.0
    return am


def _btab_piece(rpb, c, m, j):
    R0 = 32 * c + 2 * m
    KR0 = R0 + 2 * (j - 2)
    kp = np.arange(128)
    qp = np.arange(128)
    kR = KR0 + kp // 64
    kc = kp % 64
    r = R0 + qp // 64
    qc = qp % 64
    r_start = np.clip(r - 4, 0, 128 - 8)
    cstart = np.clip(qc - 8, 0, 64 - 16)
    ok_r = (kR[:, None] >= r_start[None, :]) & (kR[:, None] < r_start[None, :] + 8) & (kR[:, None] >= 0) & (kR[:, None] < 128)
    ok_c = (kc[:, None] >= cstart[None, :]) & (kc[:, None] < cstart[None, :] + 16)
    ok = ok_r & ok_c
    dr = np.clip(kR[:, None] - r[None, :] + 7, 0, 14)
    dc = np.clip(kc[:, None] - qc[None, :] + 15, 0, 30)
    out = np.full((128, 8, 128), NEG, np.float32)
    for h in range(8):
        g = rpb[h][dr, dc]
        out[:, h, :] = np.where(ok, g, np.float32(NEG))
    return out


def _btab(rpb, c):
    pieces = []
    for j in range(5):
        pieces.append(_btab_piece(rpb, 1, 8, j))
    for j in range(6):
        pieces.append(_btab_piece(rpb, c, 0, j))
    for j in range(5):
        pieces.append(_btab_piece(rpb, c, 1, j))
    for j in range(5):
        pieces.append(_btab_piece(rpb, c, 14, j))
    for j in range(-1, 5):
        pieces.append(_btab_piece(rpb, c, 15, j))
    return np.stack(pieces).reshape(27 * 128, 1024)


_NC_CACHE = {}


def kernel(x, meta_tokens, norm_gain, w_in, sink_logits, rel_pos_bias, w_proj_a, w_proj_b, w_out, final_norm_gain):
    f = np.float32
    x = np.asarray(x, f)
    w_in_l = np.ascontiguousarray(np.asarray(w_in, f)[0].reshape(8, 128, NCOL).transpose(1, 0, 2))
    w_pa_l = np.ascontiguousarray(np.asarray(w_proj_a, f)[0].reshape(4, 128, D).transpose(1, 0, 2))
    w_pb_l = np.ascontiguousarray(np.asarray(w_proj_b, f)[0].reshape(4, 128, D).transpose(1, 0, 2))
    w_out_l = np.ascontiguousarray(np.asarray(w_out, f)[0].reshape(8, 128, D).transpose(1, 0, 2))
    gain = np.ascontiguousarray(np.asarray(norm_gain, f).reshape(1, D))
    fgain = np.ascontiguousarray(np.asarray(final_norm_gain, f).reshape(1, D))
    sink = np.ascontiguousarray(np.asarray(sink_logits, f).reshape(1, 8))
    meta = np.ascontiguousarray(np.asarray(meta_tokens, f))
    rpb = np.asarray(rel_pos_bias, f)[0]
    ropes = [_rope_table(c) for c in range(4)]
    amasks = [_amask(c) for c in range(4)]
    btabs = [_btab(rpb, c) for c in range(4)]
    in_maps = []
    for ci in range(8):
        b, c = divmod(ci, 4)
        xp = np.zeros((NPB * 128, D), f)
        lo, hi = c * TOK - HALO, c * TOK + TOK + HALO
        slo, shi = max(lo, 0), min(hi, SEQ)
        xp[slo - lo:shi - lo] = x[b, slo:shi]
        in_maps.append({"xp": xp, "meta": meta, "w_in": w_in_l, "w_pa": w_pa_l, "w_pb": w_pb_l, "w_out": w_out_l,
                        "gain": gain, "fgain": fgain, "sink": sink, "rope": ropes[c], "amask": amasks[c], "btab": btabs[c]})
    if "nc" not in _NC_CACHE:
        _NC_CACHE["nc"] = build_nc()
    res = run_bass_kernel_spmd(_NC_CACHE["nc"], in_maps, core_ids=list(range(8)))
    out = np.zeros((2, SEQ, D), f)
    for ci in range(8):
        b, c = divmod(ci, 4)
        out[b, c * TOK:(c + 1) * TOK] = res.results[ci]["y"]
    return out
```

```python
import numpy as np
from contextlib import ExitStack
import concourse.bass as bass
import concourse.mybir as mybir
from concourse.bass_utils import run_bass_kernel_spmd

F32 = mybir.dt.float32
BF16 = mybir.dt.bfloat16
AF = mybir.ActivationFunctionType
ALU = mybir.AluOpType

D = 1024
NCOL = 5376
SEQ = 8192
NMETA = 16
TOK = 2048
HALO = 256
NPB = 20
EPS = 1e-6
NEG = -30000.0
C_QA, C_KA, C_VA, C_ZA, C_QB, C_KB, C_VB, C_ZB, C_GA, C_GB = 0, 512, 640, 768, 1280, 1792, 2304, 2816, 3328, 4352
W_PIECES = [("kva", 512, 768), ("vb", 2304, 2816), ("kb", 1792, 2304), ("qa", 0, 512), ("qb", 1280, 1792),
            ("za", 768, 1280), ("zb", 2816, 3328), ("ga0", 3328, 3840), ("ga1", 3840, 4352),
            ("gb0", 4352, 4864), ("gb1", 4864, 5376)]


class Buf:
    __slots__ = ("name", "lw", "rd")

    def __init__(self, name):
        self.name = name
        self.lw = None
        self.rd = []


class Prog:
    ENGS = ("pe", "act", "dve", "pool", "sp")

    def __init__(self, nc, same_raw=True):
        self.nc = nc
        self.stream = {e: [] for e in self.ENGS}
        self.seq = {e: 0 for e in self.ENGS}
        self.known = {e: {} for e in self.ENGS}
        self.same_raw = same_raw
        self.dma_sems = []
        self.rec = None
        self.efree = {}
        self.done = {}

    def _deps(self, eng, reads, writes):
        deps = {}

        def add(d, raw):
            key, val = d
            if key == eng and (eng == "pe" or not self.same_raw):
                return
            if deps.get(key, 0) < val:
                deps[key] = val

        for b in reads:
            if b.lw is not None:
                add(b.lw, True)
        for b in writes:
            if b.lw is not None:
                add(b.lw, False)
            for r in b.rd:
                add(r, False)
        out = []
        kn = self.known[eng]
        for key, val in deps.items():
            if kn.get(key, 0) < val:
                kn[key] = val
                out.append((key, val))
        return out

    @staticmethod
    def _mark(me, reads, writes):
        for b in reads:
            b.rd.append(me)
        for b in writes:
            b.lw = me
            b.rd = []

    DUR = {"act": 450.0, "dve": 350.0, "pool": 500.0, "sp": 60.0, "pe": 100.0}

    def op(self, eng, fn, reads=(), writes=(), cost=0.0, dur=None):
        if dur is None:
            dur = cost if eng == "pe" else self.DUR[eng]
        item = ("op", eng, fn, tuple(reads), tuple(writes), cost, dur, None)
        if self.rec is not None:
            self.rec.append(item)
            return
        self._play_item(item)

    def dma(self, eng, sem, fn, reads=(), writes=()):
        item = ("dma", eng, fn, tuple(reads), tuple(writes), 0.0, 60.0, sem)
        if self.rec is not None:
            self.rec.append(item)
            return
        self._play_item(item)

    def check(self, f):
        item = ("chk", None, f, (), (), 0.0, 0.0, None)
        if self.rec is not None:
            self.rec.append(item)
        else:
            f()

    def record(self, f):
        self.rec = []
        f()
        r, self.rec = self.rec, None
        return r

    def _ready(self, eng, reads, writes):
        t = 0.0
        for b in reads:
            if b.lw is not None:
                t = max(t, self.done.get(b.lw, 0.0))
        for b in writes:
            if b.lw is not None and b.lw[0] != eng:
                t = max(t, self.done.get(b.lw, 0.0))
            for r in b.rd:
                if r[0] != eng:
                    t = max(t, self.done.get(r, 0.0))
        return t

    def _play_item(self, item):
        kind, eng, fn, reads, writes, cost, dur, sem = item
        if kind == "chk":
            fn()
            return
        start = max(self.efree.get(eng, 0.0), self._ready(eng, reads, writes) + 120.0)
        if kind == "op":
            self._op(eng, fn, reads, writes)
            me = (eng, self.seq[eng])
            if eng == "pe":
                self.efree[eng] = start + dur
                self.done[me] = start + dur + 180.0
            else:
                self.efree[eng] = start + dur
                self.done[me] = start + dur
        else:
            self._dma(eng, sem, fn, reads, writes)
            me = (sem, self.seq[sem])
            self.efree[eng] = start + dur
            self.done[me] = start + 3500.0

    def play(self, items):
        for it in items:
            self._play_item(it)

    def schedule(self, threads):
        def bundles(L):
            out = []
            for it in L:
                if (it[0] == "op" and it[1] == "pe") or not out:
                    out.append([it])
                else:
                    out[-1].append(it)
            return out
        bl = [bundles(L) for L in threads]
        pos = [0] * len(bl)
        rem = [sum(it[5] for bd in b for it in bd) for b in bl]
        while True:
            best = None
            for i, b in enumerate(bl):
                if pos[i] >= len(b):
                    continue
                first = next((it for it in b[pos[i]] if it[0] != "chk"), None)
                if first is None:
                    est = 0.0
                else:
                    est = max(self.efree.get(first[1], 0.0), self._ready(first[1], first[3], first[4]) + 120.0)
                key = (est - 0.02 * rem[i], i)
                if best is None or key < best[0]:
                    best = (key, i)
            if best is None:
                break
            i = best[1]
            bd = bl[i][pos[i]]
            pos[i] += 1
            rem[i] -= sum(it[5] for it in bd)
            for it in bd:
                self._play_item(it)

    def _op(self, eng, fn, reads, writes):
        waits = self._deps(eng, reads, writes)
        self.seq[eng] += 1
        self._mark((eng, self.seq[eng]), reads, writes)
        self.stream[eng].append((waits, fn, eng, 1))

    def new_dma_sem(self, name):
        self.dma_sems.append(name)
        self.seq[name] = 0
        return name

    def _dma(self, eng, sem, fn, reads, writes):
        waits = self._deps(eng, reads, writes)
        self.seq[sem] += 16
        self._mark((sem, self.seq[sem]), reads, writes)
        self.stream[eng].append((waits, fn, sem, 16))

    def emit(self, final_waits):
        nc = self.nc
        with ExitStack() as es:
            H = {}
            for k in list(self.ENGS) + self.dma_sems:
                H[k] = es.enter_context(nc.semaphore("s_" + k))
            blk = es.enter_context(nc.Block())

            def run(e, items):
                for waits, fn, key, inc in items:
                    for wk, wv in waits:
                        e.wait_ge(H[wk], wv)
                    if fn is not None:
                        fn(e).then_inc(H[key], inc)

            self.stream["sp"].append((final_waits, None, None, 0))
            blk.tensor(lambda e: run(e, self.stream["pe"]))
            blk.scalar(lambda e: run(e, self.stream["act"]))
            blk.vector(lambda e: run(e, self.stream["dve"]))
            blk.gpsimd(lambda e: run(e, self.stream["pool"]))
            blk.sync(lambda e: run(e, self.stream["sp"]))


def V(base, dims):
    return bass.AP(base.tensor, base.offset, [list(base.ap[0])] + [list(d) for d in dims])


class Ring:
    def __init__(self, items):
        self.items = items
        self.i = 0

    def next(self):
        it = self.items[self.i % len(self.items)]
        self.i += 1
        return it


class T:
    __slots__ = ("t", "b")

    def __init__(self, t, name):
        self.t = t
        self.b = Buf(name)


def build_nc():
    nc = bass.Bass("TRN2", target_bir_lowering=False, dynamic_dma_scratch_size=12288)

    def din(name, shape):
        return nc.dram_tensor(name, shape, F32, kind="ExternalInput").ap()

    xp_d = din("xp", [NPB * 128, D])
    meta_d = din("meta", [NMETA, D])
    win_d = din("w_in", [128, 8, NCOL])
    wpa_d = din("w_pa", [128, 4, D])
    wpb_d = din("w_pb", [128, 4, D])
    wo_d = din("w_out", [128, 8, D])
    gain_d = din("gain", [1, D])
    fgain_d = din("fgain", [1, D])
    sink_d = din("sink", [1, 8])
    rope_d = din("rope", [128, 21, 32])
    amask_d = din("amask", [128, 4, 128])
    btab_d = din("btab", [27 * 128, 1024])
    y_d = nc.dram_tensor("y", [TOK, D], F32, kind="ExternalOutput").ap()

    with ExitStack() as es:
        def sb(name, shape, dt=BF16):
            return es.enter_context(nc.sbuf_tensor(name, shape, dt))

        def ps(name, shape, dt=F32):
            return es.enter_context(nc.psum_tensor(name, shape, dt))

        P = Prog(nc)
        W = sb("W", [128, 8, NCOL])
        WPA = sb("WPA", [128, 4, D])
        WPB = sb("WPB", [128, 4, D])
        WO = sb("WO", [128, 8, D])
        GREP = sb("GREP", [128, D], F32)
        FGREP = sb("FGREP", [128, D], F32)
        XR = [T(sb("XR%d" % i, [128, D], F32), "XR%d" % i) for i in range(3)]
        NBt = T(sb("NB", [128, D]), "NB")
        NT = [sb("NT%d" % i, [128, 8, 256]) for i in range(3)]
        Bnt = [[Buf("nt%d_%d" % (i, j)) for j in range(2)] for i in range(3)]
        NTM = T(sb("NTM", [128, 8, 16]), "NTM")
        QKq2 = [T(sb("QKq%d" % i, [128, 512]), "QKq%d" % i) for i in range(2)]
        QKk = T(sb("QKk", [128, 128]), "QKk")
        QTA = sb("QTA", [128, 4, 256])
        Bqta = [Buf("qta0"), Buf("qta1")]
        QTB = T(sb("QTB", [128, 4, 256]), "QTB")
        KAT = sb("KAT", [128, 8 * 128])
        VA = sb("VA", [128, 8, 2, 65])
        KBT = sb("KBT", [128, 4, 8 * 128])
        VB = sb("VB", [128, 8, 8, 65])
        Bkat = [Buf("kat%d" % i) for i in range(8)]
        Bva = [Buf("va%d" % i) for i in range(8)]
        Bkbt = [Buf("kbt%d" % i) for i in range(8)]
        Bvb = [Buf("vb%d" % i) for i in range(8)]
        KATM = T(sb("KATM", [128, 16]), "KATM")
        KBTM = T(sb("KBTM", [128, 4, 16]), "KBTM")
        VAM = T(sb("VAM", [128, 2, 65]), "VAM")
        VBM = T(sb("VBM", [128, 8, 65]), "VBM")
        PT = Ring([T(sb("PT%d" % i, [128, 512]), "PT%d" % i) for i in range(4)])
        TBE = Ring([T(sb("TBE%d" % i, [128, 1024]), "TBE%d" % i) for i in range(2)])
        AM = T(sb("AM", [128, 4, 128]), "AM")
        ROPE = T(sb("ROPE", [128, 21, 32], F32), "ROPE")
        THX = T(sb("THX", [128, 512], F32), "THX")
        THY = T(sb("THY", [128, 512], F32), "THY")
        THYb = [THY.b, Buf("thy1")]
        ZS = T(sb("ZS", [128, D]), "ZS")
        OAG = T(sb("OAG", [128, D]), "OAG")
        Boag = [[Buf("oag%d_%d" % (u_, g_)) for g_ in range(4)] for u_ in range(4)]
        Boag_all = [b_ for r_ in Boag for b_ in r_]
        ZSb = [Buf("zs0"), Buf("zs1")]
        OAGT2 = [sb("OAGT%d" % i, [128, 8, 256]) for i in range(2)]
        Boagt2 = [[Buf("oagt%d_%d" % (i, j)) for j in range(2)] for i in range(2)]
        MT = sb("MT", [128, 8, 256])
        Bmt = [Buf("mt%d" % i) for i in range(8)]
        SS = sb("SS", [128, 4, 4], F32)
        Bss = Ring([(i, Buf("ss%d" % i)) for i in range(4)])
        RT = [T(sb("RT%d" % i, [128, 8, 16], F32), "RT%d" % i) for i in range(2)]
        RTK = [T(sb("RTK%d" % i, [128, 2, 16], F32), "RTK%d" % i) for i in range(2)]
        DEN = Ring([T(sb("DEN%d" % i, [128, 8], F32), "DEN%d" % i) for i in range(2)])
        ES2 = T(sb("ES2", [128, 8], F32), "ES2")
        NEGH = T(sb("NEGH", [128, 1], F32), "NEGH")
        ID = T(sb("ID", [128, 128]), "ID")

        MMB = [T(ps("MM%d" % i, [128, 512]), "MM%d" % i) for i in range(2)]
        STB = [T(ps("ST%d" % i, [128, 512]), "ST%d" % i) for i in range(4)]
        OB = [T(ps("O%d" % i, [128, 512]), "O%d" % i) for i in range(2)]
        mm = Ring(MMB)
        st_ring = Ring(STB)
        o_ring = Ring(OB)
        xr = Ring(XR)

        def cdma(eng, name, out_ap, in_ap, buf):
            s = P.new_dma_sem(name)
            P.dma(eng, s, lambda e: e.dma_start(out=out_ap, in_=in_ap), writes=[buf])

        Bgrep, Bfgrep = Buf("grep"), Buf("fgrep")
        cdma("sp", "c_grep", GREP[:, :], gain_d.partition_broadcast(128), Bgrep)
        cdma("sp", "c_rope", ROPE.t[:, :, :], rope_d, ROPE.b)
        cdma("sp", "c_sink", ES2.t[:, :], sink_d.partition_broadcast(128), ES2.b)
        cdma("pool", "c_am", AM.t[:, :, :], amask_d, AM.b)
        Bw = {name: Buf("w_" + name) for name, _, _ in W_PIECES}
        Bwpa, Bwpb, Bwo = Buf("wpa"), Buf("wpb"), [Buf("wo0"), Buf("wo1")]

        def wdma(names):
            for name in names:
                if name == "w_pa":
                    cdma("pool", "w_pa", WPA[:, :, :], wpa_d, Bwpa)
                elif name == "w_pb":
                    cdma("pool", "w_pb", WPB[:, :, :], wpb_d, Bwpb)
                elif name == "w_o0":
                    cdma("pool", "w_o0", WO[:, :, 0:512], wo_d[:, :, 0:512], Bwo[0])
                elif name == "w_o1":
                    cdma("pool", "w_o1", WO[:, :, 512:1024], wo_d[:, :, 512:1024], Bwo[1])
                else:
                    c0, c1 = [(a, b) for n_, a, b in W_PIECES if n_ == name][0]
                    cdma("pool", "w_" + name, W[:, :, c0:c1], win_d[:, :, c0:c1], Bw[name])
        wdma(["kva", "vb", "kb"])
        cdma("sp", "c_fgrep", FGREP[:, :], fgain_d.partition_broadcast(128), Bfgrep)

        def wbuf(c0):
            for name, a, b in W_PIECES:
                if a <= c0 < b:
                    return Bw[name]
            raise KeyError(c0)

        P.op("dve", lambda e: e.memset(THY.t[:, 0:128], 0.0), writes=[THY.b])
        P.op("pool", lambda e: e.affine_select(out=THY.t[:, 0:128], in_=THY.t[:, 0:128], pattern=[[-1, 128]],
                                               compare_op=ALU.not_equal, fill=1.0, base=0, channel_multiplier=1),
             reads=[THY.b], writes=[THY.b])
        P.op("dve", lambda e: e.tensor_copy(out=ID.t[:, :], in_=THY.t[:, 0:128]), reads=[THY.b], writes=[ID.b])
        P.op("dve", lambda e: e.memset(NEGH.t[:, :], -0.5), writes=[NEGH.b])
        P.op("dve", lambda e: e.memset(VA[:, :, :, :], 1.0), writes=Bva)
        P.op("dve", lambda e: e.memset(VB[:, :, :, :], 1.0), writes=Bvb)
        P.op("dve", lambda e: e.memset(VAM.t[:, :, :], 1.0), writes=[VAM.b])
        P.op("dve", lambda e: e.memset(VBM.t[:, :, :], 1.0), writes=[VBM.b])
        P.op("act", lambda e: e.activation(out=ES2.t[:, :], in_=ES2.t[:, :], func=AF.Exp), reads=[ES2.b], writes=[ES2.b])
        P.op("dve", lambda e: e.tensor_scalar(out=ES2.t[:, :], in0=ES2.t[:, :], scalar1=2.0, scalar2=None, op0=ALU.mult),
             reads=[ES2.b], writes=[ES2.b])

        def mmc(n_mm, ncols):
            return n_mm * max(95.0, 0.53 * ncols)

        def mm_group(out_ap, pairs, reads, wbufs, ncols=512):
            n = len(pairs)

            def fn(e):
                last = None
                for i, (l, r) in enumerate(pairs):
                    last = e.matmul(out=out_ap, lhsT=l, rhs=r, start=(i == 0), stop=(i == n - 1))
                return last
            P.op("pe", fn, reads=reads, writes=wbufs, cost=mmc(n, ncols))

        cp_toggle = [0]

        def copy(out_ap, in_ap, reads, writes, eng=None, dur=None):
            if eng is None:
                eng = ("act", "act", "dve")[cp_toggle[0] % 3]
                cp_toggle[0] += 1
            if eng == "act":
                P.op("act", lambda e: e.activation(out=out_ap, in_=in_ap, func=AF.Copy), reads=reads, writes=writes, dur=dur)
            else:
                P.op(eng, lambda e: e.tensor_copy(out=out_ap, in_=in_ap), reads=reads, writes=writes, dur=dur)

        def transposes(n_in, cols, src_t, src_col0, np_, bank, src_bufs=None):
            trv = bank.t.bitcast(BF16)

            def fn(e):
                last = None
                for i in range(n_in):
                    last = e.transpose(out=trv[:, i * 128:i * 128 + np_],
                                       in_=src_t.t[0:np_, src_col0 + i * 128:src_col0 + (i + 1) * 128],
                                       identity=ID.t[0:np_, 0:np_])
                return last
            P.op("pe", fn, reads=(src_bufs if src_bufs is not None else [src_t.b]) + [ID.b], writes=[bank.b], cost=mmc(n_in, 128))
            return trv

        def norm_rows(j, np_, src_ap, gain_t, gain_b, out_t, junk_t):
            xt = XR[j]
            si, sbuf_ = Bss.next()
            ss = SS[0:np_, si, :]
            P.op("act", lambda e: e.activation(out=junk_t.t[0:np_, :], in_=xt.t[0:np_, :], func=AF.Square,
                                               accum_out=ss[:, 0:1]), reads=[xt.b], writes=[junk_t.b, sbuf_], dur=1100.0)
            P.op("dve", lambda e: e.tensor_scalar(out=ss[:, 1:2], in0=ss[:, 0:1], scalar1=1.0 / D, scalar2=EPS,
                                                  op0=ALU.mult, op1=ALU.add), reads=[sbuf_], writes=[sbuf_], dur=120.0)
            P.op("pool", lambda e: e.tensor_tensor(out=ss[:, 2:3], in0=ss[:, 1:2], in1=NEGH.t[0:np_, :], op=ALU.pow),
                 reads=[sbuf_, NEGH.b], writes=[sbuf_])
            P.op("dve", lambda e: e.scalar_tensor_tensor(out=out_t.t[0:np_, :], in0=xt.t[0:np_, :], scalar=ss[:, 2:3],
                                                         in1=gain_t[0:np_, :], op0=ALU.mult, op1=ALU.mult),
                 reads=[xt.b, sbuf_, gain_b], writes=[out_t.b], dur=1300.0)

        def rope(src, sdims, dst, ddims, np_, rslot, rts, dst_buf, src_buf):
            nh = 1
            for _, c in sdims:
                nh *= c
            zero = [(0, c) for _, c in sdims]
            tdims = []
            acc = 16
            for _, c in reversed(sdims):
                tdims.insert(0, (acc, c))
                acc *= c
            rp = ROPE.t[0:np_, rslot, :]
            t1, t2 = rts

            def sub(base, off):
                return bass.AP(base.tensor, base.offset + off, base.ap)
            P.op("dve", lambda e: e.tensor_tensor(out=V(t1.t[0:np_, 0, :], tdims + [(1, 16)]), in0=V(src, sdims + [(1, 16)]),
                                                  in1=V(rp, zero + [(1, 16)]), op=ALU.mult),
                 reads=[src_buf, ROPE.b], writes=[t1.b])
            P.op("dve", lambda e: e.tensor_tensor(out=V(t2.t[0:np_, 0, :], tdims + [(1, 8)]), in0=V(sub(src, 8), sdims + [(1, 8)]),
                                                  in1=V(sub(rp, 16), zero + [(1, 8)]), op=ALU.mult),
                 reads=[src_buf, ROPE.b], writes=[t2.b])
            P.op("dve", lambda e: e.tensor_tensor(out=V(sub(t2.t[0:np_, 0, :], 8), tdims + [(1, 8)]), in0=V(src, sdims + [(1, 8)]),
                                                  in1=V(sub(rp, 24), zero + [(1, 8)]), op=ALU.mult),
                 reads=[src_buf, ROPE.b], writes=[t2.b])
            P.op("dve", lambda e: e.tensor_tensor(out=V(dst, ddims + [(1, 16)]), in0=V(t1.t[0:np_, 0, :], tdims + [(1, 16)]),
                                                  in1=V(t2.t[0:np_, 0, :], tdims + [(1, 16)]), op=ALU.add),
                 reads=[t1.b, t2.b], writes=[dst_buf])
            P.op("act", lambda e: e.activation(out=V(sub(dst, 16), ddims + [(1, 48)]), in_=V(sub(src, 16), sdims + [(1, 48)]),
                                               func=AF.Copy), reads=[src_buf], writes=[dst_buf])

        def kv_block(np_, lhs, lhs_bufs, rslot, kat_dst, kat_buf, va_dst, va_buf, vb_dst, vb_buf):
            b1 = mm.next()
            mm_group(b1.t[0:np_, 0:256], [(lhs(kc), W[:, kc, C_KA:C_KA + 256]) for kc in range(8)],
                     lhs_bufs + [Bw["kva"]], [b1.b], ncols=256)
            rope(b1.t[0:np_, 0:64], [(64, 2)], QKk.t[0:np_, 0:64], [(64, 2)], np_, rslot, RTK, QKk.b, b1.b)
            copy(va_dst, V(b1.t[0:np_, 128:192], [(64, 2), (1, 64)]), [b1.b], [va_buf], eng="act")
            b2 = mm.next()
            mm_group(b2.t[0:np_, 0:512], [(lhs(kc), W[:, kc, C_VB:C_VB + 512]) for kc in range(8)],
                     lhs_bufs + [Bw["vb"]], [b2.b])
            copy(vb_dst, V(b2.t[0:np_, 0:64], [(64, 8), (1, 64)]), [b2.b], [vb_buf], eng="act")
            b3 = mm.next()
            trv = transposes(1, 128, QKk, 0, np_, b3)
            copy(kat_dst, trv[:, 0:np_], [b3.b], [kat_buf])

        def kb_feature(n_tok, rhs, rhs_bufs, dst, dst_bufs):
            for c in range(4):
                b = mm.next()
                mm_group(b.t[:, 0:n_tok], [(W[:, kc, C_KB + c * 128:C_KB + (c + 1) * 128], rhs(kc)) for kc in range(8)],
                         rhs_bufs + [Bw["kb"]], [b.b], ncols=n_tok)
                copy(dst(c), b.t[:, 0:n_tok], [b.b], dst_bufs)

        def do_meta():
            x0 = xr.next()
            j = XR.index(x0)
            s = P.new_dma_sem("ld_meta")
            P.dma("sp", s, lambda e: e.dma_start(out=x0.t[0:16, :], in_=meta_d), writes=[x0.b])
            norm_rows(j, 16, None, GREP, Bgrep, NBt, NBt)
            b0 = mm.next()
            trv = transposes(8, 128, NBt, 0, 16, b0)
            copy(NTM.t[:, :, :], V(trv[:, 0:16], [(128, 8), (1, 16)]), [b0.b], [NTM.b])
            kv_block(16, lambda kc: NTM.t[:, kc, 0:16], [NTM.b], 20,
                     KATM.t[:, 0:16], KATM.b, VAM.t[0:16, :, 0:64], VAM.b, VBM.t[0:16, :, 0:64], VBM.b)
            kb_feature(16, lambda kc: NTM.t[:, kc, 0:16], [NTM.b], lambda c: KBTM.t[:, c, 0:16], [KBTM.b])

        ld_sems = [P.new_dma_sem("ld%d" % i) for i in range(3)]
        st_sems = [P.new_dma_sem("st%d" % i) for i in range(3)]

        def front(t, part=None):
            s = t % 3

            def load_norm(bi):
                pb = 2 * t + bi
                x0 = xr.next()
                j = XR.index(x0)
                P.dma("sp", ld_sems[j], lambda e, x0=x0, pb=pb: e.dma_start(out=x0.t[:, :], in_=xp_d[pb * 128:(pb + 1) * 128, :]),
                      writes=[x0.b])
                norm_rows(j, 128, None, GREP, Bgrep, NBt, NBt)

            def tr_nt(bi):
                b0 = mm.next()
                trv = transposes(8, 128, NBt, 0, 128, b0)
                copy(NT[s][:, :, bi * 128:(bi + 1) * 128], V(trv[:, 0:128], [(128, 8), (1, 128)]), [b0.b], [Bnt[s][bi]], dur=950.0)

            def kv(bi):
                pb = 2 * t + bi
                slot = pb % 8
                kv_block(128, lambda kc, bi=bi: NT[s][:, kc, bi * 128:(bi + 1) * 128], [Bnt[s][bi]], pb,
                         KAT[:, slot * 128:(slot + 1) * 128], Bkat[slot],
                         VA[:, slot, :, 0:64], Bva[slot], VB[:, slot, :, 0:64], Bvb[slot])
            if part in (None, "pre"):
                load_norm(0)
            if part == "pre":
                return
            tr_nt(0)
            load_norm(1)
            kv(0)
            tr_nt(1)
            kv(1)
            slot0 = (2 * t) % 8
            kb_feature(256, lambda kc: NT[s][:, kc, 0:256], [Bnt[s][0], Bnt[s][1]],
                       lambda c: KBT[:, c, slot0 * 128:slot0 * 128 + 256], [Bkbt[slot0], Bkbt[slot0 + 1]])

        def normalize(ob, unit_is_a, k_or_u):
            den = DEN.next()
            o_den = V(ob.t[:, 64:65], [(65, 4)])
            if unit_is_a:
                k = k_or_u
                P.op("dve", lambda e: e.scalar_tensor_tensor(out=den.t[:, 0:4], in0=o_den, scalar=2.0,
                                                             in1=ES2.t[:, 4 * k:4 * k + 4], op0=ALU.mult, op1=ALU.add),
                     reads=[ob.b, ES2.b], writes=[den.b])
            else:
                P.op("dve", lambda e: e.tensor_scalar(out=den.t[:, 0:4], in0=o_den, scalar1=2.0, scalar2=None, op0=ALU.mult),
                     reads=[ob.b], writes=[den.b])
            P.op("dve", lambda e: e.reciprocal(out=den.t[:, 4:8], in_=den.t[:, 0:4]), reads=[den.b], writes=[den.b], dur=200.0)
            for g in range(4):
                col = (k_or_u * 256 + g * 64) if unit_is_a else (512 + (2 * g + k_or_u) * 64)
                P.op("dve", lambda e, g=g, col=col: e.scalar_tensor_tensor(
                    out=OAG.t[:, col:col + 64], in0=ob.t[:, g * 65:g * 65 + 64], scalar=den.t[:, 4 + g:5 + g],
                    in1=ZS.t[:, col:col + 64], op0=ALU.mult, op1=ALU.mult),
                    reads=[ob.b, den.b, ZSb[0 if unit_is_a else 1]], writes=[Boag[(0 if unit_is_a else 2) + k_or_u][g]], dur=360.0)

        def pv_op(ob, pt, nk, rhs_fn, first, last, reads):
            def fn(e):
                r = None
                for g in range(4):
                    r = e.matmul(out=ob.t[:, g * 65:(g + 1) * 65], lhsT=pt.t[0:nk, g * 128:(g + 1) * 128], rhs=rhs_fn(g),
                                 start=(first and g == 0), stop=(last and g == 3), skip_group_check=True)
                return r
            P.op("pe", fn, reads=[pt.b] + reads, writes=[ob.b], cost=220.0)

        def piece_idx(m, jj):
            if m == 0:
                return 5 + jj
            if m == 1:
                return 11 + jj
            if m == 14:
                return 16 + jj
            if m == 15:
                return 21 + jj
            return jj

        tb_sems = [P.new_dma_sem("tb%d" % i) for i in range(3)]

        def attn_block(pb, bi, extra=None):
            lb = pb - 2
            tasks = []
            obsA = [o_ring.next(), o_ring.next()]
            chunksA = [("M", 16, None), ("L", 128, pb - 1), ("C", 128, pb), ("R", 128, pb + 1)]
            for ci, (typ, nk, kb_) in enumerate(chunksA):
                def S(state, typ=typ, nk=nk, kb_=kb_):
                    sts = [st_ring.next(), st_ring.next()]
                    if typ == "M":
                        kbuf = KATM.b
                        lhs_fn = lambda k: KATM.t[64 * k:64 * k + 64, 0:16]
                    else:
                        slot = kb_ % 8
                        kbuf = Bkat[slot]
                        lhs_fn = lambda k, slot=slot: KAT[64 * k:64 * k + 64, slot * 128:(slot + 1) * 128]

                    def sfn(e):
                        r = None
                        for k in range(2):
                            r = e.matmul(out=sts[k].t[0:nk, 0:512], lhsT=lhs_fn(k),
                                         rhs=V(QTA[64 * k:64 * k + 64, 0, bi * 128:(bi + 1) * 128], [(256, 4), (1, 128)]),
                                         start=True, stop=True)
                        return r
                    P.op("pe", sfn, reads=[kbuf, Bqta[bi]], writes=[sts[0].b, sts[1].b], cost=410.0)
                    state["pts"] = []
                    for k in range(2):
                        st = sts[k]
                        pt = PT.next()
                        P.op("act", lambda e, st=st, pt=pt: e.activation(out=pt.t[0:nk, :], in_=st.t[0:nk, 0:512], func=AF.Exp, scale=0.125),
                             reads=[st.b], writes=[pt.b], dur=560.0)
                        if typ in ("L", "R"):
                            mt_ = (2 if lb == 0 else 0) if typ == "L" else (3 if lb == 15 else 1)
                            P.op("dve", lambda e, pt=pt, mt_=mt_: e.tensor_tensor(out=V(pt.t[:, 0:128], [(128, 4), (1, 128)]),
                                                                                  in0=V(pt.t[:, 0:128], [(128, 4), (1, 128)]),
                                                                                  in1=V(AM.t[:, mt_, :], [(0, 4), (1, 128)]), op=ALU.mult),
                                 reads=[pt.b, AM.b], writes=[pt.b])
                        state["pts"].append(pt)

                def PV(state, ci=ci, typ=typ, nk=nk, kb_=kb_):
                    for k in range(2):
                        if typ == "M":
                            vb_, rhs_v = VAM.b, (lambda g, k=k: VAM.t[0:16, k, :])
                        else:
                            slot = kb_ % 8
                            vb_, rhs_v = Bva[slot], (lambda g, k=k, slot=slot: VA[:, slot, k, :])
                        pv_op(obsA[k], state["pts"][k], nk, rhs_v, ci == 0, ci == 3, [vb_])
                    if ci == 3:
                        for k in range(2):
                            normalize(obsA[k], True, k)
                tasks.append((S, PV, {}))
            m = lb
            js = list(range(0, 6)) if m == 0 else (list(range(-1, 5)) if m == 15 else list(range(0, 5)))
            obsB = [o_ring.next(), o_ring.next()]
            chunksB = [("M", 16, None, None)] + [("W", 128, pb + j - 2, jj) for jj, j in enumerate(js)]
            nB = len(chunksB)
            for ci, (typ, nk, kb_, jj) in enumerate(chunksB):
                def S(state, typ=typ, nk=nk, kb_=kb_, jj=jj):
                    tb = None
                    if typ == "W":
                        tb = TBE.next()
                        ti = TBE.items.index(tb)
                        pi = piece_idx(m, jj)
                        P.dma("pool", tb_sems[ti], lambda e, tb=tb, pi=pi: e.dma_start(out=tb.t[:, :], in_=btab_d[pi * 128:(pi + 1) * 128, :]),
                              writes=[tb.b])
                    sts = [st_ring.next(), st_ring.next()]
                    if typ == "M":
                        kbuf = KBTM.b
                        lhs_fn = lambda c, u: KBTM.t[64 * u:64 * u + 64, c, 0:16]
                    else:
                        slot = kb_ % 8
                        kbuf = Bkbt[slot]
                        lhs_fn = lambda c, u, slot=slot: KBT[64 * u:64 * u + 64, c, slot * 128:(slot + 1) * 128]

                    def sfn(e):
                        r = None
                        for c in range(4):
                            for u in range(2):
                                r = e.matmul(out=sts[u].t[0:nk, c * 128:(c + 1) * 128], lhsT=lhs_fn(c, u),
                                             rhs=QTB.t[64 * u:64 * u + 64, c, bi * 128:(bi + 1) * 128], start=True, stop=True)
                        return r
                    P.op("pe", sfn, reads=[kbuf, QTB.b], writes=[sts[0].b, sts[1].b], cost=460.0)
                    state["pts"] = []
                    for u in range(2):
                        st = sts[u]
                        pt = PT.next()
                        if typ == "W":
                            P.op("dve", lambda e, st=st, tb=tb, u=u: e.scalar_tensor_tensor(
                                out=st.t[:, 0:512], in0=st.t[:, 0:512], scalar=0.125,
                                in1=V(tb.t[:, u * 128:(u + 1) * 128], [(256, 4), (1, 128)]), op0=ALU.mult, op1=ALU.add),
                                reads=[st.b, tb.b], writes=[st.b], dur=520.0)
                            P.op("act", lambda e, st=st, pt=pt: e.activation(out=pt.t[:, :], in_=st.t[:, 0:512], func=AF.Exp),
                                 reads=[st.b], writes=[pt.b], dur=620.0)
                        else:
                            P.op("act", lambda e, st=st, pt=pt: e.activation(out=pt.t[0:16, :], in_=st.t[0:16, 0:512], func=AF.Exp, scale=0.125),
                                 reads=[st.b], writes=[pt.b])
                        state["pts"].append(pt)

                def PV(state, ci=ci, typ=typ, nk=nk, kb_=kb_):
                    for u in range(2):
                        if typ == "M":
                            vb_, rhs_v = VBM.b, (lambda c, u=u: VBM.t[0:16, 2 * c + u, :])
                        else:
                            slot = kb_ % 8
                            vb_, rhs_v = Bvb[slot], (lambda c, u=u, slot=slot: VB[:, slot, 2 * c + u, :])
                        pv_op(obsB[u], state["pts"][u], nk, rhs_v, ci == 0, ci == nB - 1, [vb_])
                    if ci == nB - 1:
                        for u in range(2):
                            normalize(obsB[u], False, u)
                tasks.append((S, PV, {}))
            prev = None
            for ti, (S, PV, state) in enumerate(tasks):
                S(state)
                if prev is not None:
                    prev[0](prev[1])
                prev = (PV, state)
                if extra is not None and ti == extra[0]:
                    extra[1]()
            prev[0](prev[1])

        oagt_ver = [0, 0]
        dc_done = [0]

        def qproj(t):
            s = t % 3

            def qa(bi):
                pb = 2 * t + bi
                b = st_ring.next()
                mm_group(b.t[:, 0:512], [(NT[s][:, kc, bi * 128:(bi + 1) * 128], W[:, kc, C_QA:C_QA + 512]) for kc in range(8)],
                         [Bnt[s][bi], Bw["qa"]], [b.b])
                rope(b.t[:, 0:64], [(256, 2), (64, 4)], QKq2[bi].t[:, 0:64], [(64, 2), (128, 4)], 128, pb, RT, QKq2[bi].b, b.b)

            def qtr(bi):
                b2 = st_ring.next()
                trv = transposes(4, 128, QKq2[bi], 0, 128, b2)
                copy(QTA[:, :, bi * 128:(bi + 1) * 128], V(trv[:, 0:128], [(128, 4), (1, 128)]), [b2.b], [Bqta[bi]])

            def qb(c):
                b = st_ring.next()
                mm_group(b.t[:, 0:256], [(W[:, kc, C_QB + c * 128:C_QB + (c + 1) * 128], NT[s][:, kc, 0:256]) for kc in range(8)],
                         [Bnt[s][0], Bnt[s][1], Bw["qb"]], [b.b], ncols=256)
                copy(QTB.t[:, c, :], b.t[:, 0:256], [b.b], [QTB.b])
            qa(0)
            qa(1)
            qb(0)
            qtr(0)
            qb(1)
            qtr(1)
            qb(2)
            qb(3)

        zstate = {}

        def attn(t, part, sec=None):
            s = t % 3
            zbanks = zstate.setdefault(t, [])
            if part == 1 and sec in (None, "a"):
                for br, c0, wn in ((0, C_ZA, "za"), (1, C_ZB, "zb")):
                    b = st_ring.next()
                    mm_group(b.t[:, 0:512], [(NT[s][:, kc, 128:256], W[:, kc, c0:c0 + 512]) for kc in range(8)],
                             [Bnt[s][1], Bw[wn]], [b.b])
                    zbanks.append(b)

            def oag_b0():
                def chk0():
                    assert dc_done[0] >= t - 2, ("OAGT blk0 overwritten before back_dc", t, dc_done[0])
                P.check(chk0)
                b2 = st_ring.next()
                trv = transposes(8, 128, OAG, 0, 128, b2, src_bufs=Boag_all)
                copy(OAGT2[t % 2][:, :, 0:128], V(trv[:, 0:128], [(128, 8), (1, 128)]), [b2.b], [Boagt2[t % 2][0]], dur=950.0)

                def set0():
                    oagt_ver[0] = t
                P.check(set0)
            if part == 1 and sec in (None, "a") and t == 8:
                oag_b0()
            for bi in ((0,) if part == 0 else (1,)):
                pb = 2 * t + bi
                for br, c0, wn in ((0, C_ZA, "za"), (1, C_ZB, "zb")):
                    if part == 1 and sec == "b":
                        continue
                    if part == 1:
                        b = zbanks[br]
                        th = THX
                        P.op("act", lambda e, b=b, th=th: e.activation(out=th.t[:, :], in_=b.t[:, 0:512], func=AF.Tanh, scale=0.5),
                             reads=[b.b], writes=[th.b])
                        P.op("dve", lambda e, b=b, th=th, br=br: e.scalar_tensor_tensor(
                            out=ZS.t[:, br * 512:(br + 1) * 512], in0=th.t[:, :], scalar=1.0, in1=b.t[:, 0:512],
                            op0=ALU.add, op1=ALU.mult), reads=[th.b, b.b], writes=[ZSb[br]])
                        continue
                    b = st_ring.next()
                    mm_group(b.t[:, 0:512], [(NT[s][:, kc, bi * 128:(bi + 1) * 128], W[:, kc, c0:c0 + 512]) for kc in range(8)],
                             [Bnt[s][bi], Bw[wn]], [b.b])
                    th = THX
                    P.op("act", lambda e, b=b, th=th: e.activation(out=th.t[:, :], in_=b.t[:, 0:512], func=AF.Tanh, scale=0.5),
                         reads=[b.b], writes=[th.b])
                    P.op("dve", lambda e, b=b, th=th, br=br: e.scalar_tensor_tensor(
                        out=ZS.t[:, br * 512:(br + 1) * 512], in0=th.t[:, :], scalar=1.0, in1=b.t[:, 0:512],
                        op0=ALU.add, op1=ALU.mult), reads=[th.b, b.b], writes=[ZSb[br]])
                if part == 1 and sec == "a":
                    continue
                attn_block(pb, bi, extra=((1, oag_b0) if (bi == 1 and t < 8) else None))
                if bi == 1:
                    if t < 8:
                        qproj(t + 1)
                    def chk1():
                        assert dc_done[0] >= t - 2, ("OAGT blk1 overwritten before back_dc", t, dc_done[0])
                    P.check(chk1)
                    b2 = st_ring.next()
                    trv = transposes(8, 128, OAG, 0, 128, b2, src_bufs=Boag_all)
                    copy(OAGT2[t % 2][:, :, 128:256], V(trv[:, 0:128], [(128, 8), (1, 128)]), [b2.b], [Boagt2[t % 2][1]], dur=950.0)

                    def set1():
                        oagt_ver[1] = t
                    P.check(set1)

        def back_dc(t, blks=(0, 1)):
            s = t % 3
            lo, n = blks[0] * 128, 128 * len(blks)

            def chk():
                for bb in blks:
                    assert oagt_ver[bb] == t, ("back_dc reads stale OAGT", t, bb, oagt_ver)
            P.check(chk)
            for dc in range(8):
                bA, bB = mm.next(), mm.next()
                th = THY
                for bnk, wp, bwp, koff, cg, half in ((bA, WPA, Bwpa, 0, C_GA, 0), (bB, WPB, Bwpb, 4, C_GB, 1)):
                    mm_group(bnk.t[:, 0:n], [(wp[:, kc, dc * 128:(dc + 1) * 128], OAGT2[t % 2][:, koff + kc, lo:lo + n]) for kc in range(4)],
                             [bwp] + [Boagt2[t % 2][bb] for bb in blks], [bnk.b], ncols=n)
                    gc = cg + dc * 128
                    mm_group(bnk.t[:, 256:256 + n], [(W[:, kc, gc:gc + 128], NT[s][:, kc, lo:lo + n]) for kc in range(8)],
                             [wbuf(gc)] + [Bnt[s][bb] for bb in blks], [bnk.b], ncols=n)
                    P.op("act", lambda e, bnk=bnk, th=th, half=half: e.activation(out=th.t[:, half * 256:half * 256 + n], in_=bnk.t[:, 256:256 + n],
                                                                                 func=AF.Tanh, scale=0.5), reads=[bnk.b], writes=[THYb[half]])
                    P.op("dve", lambda e, bnk=bnk, th=th, half=half: e.scalar_tensor_tensor(
                        out=th.t[:, half * 256:half * 256 + n], in0=th.t[:, half * 256:half * 256 + n], scalar=1.0,
                        in1=bnk.t[:, 0:n], op0=ALU.add, op1=ALU.mult), reads=[THYb[half], bnk.b], writes=[THYb[half]])
                P.op("dve", lambda e, th=th, dc=dc: e.tensor_tensor(out=MT[:, dc, lo:lo + n], in0=th.t[:, 0:n], in1=th.t[:, 256:256 + n], op=ALU.add),
                     reads=THYb, writes=[Bmt[dc]])

            def done():
                if len(blks) == 2 or blks[0] == 1:
                    dc_done[0] = t
            P.check(done)

        def back_out(t, blks=(0, 1)):
            xs = {}
            for bi in blks:
                pb = 2 * t + bi
                x0 = xr.next()
                j = XR.index(x0)
                P.dma("sp", ld_sems[j], lambda e, x0=x0, pb=pb: e.dma_start(out=x0.t[:, :], in_=xp_d[pb * 128:(pb + 1) * 128, :]),
                      writes=[x0.b])
                xs[bi] = (x0, j)
            for bi in blks:
                pb = 2 * t + bi
                lb = pb - 2
                x0, j = xs[bi]
                for half in range(2):
                    b = mm.next()
                    mm_group(b.t[:, 0:512], [(MT[:, kc, bi * 128:(bi + 1) * 128], WO[:, kc, half * 512:(half + 1) * 512]) for kc in range(8)],
                             Bmt + [Bwo[half]], [b.b])
                    P.op("dve", lambda e, b=b, x0=x0, half=half: e.scalar_tensor_tensor(
                        out=x0.t[:, half * 512:(half + 1) * 512], in0=b.t[:, 0:512], scalar=0.5,
                        in1=x0.t[:, half * 512:(half + 1) * 512], op0=ALU.mult, op1=ALU.add), reads=[b.b, x0.b], writes=[x0.b], dur=560.0)
                norm_rows(j, 128, None, FGREP, Bfgrep, x0, NBt)
                P.dma("sp", st_sems[j], lambda e, x0=x0, lb=lb: e.dma_start(out=y_d[lb * 128:(lb + 1) * 128, :], in_=x0.t[:, :]),
                      reads=[x0.b])

        import os
        if os.environ.get('KDEBUG'):
            print('SBUF base/top', nc.sbuf_base, nc.sbuf_top, 'free', nc.sbuf_top - nc.sbuf_base)
        do_meta()
        wdma(["qa", "qb"])
        front(0)
        wdma(["za", "zb", "ga0", "ga1"])
        front(1)
        wdma(["gb0", "gb1", "w_pa", "w_pb", "w_o0", "w_o1"])
        front(2)
        qproj(1)
        for t in range(1, 9):
            if t < 8:
                def xthread():
                    attn(t, 0)
                    attn(t, 1)

                def ythread():
                    if t + 2 <= 9:
                        front(t + 2, "pre")
                    if t > 1:
                        back_dc(t - 1)
                    if t + 2 <= 9:
                        front(t + 2, "rest")
                    if t > 1:
                        back_out(t - 1)
                P.schedule([P.record(xthread), P.record(ythread)])
            else:
                def y8a():
                    back_dc(7)
                    back_out(7)
                def x8a():
                    attn(8, 0)
                    attn(8, 1, "a")
                P.schedule([P.record(x8a), P.record(y8a)])
                X1b = P.record(lambda: attn(8, 1, "b"))

                def y8b():
                    back_dc(8, (0,))
                    back_out(8, (0,))
                P.schedule([X1b, P.record(y8b)])
        back_dc(8, (1,))
        back_out(8, (1,))
        final = [(s, P.seq[s]) for s in st_sems if P.seq[s] > 0]
        P.emit(final)
    return nc


def _rope_table(c):
    half = 8
    inv_freq = (np.float32(500000.0) ** (-np.arange(half, dtype=np.float32) / np.float32(half))).astype(np.float32)
    tab = np.zeros((128, 21, 32), np.float32)
    p = np.arange(128)
    for pb in range(NPB):
        tok = c * TOK - HALO + pb * 128 + p
        pos = (tok + NMETA).astype(np.float32)
        ang = (pos[:, None] * inv_freq[None, :]).astype(np.float32)
        cs, sn = np.cos(ang).astype(np.float32), np.sin(ang).astype(np.float32)
        tab[:, pb, 0:8] = cs
        tab[:, pb, 8:16] = cs
        tab[:, pb, 16:24] = -sn
        tab[:, pb, 24:32] = sn
    pos = np.arange(128).astype(np.float32)
    ang = (pos[:, None] * inv_freq[None, :]).astype(np.float32)
    cs, sn = np.cos(ang).astype(np.float32), np.sin(ang).astype(np.float32)
    tab[:, 20, 0:8] = cs
    tab[:, 20, 8:16] = cs
    tab[:, 20, 16:24] = -sn
    tab[:, 20, 24:32] = sn
    return tab


def _amask(c):
    j = np.arange(128)[:, None]
    i = np.arange(128)[None, :]
    L = (j >= i).astype(np.float32)
    R = (j <= i).astype(np.float32)
    am = np.zeros((128, 4, 128), np.float32)
    am[:, 0] = L
    am[:, 1] = R
    am[:, 2] = L if c > 0 else 0.0
    am[:, 3] = R if c < 3 else 0.0
    return am


def _btab_piece(rpb, c, m, j):
    R0 = 32 * c + 2 * m
    KR0 = R0 + 2 * (j - 2)
    kp = np.arange(128)
    qp = np.arange(128)
    kR = KR0 + kp // 64
    kc = kp % 64
    r = R0 + qp // 64
    qc = qp % 64
    r_start = np.clip(r - 4, 0, 128 - 8)
    cstart = np.clip(qc - 8, 0, 64 - 16)
    ok_r = (kR[:, None] >= r_start[None, :]) & (kR[:, None] < r_start[None, :] + 8) & (kR[:, None] >= 0) & (kR[:, None] < 128)
    ok_c = (kc[:, None] >= cstart[None, :]) & (kc[:, None] < cstart[None, :] + 16)
    ok = ok_r & ok_c
    dr = np.clip(kR[:, None] - r[None, :] + 7, 0, 14)
    dc = np.clip(kc[:, None] - qc[None, :] + 15, 0, 30)
    out = np.full((128, 8, 128), NEG, np.float32)
    for h in range(8):
        g = rpb[h][dr, dc]
        out[:, h, :] = np.where(ok, g, np.float32(NEG))
    return out


def _btab(rpb, c):
    pieces = []
    for j in range(5):
        pieces.append(_btab_piece(rpb, 1, 8, j))
    for j in range(6):
        pieces.append(_btab_piece(rpb, c, 0, j))
    for j in range(5):
        pieces.append(_btab_piece(rpb, c, 1, j))
    for j in range(5):
        pieces.append(_btab_piece(rpb, c, 14, j))
    for j in range(-1, 5):
        pieces.append(_btab_piece(rpb, c, 15, j))
    return np.stack(pieces).reshape(27 * 128, 1024)


_NC_CACHE = {}


def kernel(x, meta_tokens, norm_gain, w_in, sink_logits, rel_pos_bias, w_proj_a, w_proj_b, w_out, final_norm_gain):
    f = np.float32
    x = np.asarray(x, f)
    w_in_l = np.ascontiguousarray(np.asarray(w_in, f)[0].reshape(8, 128, NCOL).transpose(1, 0, 2))
    w_pa_l = np.ascontiguousarray(np.asarray(w_proj_a, f)[0].reshape(4, 128, D).transpose(1, 0, 2))
    w_pb_l = np.ascontiguousarray(np.asarray(w_proj_b, f)[0].reshape(4, 128, D).transpose(1, 0, 2))
    w_out_l = np.ascontiguousarray(np.asarray(w_out, f)[0].reshape(8, 128, D).transpose(1, 0, 2))
    gain = np.ascontiguousarray(np.asarray(norm_gain, f).reshape(1, D))
    fgain = np.ascontiguousarray(np.asarray(final_norm_gain, f).reshape(1, D))
    sink = np.ascontiguousarray(np.asarray(sink_logits, f).reshape(1, 8))
    meta = np.ascontiguousarray(np.asarray(meta_tokens, f))
    rpb = np.asarray(rel_pos_bias, f)[0]
    ropes = [_rope_table(c) for c in range(4)]
    amasks = [_amask(c) for c in range(4)]
    btabs = [_btab(rpb, c) for c in range(4)]
    in_maps = []
    for ci in range(8):
        b, c = divmod(ci, 4)
        xp = np.zeros((NPB * 128, D), f)
        lo, hi = c * TOK - HALO, c * TOK + TOK + HALO
        slo, shi = max(lo, 0), min(hi, SEQ)
        xp[slo - lo:shi - lo] = x[b, slo:shi]
        in_maps.append({"xp": xp, "meta": meta, "w_in": w_in_l, "w_pa": w_pa_l, "w_pb": w_pb_l, "w_out": w_out_l,
                        "gain": gain, "fgain": fgain, "sink": sink, "rope": ropes[c], "amask": amasks[c], "btab": btabs[c]})
    if "nc" not in _NC_CACHE:
        _NC_CACHE["nc"] = build_nc()
    res = run_bass_kernel_spmd(_NC_CACHE["nc"], in_maps, core_ids=list(range(8)))
    out = np.zeros((2, SEQ, D), f)
    for ci in range(8):
        b, c = divmod(ci, 4)
        out[b, c * TOK:(c + 1) * TOK] = res.results[ci]["y"]
    return out
```

```python
import numpy as np
from contextlib import ExitStack
import concourse.bass as bass
import concourse.mybir as mybir
from concourse.bass_utils import run_bass_kernel_spmd

F32 = mybir.dt.float32
BF16 = mybir.dt.bfloat16
AF = mybir.ActivationFunctionType
ALU = mybir.AluOpType

D = 1024
NCOL = 5376
SEQ = 8192
NMETA = 16
TOK = 2048
HALO = 256
NPB = 20
EPS = 1e-6
NEG = -30000.0
C_QA, C_KA, C_VA, C_ZA, C_QB, C_KB, C_VB, C_ZB, C_GA, C_GB = 0, 512, 640, 768, 1280, 1792, 2304, 2816, 3328, 4352
W_PIECES = [("kva", 512, 768), ("vb", 2304, 2816), ("kb", 1792, 2304), ("qa", 0, 512), ("qb", 1280, 1792),
            ("za", 768, 1280), ("zb", 2816, 3328), ("ga0", 3328, 3840), ("ga1", 3840, 4352),
            ("gb0", 4352, 4864), ("gb1", 4864, 5376)]


class Buf:
    __slots__ = ("name", "lw", "rd")

    def __init__(self, name):
        self.name = name
        self.lw = None
        self.rd = []


class Prog:
    ENGS = ("pe", "act", "dve", "pool", "sp")

    def __init__(self, nc, same_raw=True):
        self.nc = nc
        self.stream = {e: [] for e in self.ENGS}
        self.seq = {e: 0 for e in self.ENGS}
        self.known = {e: {} for e in self.ENGS}
        self.same_raw = same_raw
        self.dma_sems = []
        self.rec = None
        self.efree = {}
        self.done = {}

    def _deps(self, eng, reads, writes):
        deps = {}

        def add(d, raw):
            key, val = d
            if key == eng and (eng == "pe" or not self.same_raw):
                return
            if deps.get(key, 0) < val:
                deps[key] = val

        for b in reads:
            if b.lw is not None:
                add(b.lw, True)
        for b in writes:
            if b.lw is not None:
                add(b.lw, False)
            for r in b.rd:
                add(r, False)
        out = []
        kn = self.known[eng]
        for key, val in deps.items():
            if kn.get(key, 0) < val:
                kn[key] = val
                out.append((key, val))
        return out

    @staticmethod
    def _mark(me, reads, writes):
        for b in reads:
            b.rd.append(me)
        for b in writes:
            b.lw = me
            b.rd = []

    DUR = {"act": 450.0, "dve": 350.0, "pool": 500.0, "sp": 60.0, "pe": 100.0}

    def op(self, eng, fn, reads=(), writes=(), cost=0.0, dur=None):
        if dur is None:
            dur = cost if eng == "pe" else self.DUR[eng]
        item = ("op", eng, fn, tuple(reads), tuple(writes), cost, dur, None)
        if self.rec is not None:
            self.rec.append(item)
            return
        self._play_item(item)

    def dma(self, eng, sem, fn, reads=(), writes=()):
        item = ("dma", eng, fn, tuple(reads), tuple(writes), 0.0, 60.0, sem)
        if self.rec is not None:
            self.rec.append(item)
            return
        self._play_item(item)

    def check(self, f):
        item = ("chk", None, f, (), (), 0.0, 0.0, None)
        if self.rec is not None:
            self.rec.append(item)
        else:
            f()

    def record(self, f):
        self.rec = []
        f()
        r, self.rec = self.rec, None
        return r

    def _ready(self, eng, reads, writes):
        t = 0.0
        for b in reads:
            if b.lw is not None:
                t = max(t, self.done.get(b.lw, 0.0))
        for b in writes:
            if b.lw is not None and b.lw[0] != eng:
                t = max(t, self.done.get(b.lw, 0.0))
            for r in b.rd:
                if r[0] != eng:
                    t = max(t, self.done.get(r, 0.0))
        return t

    def _play_item(self, item):
        kind, eng, fn, reads, writes, cost, dur, sem = item
        if kind == "chk":
            fn()
            return
        start = max(self.efree.get(eng, 0.0), self._ready(eng, reads, writes) + 120.0)
        if kind == "op":
            self._op(eng, fn, reads, writes)
            me = (eng, self.seq[eng])
            if eng == "pe":
                self.efree[eng] = start + dur
                self.done[me] = start + dur + 180.0
            else:
                self.efree[eng] = start + dur
                self.done[me] = start + dur
        else:
            self._dma(eng, sem, fn, reads, writes)
            me = (sem, self.seq[sem])
            self.efree[eng] = start + dur
            self.done[me] = start + 3500.0

    def play(self, items):
        for it in items:
            self._play_item(it)

    def schedule(self, threads):
        def bundles(L):
            out = []
            for it in L:
                if (it[0] == "op" and it[1] == "pe") or not out:
                    out.append([it])
                else:
                    out[-1].append(it)
            return out
        bl = [bundles(L) for L in threads]
        pos = [0] * len(bl)
        rem = [sum(it[5] for bd in b for it in bd) for b in bl]
        while True:
            best = None
            for i, b in enumerate(bl):
                if pos[i] >= len(b):
                    continue
                first = next((it for it in b[pos[i]] if it[0] != "chk"), None)
                if first is None:
                    est = 0.0
                else:
                    est = max(self.efree.get(first[1], 0.0), self._ready(first[1], first[3], first[4]) + 120.0)
                key = (est - 0.02 * rem[i], i)
                if best is None or key < best[0]:
                    best = (key, i)
            if best is None:
                break
            i = best[1]
            bd = bl[i][pos[i]]
            pos[i] += 1
            rem[i] -= sum(it[5] for it in bd)
            for it in bd:
                self._play_item(it)

    def _op(self, eng, fn, reads, writes):
        waits = self._deps(eng, reads, writes)
        self.seq[eng] += 1
        self._mark((eng, self.seq[eng]), reads, writes)
        self.stream[eng].append((waits, fn, eng, 1))

    def new_dma_sem(self, name):
        self.dma_sems.append(name)
        self.seq[name] = 0
        return name

    def _dma(self, eng, sem, fn, reads, writes):
        waits = self._deps(eng, reads, writes)
        self.seq[sem] += 16
        self._mark((sem, self.seq[sem]), reads, writes)
        self.stream[eng].append((waits, fn, sem, 16))

    def emit(self, final_waits):
        nc = self.nc
        with ExitStack() as es:
            H = {}
            for k in list(self.ENGS) + self.dma_sems:
                H[k] = es.enter_context(nc.semaphore("s_" + k))
            blk = es.enter_context(nc.Block())

            def run(e, items):
                for waits, fn, key, inc in items:
                    for wk, wv in waits:
                        e.wait_ge(H[wk], wv)
                    if fn is not None:
                        fn(e).then_inc(H[key], inc)

            self.stream["sp"].append((final_waits, None, None, 0))
            blk.tensor(lambda e: run(e, self.stream["pe"]))
            blk.scalar(lambda e: run(e, self.stream["act"]))
            blk.vector(lambda e: run(e, self.stream["dve"]))
            blk.gpsimd(lambda e: run(e, self.stream["pool"]))
            blk.sync(lambda e: run(e, self.stream["sp"]))


def V(base, dims):
    return bass.AP(base.tensor, base.offset, [list(base.ap[0])] + [list(d) for d in dims])


class Ring:
    def __init__(self, items):
        self.items = items
        self.i = 0

    def next(self):
        it = self.items[self.i % len(self.items)]
        self.i += 1
        return it


class T:
    __slots__ = ("t", "b")

    def __init__(self, t, name):
        self.t = t
        self.b = Buf(name)


def build_nc():
    nc = bass.Bass("TRN2", target_bir_lowering=False, dynamic_dma_scratch_size=12288)

    def din(name, shape):
        return nc.dram_tensor(name, shape, F32, kind="ExternalInput").ap()

    xp_d = din("xp", [NPB * 128, D])
    meta_d = din("meta", [NMETA, D])
    win_d = din("w_in", [128, 8, NCOL])
    wpa_d = din("w_pa", [128, 4, D])
    wpb_d = din("w_pb", [128, 4, D])
    wo_d = din("w_out", [128, 8, D])
    gain_d = din("gain", [1, D])
    fgain_d = din("fgain", [1, D])
    sink_d = din("sink", [1, 8])
    rope_d = din("rope", [128, 21, 32])
    amask_d = din("amask", [128, 4, 128])
    btab_d = din("btab", [27 * 128, 1024])
    y_d = nc.dram_tensor("y", [TOK, D], F32, kind="ExternalOutput").ap()

    with ExitStack() as es:
        def sb(name, shape, dt=BF16):
            return es.enter_context(nc.sbuf_tensor(name, shape, dt))

        def ps(name, shape, dt=F32):
            return es.enter_context(nc.psum_tensor(name, shape, dt))

        P = Prog(nc)
        W = sb("W", [128, 8, NCOL])
        WPA = sb("WPA", [128, 4, D])
        WPB = sb("WPB", [128, 4, D])
        WO = sb("WO", [128, 8, D])
        GREP = sb("GREP", [128, D], F32)
        FGREP = sb("FGREP", [128, D], F32)
        XR = [T(sb("XR%d" % i, [128, D], F32), "XR%d" % i) for i in range(3)]
        NBt = T(sb("NB", [128, D]), "NB")
        NT = [sb("NT%d" % i, [128, 8, 256]) for i in range(3)]
        Bnt = [[Buf("nt%d_%d" % (i, j)) for j in range(2)] for i in range(3)]
        NTM = T(sb("NTM", [128, 8, 16]), "NTM")
        QKq2 = [T(sb("QKq%d" % i, [128, 512]), "QKq%d" % i) for i in range(2)]
        QKk = T(sb("QKk", [128, 128]), "QKk")
        QTA = sb("QTA", [128, 4, 256])
        Bqta = [Buf("qta0"), Buf("qta1")]
        QTB = T(sb("QTB", [128, 4, 256]), "QTB")
        QTBb = [Buf("qtb%d" % c_) for c_ in range(4)]
        KAT = sb("KAT", [128, 8 * 128])
        VA = sb("VA", [128, 8, 2, 65])
        KBT = sb("KBT", [128, 4, 8 * 128])
        VB = sb("VB", [128, 8, 8, 65])
        Bkat = [Buf("kat%d" % i) for i in range(8)]
        Bva = [Buf("va%d" % i) for i in range(8)]
        Bkbt = [[Buf("kbt%d_%d" % (i, c_)) for c_ in range(4)] for i in range(8)]
        Bvb = [Buf("vb%d" % i) for i in range(8)]
        KATM = T(sb("KATM", [128, 16]), "KATM")
        KBTM = T(sb("KBTM", [128, 4, 16]), "KBTM")
        VAM = T(sb("VAM", [128, 2, 65]), "VAM")
        VBM = T(sb("VBM", [128, 8, 65]), "VBM")
        PT = Ring([T(sb("PT%d" % i, [128, 512]), "PT%d" % i) for i in range(4)])
        TBE = Ring([T(sb("TBE%d" % i, [128, 1024]), "TBE%d" % i) for i in range(2)])
        AM = T(sb("AM", [128, 4, 128]), "AM")
        ROPE = T(sb("ROPE", [128, 21, 32], F32), "ROPE")
        THX = T(sb("THX", [128, 512], F32), "THX")
        THY = T(sb("THY", [128, 512], F32), "THY")
        THYb = [THY.b, Buf("thy1")]
        ZS = T(sb("ZS", [128, D]), "ZS")
        OAG = T(sb("OAG", [128, D]), "OAG")
        Boag = [[Buf("oag%d_%d" % (u_, g_)) for g_ in range(4)] for u_ in range(4)]
        Boag_all = [b_ for r_ in Boag for b_ in r_]
        ZSb = [Buf("zs0"), Buf("zs1")]
        OAGT2 = [sb("OAGT%d" % i, [128, 8, 256]) for i in range(2)]
        Boagt2 = [[Buf("oagt%d_%d" % (i, j)) for j in range(2)] for i in range(2)]
        MT = sb("MT", [128, 8, 256])
        Bmt = [Buf("mt%d" % i) for i in range(8)]
        SS = sb("SS", [128, 4, 4], F32)
        Bss = Ring([(i, Buf("ss%d" % i)) for i in range(4)])
        RT = [T(sb("RT%d" % i, [128, 8, 16], F32), "RT%d" % i) for i in range(2)]
        RTK = [T(sb("RTK%d" % i, [128, 2, 16], F32), "RTK%d" % i) for i in range(2)]
        DEN = Ring([T(sb("DEN%d" % i, [128, 8], F32), "DEN%d" % i) for i in range(2)])
        ES2 = T(sb("ES2", [128, 8], F32), "ES2")
        NEGH = T(sb("NEGH", [128, 1], F32), "NEGH")
        ID = T(sb("ID", [128, 128]), "ID")

        MMB = [T(ps("MM%d" % i, [128, 512]), "MM%d" % i) for i in range(2)]
        STB = [T(ps("ST%d" % i, [128, 512]), "ST%d" % i) for i in range(4)]
        OB = [T(ps("O%d" % i, [128, 512]), "O%d" % i) for i in range(2)]
        mm = Ring(MMB)
        st_ring = Ring(STB)
        o_ring = Ring(OB)
        xr = Ring(XR)

        def cdma(eng, name, out_ap, in_ap, buf):
            s = P.new_dma_sem(name)
            P.dma(eng, s, lambda e: e.dma_start(out=out_ap, in_=in_ap), writes=[buf])

        Bgrep, Bfgrep = Buf("grep"), Buf("fgrep")
        cdma("sp", "c_grep", GREP[:, :], gain_d.partition_broadcast(128), Bgrep)
        cdma("sp", "c_rope", ROPE.t[:, :, :], rope_d, ROPE.b)
        cdma("sp", "c_sink", ES2.t[:, :], sink_d.partition_broadcast(128), ES2.b)
        cdma("pool", "c_am", AM.t[:, :, :], amask_d, AM.b)
        Bw = {name: Buf("w_" + name) for name, _, _ in W_PIECES}
        Bwpa, Bwpb, Bwo = Buf("wpa"), Buf("wpb"), [Buf("wo0"), Buf("wo1")]

        def wdma(names):
            for name in names:
                if name == "w_pa":
                    cdma("pool", "w_pa", WPA[:, :, :], wpa_d, Bwpa)
                elif name == "w_pb":
                    cdma("pool", "w_pb", WPB[:, :, :], wpb_d, Bwpb)
                elif name == "w_o0":
                    cdma("pool", "w_o0", WO[:, :, 0:512], wo_d[:, :, 0:512], Bwo[0])
                elif name == "w_o1":
                    cdma("pool", "w_o1", WO[:, :, 512:1024], wo_d[:, :, 512:1024], Bwo[1])
                else:
                    c0, c1 = [(a, b) for n_, a, b in W_PIECES if n_ == name][0]
                    cdma("pool", "w_" + name, W[:, :, c0:c1], win_d[:, :, c0:c1], Bw[name])
        wdma(["kva", "vb", "kb"])
        cdma("sp", "c_fgrep", FGREP[:, :], fgain_d.partition_broadcast(128), Bfgrep)

        def wbuf(c0):
            for name, a, b in W_PIECES:
                if a <= c0 < b:
                    return Bw[name]
            raise KeyError(c0)

        P.op("dve", lambda e: e.memset(THY.t[:, 0:128], 0.0), writes=[THY.b])
        P.op("pool", lambda e: e.affine_select(out=THY.t[:, 0:128], in_=THY.t[:, 0:128], pattern=[[-1, 128]],
                                               compare_op=ALU.not_equal, fill=1.0, base=0, channel_multiplier=1),
             reads=[THY.b], writes=[THY.b])
        P.op("dve", lambda e: e.tensor_copy(out=ID.t[:, :], in_=THY.t[:, 0:128]), reads=[THY.b], writes=[ID.b])
        P.op("dve", lambda e: e.memset(NEGH.t[:, :], -0.5), writes=[NEGH.b])
        P.op("dve", lambda e: e.memset(VA[:, :, :, :], 1.0), writes=Bva)
        P.op("dve", lambda e: e.memset(VB[:, :, :, :], 1.0), writes=Bvb)
        P.op("dve", lambda e: e.memset(VAM.t[:, :, :], 1.0), writes=[VAM.b])
        P.op("dve", lambda e: e.memset(VBM.t[:, :, :], 1.0), writes=[VBM.b])
        P.op("act", lambda e: e.activation(out=ES2.t[:, :], in_=ES2.t[:, :], func=AF.Exp), reads=[ES2.b], writes=[ES2.b])
        P.op("dve", lambda e: e.tensor_scalar(out=ES2.t[:, :], in0=ES2.t[:, :], scalar1=2.0, scalar2=None, op0=ALU.mult),
             reads=[ES2.b], writes=[ES2.b])

        def mmc(n_mm, ncols):
            return n_mm * max(95.0, 0.53 * ncols)

        def mm_group(out_ap, pairs, reads, wbufs, ncols=512):
            n = len(pairs)

            def fn(e):
                last = None
                for i, (l, r) in enumerate(pairs):
                    last = e.matmul(out=out_ap, lhsT=l, rhs=r, start=(i == 0), stop=(i == n - 1))
                return last
            P.op("pe", fn, reads=reads, writes=wbufs, cost=mmc(n, ncols))

        cp_toggle = [0]

        def copy(out_ap, in_ap, reads, writes, eng=None, dur=None):
            if eng is None:
                eng = ("act", "act", "dve")[cp_toggle[0] % 3]
                cp_toggle[0] += 1
            if eng == "act":
                P.op("act", lambda e: e.activation(out=out_ap, in_=in_ap, func=AF.Copy), reads=reads, writes=writes, dur=dur)
            else:
                P.op(eng, lambda e: e.tensor_copy(out=out_ap, in_=in_ap), reads=reads, writes=writes, dur=dur)

        def transposes(n_in, cols, src_t, src_col0, np_, bank, src_bufs=None):
            trv = bank.t.bitcast(BF16)

            def fn(e):
                last = None
                for i in range(n_in):
                    last = e.transpose(out=trv[:, i * 128:i * 128 + np_],
                                       in_=src_t.t[0:np_, src_col0 + i * 128:src_col0 + (i + 1) * 128],
                                       identity=ID.t[0:np_, 0:np_])
                return last
            P.op("pe", fn, reads=(src_bufs if src_bufs is not None else [src_t.b]) + [ID.b], writes=[bank.b], cost=mmc(n_in, 128))
            return trv

        def norm_rows(j, np_, src_ap, gain_t, gain_b, out_t, junk_t):
            xt = XR[j]
            si, sbuf_ = Bss.next()
            ss = SS[0:np_, si, :]
            P.op("act", lambda e: e.activation(out=junk_t.t[0:np_, :], in_=xt.t[0:np_, :], func=AF.Square,
                                               accum_out=ss[:, 0:1]), reads=[xt.b], writes=[junk_t.b, sbuf_], dur=1100.0)
            P.op("dve", lambda e: e.tensor_scalar(out=ss[:, 1:2], in0=ss[:, 0:1], scalar1=1.0 / D, scalar2=EPS,
                                                  op0=ALU.mult, op1=ALU.add), reads=[sbuf_], writes=[sbuf_], dur=120.0)
            P.op("pool", lambda e: e.tensor_tensor(out=ss[:, 2:3], in0=ss[:, 1:2], in1=NEGH.t[0:np_, :], op=ALU.pow),
                 reads=[sbuf_, NEGH.b], writes=[sbuf_])
            P.op("dve", lambda e: e.scalar_tensor_tensor(out=out_t.t[0:np_, :], in0=xt.t[0:np_, :], scalar=ss[:, 2:3],
                                                         in1=gain_t[0:np_, :], op0=ALU.mult, op1=ALU.mult),
                 reads=[xt.b, sbuf_, gain_b], writes=[out_t.b], dur=1300.0)

        def rope(src, sdims, dst, ddims, np_, rslot, rts, dst_buf, src_buf):
            nh = 1
            for _, c in sdims:
                nh *= c
            zero = [(0, c) for _, c in sdims]
            tdims = []
            acc = 16
            for _, c in reversed(sdims):
                tdims.insert(0, (acc, c))
                acc *= c
            rp = ROPE.t[0:np_, rslot, :]
            t1, t2 = rts

            def sub(base, off):
                return bass.AP(base.tensor, base.offset + off, base.ap)
            P.op("dve", lambda e: e.tensor_tensor(out=V(t1.t[0:np_, 0, :], tdims + [(1, 16)]), in0=V(src, sdims + [(1, 16)]),
                                                  in1=V(rp, zero + [(1, 16)]), op=ALU.mult),
                 reads=[src_buf, ROPE.b], writes=[t1.b])
            P.op("dve", lambda e: e.tensor_tensor(out=V(t2.t[0:np_, 0, :], tdims + [(1, 8)]), in0=V(sub(src, 8), sdims + [(1, 8)]),
                                                  in1=V(sub(rp, 16), zero + [(1, 8)]), op=ALU.mult),
                 reads=[src_buf, ROPE.b], writes=[t2.b])
            P.op("dve", lambda e: e.tensor_tensor(out=V(sub(t2.t[0:np_, 0, :], 8), tdims + [(1, 8)]), in0=V(src, sdims + [(1, 8)]),
                                                  in1=V(sub(rp, 24), zero + [(1, 8)]), op=ALU.mult),
                 reads=[src_buf, ROPE.b], writes=[t2.b])
            P.op("dve", lambda e: e.tensor_tensor(out=V(dst, ddims + [(1, 16)]), in0=V(t1.t[0:np_, 0, :], tdims + [(1, 16)]),
                                                  in1=V(t2.t[0:np_, 0, :], tdims + [(1, 16)]), op=ALU.add),
                 reads=[t1.b, t2.b], writes=[dst_buf])
            P.op("act", lambda e: e.activation(out=V(sub(dst, 16), ddims + [(1, 48)]), in_=V(sub(src, 16), sdims + [(1, 48)]),
                                               func=AF.Copy), reads=[src_buf], writes=[dst_buf])

        def kv_block(np_, lhs, lhs_bufs, rslot, kat_dst, kat_buf, va_dst, va_buf, vb_dst, vb_buf):
            b1 = mm.next()
            mm_group(b1.t[0:np_, 0:256], [(lhs(kc), W[:, kc, C_KA:C_KA + 256]) for kc in range(8)],
                     lhs_bufs + [Bw["kva"]], [b1.b], ncols=256)
            rope(b1.t[0:np_, 0:64], [(64, 2)], QKk.t[0:np_, 0:64], [(64, 2)], np_, rslot, RTK, QKk.b, b1.b)
            copy(va_dst, V(b1.t[0:np_, 128:192], [(64, 2), (1, 64)]), [b1.b], [va_buf], eng="act")
            b2 = mm.next()
            mm_group(b2.t[0:np_, 0:512], [(lhs(kc), W[:, kc, C_VB:C_VB + 512]) for kc in range(8)],
                     lhs_bufs + [Bw["vb"]], [b2.b])
            copy(vb_dst, V(b2.t[0:np_, 0:64], [(64, 8), (1, 64)]), [b2.b], [vb_buf], eng="act")
            b3 = mm.next()
            trv = transposes(1, 128, QKk, 0, np_, b3)
            copy(kat_dst, trv[:, 0:np_], [b3.b], [kat_buf])

        def kb_feature(n_tok, rhs, rhs_bufs, dst, dst_bufs):
            for c in range(4):
                b = mm.next()
                mm_group(b.t[:, 0:n_tok], [(W[:, kc, C_KB + c * 128:C_KB + (c + 1) * 128], rhs(kc)) for kc in range(8)],
                         rhs_bufs + [Bw["kb"]], [b.b], ncols=n_tok)
                copy(dst(c), b.t[:, 0:n_tok], [b.b], dst_bufs(c))

        def do_meta():
            x0 = xr.next()
            j = XR.index(x0)
            s = P.new_dma_sem("ld_meta")
            P.dma("sp", s, lambda e: e.dma_start(out=x0.t[0:16, :], in_=meta_d), writes=[x0.b])
            norm_rows(j, 16, None, GREP, Bgrep, NBt, NBt)
            b0 = mm.next()
            trv = transposes(8, 128, NBt, 0, 16, b0)
            copy(NTM.t[:, :, :], V(trv[:, 0:16], [(128, 8), (1, 16)]), [b0.b], [NTM.b])
            kv_block(16, lambda kc: NTM.t[:, kc, 0:16], [NTM.b], 20,
                     KATM.t[:, 0:16], KATM.b, VAM.t[0:16, :, 0:64], VAM.b, VBM.t[0:16, :, 0:64], VBM.b)
            kb_feature(16, lambda kc: NTM.t[:, kc, 0:16], [NTM.b], lambda c: KBTM.t[:, c, 0:16], lambda c: [KBTM.b])

        ld_sems = [P.new_dma_sem("ld%d" % i) for i in range(3)]
        st_sems = [P.new_dma_sem("st%d" % i) for i in range(3)]

        def front(t, part=None):
            s = t % 3

            def load_norm(bi):
                pb = 2 * t + bi
                x0 = xr.next()
                j = XR.index(x0)
                P.dma("sp", ld_sems[j], lambda e, x0=x0, pb=pb: e.dma_start(out=x0.t[:, :], in_=xp_d[pb * 128:(pb + 1) * 128, :]),
                      writes=[x0.b])
                norm_rows(j, 128, None, GREP, Bgrep, NBt, NBt)

            def tr_nt(bi):
                b0 = mm.next()
                trv = transposes(8, 128, NBt, 0, 128, b0)
                copy(NT[s][:, :, bi * 128:(bi + 1) * 128], V(trv[:, 0:128], [(128, 8), (1, 128)]), [b0.b], [Bnt[s][bi]], dur=950.0)

            def kv(bi):
                pb = 2 * t + bi
                slot = pb % 8
                kv_block(128, lambda kc, bi=bi: NT[s][:, kc, bi * 128:(bi + 1) * 128], [Bnt[s][bi]], pb,
                         KAT[:, slot * 128:(slot + 1) * 128], Bkat[slot],
                         VA[:, slot, :, 0:64], Bva[slot], VB[:, slot, :, 0:64], Bvb[slot])
            if part in (None, "pre"):
                load_norm(0)
            if part == "pre":
                return
            tr_nt(0)
            load_norm(1)
            kv(0)
            tr_nt(1)
            kv(1)
            slot0 = (2 * t) % 8
            kb_feature(256, lambda kc: NT[s][:, kc, 0:256], [Bnt[s][0], Bnt[s][1]],
                       lambda c: KBT[:, c, slot0 * 128:slot0 * 128 + 256], lambda c: [Bkbt[slot0][c], Bkbt[slot0 + 1][c]])

        def normalize(ob, unit_is_a, k_or_u):
            den = DEN.next()
            o_den = V(ob.t[:, 64:65], [(65, 4)])
            if unit_is_a:
                k = k_or_u
                P.op("dve", lambda e: e.scalar_tensor_tensor(out=den.t[:, 0:4], in0=o_den, scalar=2.0,
                                                             in1=ES2.t[:, 4 * k:4 * k + 4], op0=ALU.mult, op1=ALU.add),
                     reads=[ob.b, ES2.b], writes=[den.b])
            else:
                P.op("dve", lambda e: e.tensor_scalar(out=den.t[:, 0:4], in0=o_den, scalar1=2.0, scalar2=None, op0=ALU.mult),
                     reads=[ob.b], writes=[den.b])
            P.op("dve", lambda e: e.reciprocal(out=den.t[:, 4:8], in_=den.t[:, 0:4]), reads=[den.b], writes=[den.b], dur=200.0)
            for g in range(4):
                col = (k_or_u * 256 + g * 64) if unit_is_a else (512 + (2 * g + k_or_u) * 64)
                P.op("dve", lambda e, g=g, col=col: e.scalar_tensor_tensor(
                    out=OAG.t[:, col:col + 64], in0=ob.t[:, g * 65:g * 65 + 64], scalar=den.t[:, 4 + g:5 + g],
                    in1=ZS.t[:, col:col + 64], op0=ALU.mult, op1=ALU.mult),
                    reads=[ob.b, den.b, ZSb[0 if unit_is_a else 1]], writes=[Boag[(0 if unit_is_a else 2) + k_or_u][g]], dur=360.0)

        def pv_op(ob, pt, nk, rhs_fn, first, last, reads):
            def fn(e):
                r = None
                for g in range(4):
                    r = e.matmul(out=ob.t[:, g * 65:(g + 1) * 65], lhsT=pt.t[0:nk, g * 128:(g + 1) * 128], rhs=rhs_fn(g),
                                 start=(first and g == 0), stop=(last and g == 3), skip_group_check=True)
                return r
            P.op("pe", fn, reads=[pt.b] + reads, writes=[ob.b], cost=220.0)

        def piece_idx(m, jj):
            if m == 0:
                return 5 + jj
            if m == 1:
                return 11 + jj
            if m == 14:
                return 16 + jj
            if m == 15:
                return 21 + jj
            return jj

        tb_sems = [P.new_dma_sem("tb%d" % i) for i in range(3)]

        def attn_block(pb, bi, extra=None):
            lb = pb - 2
            tasks = []
            obsA = [o_ring.next(), o_ring.next()]
            chunksA = [("M", 16, None), ("L", 128, pb - 1), ("C", 128, pb), ("R", 128, pb + 1)]
            for ci, (typ, nk, kb_) in enumerate(chunksA):
                def S(state, typ=typ, nk=nk, kb_=kb_):
                    sts = [st_ring.next(), st_ring.next()]
                    if typ == "M":
                        kbuf = KATM.b
                        lhs_fn = lambda k: KATM.t[64 * k:64 * k + 64, 0:16]
                    else:
                        slot = kb_ % 8
                        kbuf = Bkat[slot]
                        lhs_fn = lambda k, slot=slot: KAT[64 * k:64 * k + 64, slot * 128:(slot + 1) * 128]

                    def sfn(e):
                        r = None
                        for k in range(2):
                            r = e.matmul(out=sts[k].t[0:nk, 0:512], lhsT=lhs_fn(k),
                                         rhs=V(QTA[64 * k:64 * k + 64, 0, bi * 128:(bi + 1) * 128], [(256, 4), (1, 128)]),
                                         start=True, stop=True)
                        return r
                    P.op("pe", sfn, reads=[kbuf, Bqta[bi]], writes=[sts[0].b, sts[1].b], cost=410.0)
                    state["pts"] = []
                    for k in range(2):
                        st = sts[k]
                        pt = PT.next()
                        P.op("act", lambda e, st=st, pt=pt: e.activation(out=pt.t[0:nk, :], in_=st.t[0:nk, 0:512], func=AF.Exp, scale=0.125),
                             reads=[st.b], writes=[pt.b], dur=560.0)
                        if typ in ("L", "R"):
                            mt_ = (2 if lb == 0 else 0) if typ == "L" else (3 if lb == 15 else 1)
                            P.op("dve", lambda e, pt=pt, mt_=mt_: e.tensor_tensor(out=V(pt.t[:, 0:128], [(128, 4), (1, 128)]),
                                                                                  in0=V(pt.t[:, 0:128], [(128, 4), (1, 128)]),
                                                                                  in1=V(AM.t[:, mt_, :], [(0, 4), (1, 128)]), op=ALU.mult),
                                 reads=[pt.b, AM.b], writes=[pt.b])
                        state["pts"].append(pt)

                def PV(state, ci=ci, typ=typ, nk=nk, kb_=kb_):
                    for k in range(2):
                        if typ == "M":
                            vb_, rhs_v = VAM.b, (lambda g, k=k: VAM.t[0:16, k, :])
                        else:
                            slot = kb_ % 8
                            vb_, rhs_v = Bva[slot], (lambda g, k=k, slot=slot: VA[:, slot, k, :])
                        pv_op(obsA[k], state["pts"][k], nk, rhs_v, ci == 0, ci == 3, [vb_])
                    if ci == 3:
                        for k in range(2):
                            normalize(obsA[k], True, k)
                tasks.append((S, PV, {}))
            m = lb
            js = list(range(0, 6)) if m == 0 else (list(range(-1, 5)) if m == 15 else list(range(0, 5)))
            obsB = [o_ring.next(), o_ring.next()]
            chunksB = [("M", 16, None, None)] + [("W", 128, pb + j - 2, jj) for jj, j in enumerate(js)]
            nB = len(chunksB)
            for ci, (typ, nk, kb_, jj) in enumerate(chunksB):
                def S(state, typ=typ, nk=nk, kb_=kb_, jj=jj):
                    tb = None
                    if typ == "W":
                        tb = TBE.next()
                        ti = TBE.items.index(tb)
                        pi = piece_idx(m, jj)
                        P.dma("pool", tb_sems[ti], lambda e, tb=tb, pi=pi: e.dma_start(out=tb.t[:, :], in_=btab_d[pi * 128:(pi + 1) * 128, :]),
                              writes=[tb.b])
                    sts = [st_ring.next(), st_ring.next()]
                    if typ == "M":
                        kbufs = [KBTM.b]
                        lhs_fn = lambda c, u: KBTM.t[64 * u:64 * u + 64, c, 0:16]
                    else:
                        slot = kb_ % 8
                        kbufs = list(Bkbt[slot])
                        lhs_fn = lambda c, u, slot=slot: KBT[64 * u:64 * u + 64, c, slot * 128:(slot + 1) * 128]

                    def sfn(e):
                        r = None
                        for c in range(4):
                            for u in range(2):
                                r = e.matmul(out=sts[u].t[0:nk, c * 128:(c + 1) * 128], lhsT=lhs_fn(c, u),
                                             rhs=QTB.t[64 * u:64 * u + 64, c, bi * 128:(bi + 1) * 128], start=True, stop=True)
                        return r
                    P.op("pe", sfn, reads=kbufs + QTBb, writes=[sts[0].b, sts[1].b], cost=460.0)
                    state["pts"] = []
                    for u in range(2):
                        st = sts[u]
                        pt = PT.next()
                        if typ == "W":
                            P.op("dve", lambda e, st=st, tb=tb, u=u: e.scalar_tensor_tensor(
                                out=st.t[:, 0:512], in0=st.t[:, 0:512], scalar=0.125,
                                in1=V(tb.t[:, u * 128:(u + 1) * 128], [(256, 4), (1, 128)]), op0=ALU.mult, op1=ALU.add),
                                reads=[st.b, tb.b], writes=[st.b], dur=520.0)
                            P.op("act", lambda e, st=st, pt=pt: e.activation(out=pt.t[:, :], in_=st.t[:, 0:512], func=AF.Exp),
                                 reads=[st.b], writes=[pt.b], dur=620.0)
                        else:
                            P.op("act", lambda e, st=st, pt=pt: e.activation(out=pt.t[0:16, :], in_=st.t[0:16, 0:512], func=AF.Exp, scale=0.125),
                                 reads=[st.b], writes=[pt.b])
                        state["pts"].append(pt)

                def PV(state, ci=ci, typ=typ, nk=nk, kb_=kb_):
                    for u in range(2):
                        if typ == "M":
                            vb_, rhs_v = VBM.b, (lambda c, u=u: VBM.t[0:16, 2 * c + u, :])
                        else:
                            slot = kb_ % 8
                            vb_, rhs_v = Bvb[slot], (lambda c, u=u, slot=slot: VB[:, slot, 2 * c + u, :])
                        pv_op(obsB[u], state["pts"][u], nk, rhs_v, ci == 0, ci == nB - 1, [vb_])
                    if ci == nB - 1:
                        for u in range(2):
                            normalize(obsB[u], False, u)
                tasks.append((S, PV, {}))
            prev = None
            for ti, (S, PV, state) in enumerate(tasks):
                S(state)
                if prev is not None:
                    prev[0](prev[1])
                prev = (PV, state)
                if extra is not None and ti == extra[0]:
                    extra[1]()
            prev[0](prev[1])

        oagt_ver = [0, 0]
        dc_done = [0]

        def qproj(t):
            s = t % 3

            def qa(bi):
                pb = 2 * t + bi
                b = st_ring.next()
                mm_group(b.t[:, 0:512], [(NT[s][:, kc, bi * 128:(bi + 1) * 128], W[:, kc, C_QA:C_QA + 512]) for kc in range(8)],
                         [Bnt[s][bi], Bw["qa"]], [b.b])
                rope(b.t[:, 0:64], [(256, 2), (64, 4)], QKq2[bi].t[:, 0:64], [(64, 2), (128, 4)], 128, pb, RT, QKq2[bi].b, b.b)

            def qtr(bi):
                b2 = st_ring.next()
                trv = transposes(4, 128, QKq2[bi], 0, 128, b2)
                copy(QTA[:, :, bi * 128:(bi + 1) * 128], V(trv[:, 0:128], [(128, 4), (1, 128)]), [b2.b], [Bqta[bi]])

            def qb(c):
                b = st_ring.next()
                mm_group(b.t[:, 0:256], [(W[:, kc, C_QB + c * 128:C_QB + (c + 1) * 128], NT[s][:, kc, 0:256]) for kc in range(8)],
                         [Bnt[s][0], Bnt[s][1], Bw["qb"]], [b.b], ncols=256)
                copy(QTB.t[:, c, :], b.t[:, 0:256], [b.b], [QTBb[c]])
            qa(0)
            qa(1)
            qb(0)
            qtr(0)
            qb(1)
            qtr(1)
            qb(2)
            qb(3)

        zstate = {}

        def attn(t, part, sec=None):
            s = t % 3
            zbanks = zstate.setdefault(t, [])
            if part == 1 and sec in (None, "a"):
                for br, c0, wn in ((0, C_ZA, "za"), (1, C_ZB, "zb")):
                    b = st_ring.next()
                    mm_group(b.t[:, 0:512], [(NT[s][:, kc, 128:256], W[:, kc, c0:c0 + 512]) for kc in range(8)],
                             [Bnt[s][1], Bw[wn]], [b.b])
                    zbanks.append(b)

            def oag_b0():
                def chk0():
                    assert dc_done[0] >= t - 2, ("OAGT blk0 overwritten before back_dc", t, dc_done[0])
                P.check(chk0)
                b2 = st_ring.next()
                trv = transposes(8, 128, OAG, 0, 128, b2, src_bufs=Boag_all)
                copy(OAGT2[t % 2][:, :, 0:128], V(trv[:, 0:128], [(128, 8), (1, 128)]), [b2.b], [Boagt2[t % 2][0]], dur=950.0)

                def set0():
                    oagt_ver[0] = t
                P.check(set0)
            if part == 1 and sec in (None, "a") and t == 8:
                oag_b0()
            for bi in ((0,) if part == 0 else (1,)):
                pb = 2 * t + bi
                for br, c0, wn in ((0, C_ZA, "za"), (1, C_ZB, "zb")):
                    if part == 1 and sec == "b":
                        continue
                    if part == 1:
                        b = zbanks[br]
                        th = THX
                        P.op("act", lambda e, b=b, th=th: e.activation(out=th.t[:, :], in_=b.t[:, 0:512], func=AF.Tanh, scale=0.5),
                             reads=[b.b], writes=[th.b])
                        P.op("dve", lambda e, b=b, th=th, br=br: e.scalar_tensor_tensor(
                            out=ZS.t[:, br * 512:(br + 1) * 512], in0=th.t[:, :], scalar=1.0, in1=b.t[:, 0:512],
                            op0=ALU.add, op1=ALU.mult), reads=[th.b, b.b], writes=[ZSb[br]])
                        continue
                    b = st_ring.next()
                    mm_group(b.t[:, 0:512], [(NT[s][:, kc, bi * 128:(bi + 1) * 128], W[:, kc, c0:c0 + 512]) for kc in range(8)],
                             [Bnt[s][bi], Bw[wn]], [b.b])
                    th = THX
                    P.op("act", lambda e, b=b, th=th: e.activation(out=th.t[:, :], in_=b.t[:, 0:512], func=AF.Tanh, scale=0.5),
                         reads=[b.b], writes=[th.b])
                    P.op("dve", lambda e, b=b, th=th, br=br: e.scalar_tensor_tensor(
                        out=ZS.t[:, br * 512:(br + 1) * 512], in0=th.t[:, :], scalar=1.0, in1=b.t[:, 0:512],
                        op0=ALU.add, op1=ALU.mult), reads=[th.b, b.b], writes=[ZSb[br]])
                if part == 1 and sec == "a":
                    continue
                attn_block(pb, bi, extra=((1, oag_b0) if (bi == 1 and t < 8) else None))
                if bi == 1:
                    if t < 8:
                        qproj(t + 1)
                    def chk1():
                        assert dc_done[0] >= t - 2, ("OAGT blk1 overwritten before back_dc", t, dc_done[0])
                    P.check(chk1)
                    b2 = st_ring.next()
                    trv = transposes(8, 128, OAG, 0, 128, b2, src_bufs=Boag_all)
                    copy(OAGT2[t % 2][:, :, 128:256], V(trv[:, 0:128], [(128, 8), (1, 128)]), [b2.b], [Boagt2[t % 2][1]], dur=950.0)

                    def set1():
                        oagt_ver[1] = t
                    P.check(set1)

        def back_dc(t, blks=(0, 1)):
            s = t % 3
            lo, n = blks[0] * 128, 128 * len(blks)

            def chk():
                for bb in blks:
                    assert oagt_ver[bb] == t, ("back_dc reads stale OAGT", t, bb, oagt_ver)
            P.check(chk)
            for dc in range(8):
                bA, bB = mm.next(), mm.next()
                th = THY
                for bnk, wp, bwp, koff, cg, half in ((bA, WPA, Bwpa, 0, C_GA, 0), (bB, WPB, Bwpb, 4, C_GB, 1)):
                    mm_group(bnk.t[:, 0:n], [(wp[:, kc, dc * 128:(dc + 1) * 128], OAGT2[t % 2][:, koff + kc, lo:lo + n]) for kc in range(4)],
                             [bwp] + [Boagt2[t % 2][bb] for bb in blks], [bnk.b], ncols=n)
                    gc = cg + dc * 128
                    mm_group(bnk.t[:, 256:256 + n], [(W[:, kc, gc:gc + 128], NT[s][:, kc, lo:lo + n]) for kc in range(8)],
                             [wbuf(gc)] + [Bnt[s][bb] for bb in blks], [bnk.b], ncols=n)
                    P.op("act", lambda e, bnk=bnk, th=th, half=half: e.activation(out=th.t[:, half * 256:half * 256 + n], in_=bnk.t[:, 256:256 + n],
                                                                                 func=AF.Tanh, scale=0.5), reads=[bnk.b], writes=[THYb[half]])
                    P.op("dve", lambda e, bnk=bnk, th=th, half=half: e.scalar_tensor_tensor(
                        out=th.t[:, half * 256:half * 256 + n], in0=th.t[:, half * 256:half * 256 + n], scalar=1.0,
                        in1=bnk.t[:, 0:n], op0=ALU.add, op1=ALU.mult), reads=[THYb[half], bnk.b], writes=[THYb[half]])
                P.op("dve", lambda e, th=th, dc=dc: e.tensor_tensor(out=MT[:, dc, lo:lo + n], in0=th.t[:, 0:n], in1=th.t[:, 256:256 + n], op=ALU.add),
                     reads=THYb, writes=[Bmt[dc]])

            def done():
                if len(blks) == 2 or blks[0] == 1:
                    dc_done[0] = t
            P.check(done)

        def back_out(t, blks=(0, 1)):
            xs = {}
            for bi in blks:
                pb = 2 * t + bi
                x0 = xr.next()
                j = XR.index(x0)
                P.dma("sp", ld_sems[j], lambda e, x0=x0, pb=pb: e.dma_start(out=x0.t[:, :], in_=xp_d[pb * 128:(pb + 1) * 128, :]),
                      writes=[x0.b])
                xs[bi] = (x0, j)
            for bi in blks:
                pb = 2 * t + bi
                lb = pb - 2
                x0, j = xs[bi]
                for half in range(2):
                    b = mm.next()
                    mm_group(b.t[:, 0:512], [(MT[:, kc, bi * 128:(bi + 1) * 128], WO[:, kc, half * 512:(half + 1) * 512]) for kc in range(8)],
                             Bmt + [Bwo[half]], [b.b])
                    P.op("dve", lambda e, b=b, x0=x0, half=half: e.scalar_tensor_tensor(
                        out=x0.t[:, half * 512:(half + 1) * 512], in0=b.t[:, 0:512], scalar=0.5,
                        in1=x0.t[:, half * 512:(half + 1) * 512], op0=ALU.mult, op1=ALU.add), reads=[b.b, x0.b], writes=[x0.b], dur=560.0)
                norm_rows(j, 128, None, FGREP, Bfgrep, x0, NBt)
                P.dma("sp", st_sems[j], lambda e, x0=x0, lb=lb: e.dma_start(out=y_d[lb * 128:(lb + 1) * 128, :], in_=x0.t[:, :]),
                      reads=[x0.b])

        import os
        if os.environ.get('KDEBUG'):
            print('SBUF base/top', nc.sbuf_base, nc.sbuf_top, 'free', nc.sbuf_top - nc.sbuf_base)
        do_meta()
        wdma(["qa", "qb"])
        front(0)
        wdma(["za", "zb", "ga0", "ga1"])
        front(1)
        wdma(["gb0", "gb1", "w_pa", "w_pb", "w_o0", "w_o1"])
        front(2)
        qproj(1)
        for t in range(1, 9):
            if t < 8:
                def xthread():
                    attn(t, 0)
                    attn(t, 1)

                def ythread():
                    if t + 2 <= 9:
                        front(t + 2, "pre")
                    if t > 1:
                        back_dc(t - 1)
                    if t + 2 <= 9:
                        front(t + 2, "rest")
                    if t > 1:
                        back_out(t - 1)
                P.schedule([P.record(xthread), P.record(ythread)])
            else:
                def y8a():
                    back_dc(7)
                    back_out(7)
                def x8a():
                    attn(8, 0)
                    attn(8, 1, "a")
                P.schedule([P.record(x8a), P.record(y8a)])
                X1b = P.record(lambda: attn(8, 1, "b"))

                def y8b():
                    back_dc(8, (0,))
                    back_out(8, (0,))
                P.schedule([X1b, P.record(y8b)])
        back_dc(8, (1,))
        back_out(8, (1,))
        final = [(s, P.seq[s]) for s in st_sems if P.seq[s] > 0]
        P.emit(final)
    return nc


def _rope_table(c):
    half = 8
    inv_freq = (np.float32(500000.0) ** (-np.arange(half, dtype=np.float32) / np.float32(half))).astype(np.float32)
    tab = np.zeros((128, 21, 32), np.float32)
    p = np.arange(128)
    for pb in range(NPB):
        tok = c * TOK - HALO + pb * 128 + p
        pos = (tok + NMETA).astype(np.float32)
        ang = (pos[:, None] * inv_freq[None, :]).astype(np.float32)
        cs, sn = np.cos(ang).astype(np.float32), np.sin(ang).astype(np.float32)
        tab[:, pb, 0:8] = cs
        tab[:, pb, 8:16] = cs
        tab[:, pb, 16:24] = -sn
        tab[:, pb, 24:32] = sn
    pos = np.arange(128).astype(np.float32)
    ang = (pos[:, None] * inv_freq[None, :]).astype(np.float32)
    cs, sn = np.cos(ang).astype(np.float32), np.sin(ang).astype(np.float32)
    tab[:, 20, 0:8] = cs
    tab[:, 20, 8:16] = cs
    tab[:, 20, 16:24] = -sn
    tab[:, 20, 24:32] = sn
    return tab


def _amask(c):
    j = np.arange(128)[:, None]
    i = np.arange(128)[None, :]
    L = (j >= i).astype(np.float32)
    R = (j <= i).astype(np.float32)
    am = np.zeros((128, 4, 128), np.float32)
    am[:, 0] = L
    am[:, 1] = R
    am[:, 2] = L if c > 0 else 0.0
    am[:, 3] = R if c < 3 else 0.0
    return am


def _btab_piece(rpb, c, m, j):
    R0 = 32 * c + 2 * m
    KR0 = R0 + 2 * (j - 2)
    kp = np.arange(128)
    qp = np.arange(128)
    kR = KR0 + kp // 64
    kc = kp % 64
    r = R0 + qp // 64
    qc = qp % 64
    r_start = np.clip(r - 4, 0, 128 - 8)
    cstart = np.clip(qc - 8, 0, 64 - 16)
    ok_r = (kR[:, None] >= r_start[None, :]) & (kR[:, None] < r_start[None, :] + 8) & (kR[:, None] >= 0) & (kR[:, None] < 128)
    ok_c = (kc[:, None] >= cstart[None, :]) & (kc[:, None] < cstart[None, :] + 16)
    ok = ok_r & ok_c
    dr = np.clip(kR[:, None] - r[None, :] + 7, 0, 14)
    dc = np.clip(kc[:, None] - qc[None, :] + 15, 0, 30)
    out = np.full((128, 8, 128), NEG, np.float32)
    for h in range(8):
        g = rpb[h][dr, dc]
        out[:, h, :] = np.where(ok, g, np.float32(NEG))
    return out


def _btab(rpb, c):
    pieces = []
    for j in range(5):
        pieces.append(_btab_piece(rpb, 1, 8, j))
    for j in range(6):
        pieces.append(_btab_piece(rpb, c, 0, j))
    for j in range(5):
        pieces.append(_btab_piece(rpb, c, 1, j))
    for j in range(5):
        pieces.append(_btab_piece(rpb, c, 14, j))
    for j in range(-1, 5):
        pieces.append(_btab_piece(rpb, c, 15, j))
    return np.stack(pieces).reshape(27 * 128, 1024)


_NC_CACHE = {}


def kernel(x, meta_tokens, norm_gain, w_in, sink_logits, rel_pos_bias, w_proj_a, w_proj_b, w_out, final_norm_gain):
    f = np.float32
    x = np.asarray(x, f)
    w_in_l = np.ascontiguousarray(np.asarray(w_in, f)[0].reshape(8, 128, NCOL).transpose(1, 0, 2))
    w_pa_l = np.ascontiguousarray(np.asarray(w_proj_a, f)[0].reshape(4, 128, D).transpose(1, 0, 2))
    w_pb_l = np.ascontiguousarray(np.asarray(w_proj_b, f)[0].reshape(4, 128, D).transpose(1, 0, 2))
    w_out_l = np.ascontiguousarray(np.asarray(w_out, f)[0].reshape(8, 128, D).transpose(1, 0, 2))
    gain = np.ascontiguousarray(np.asarray(norm_gain, f).reshape(1, D))
    fgain = np.ascontiguousarray(np.asarray(final_norm_gain, f).reshape(1, D))
    sink = np.ascontiguousarray(np.asarray(sink_logits, f).reshape(1, 8))
    meta = np.ascontiguousarray(np.asarray(meta_tokens, f))
    rpb = np.asarray(rel_pos_bias, f)[0]
    ropes = [_rope_table(c) for c in range(4)]
    amasks = [_amask(c) for c in range(4)]
    btabs = [_btab(rpb, c) for c in range(4)]
    in_maps = []
    for ci in range(8):
        b, c = divmod(ci, 4)
        xp = np.zeros((NPB * 128, D), f)
        lo, hi = c * TOK - HALO, c * TOK + TOK + HALO
        slo, shi = max(lo, 0), min(hi, SEQ)
        xp[slo - lo:shi - lo] = x[b, slo:shi]
        in_maps.append({"xp": xp, "meta": meta, "w_in": w_in_l, "w_pa": w_pa_l, "w_pb": w_pb_l, "w_out": w_out_l,
                        "gain": gain, "fgain": fgain, "sink": sink, "rope": ropes[c], "amask": amasks[c], "btab": btabs[c]})
    if "nc" not in _NC_CACHE:
        _NC_CACHE["nc"] = build_nc()
    res = run_bass_kernel_spmd(_NC_CACHE["nc"], in_maps, core_ids=list(range(8)))
    out = np.zeros((2, SEQ, D), f)
    for ci in range(8):
        b, c = divmod(ci, 4)
        out[b, c * TOK:(c + 1) * TOK] = res.results[ci]["y"]
    return out
```
